# Optimizing a Trainium2 kernel written in Bass

```python
import math
import numpy as np
import jax
import jax.numpy as jnp
from jax import lax

D_MODEL = 1024
BATCH = 2
SEQ = 16384
DEPTH = 2
DEC_BATCH = 4
DEC_SEQ = 4096
PAST_LEN = 128

GLA_HEADS = 4
GLA_DK = 128
GLA_DV = 128
GLA_KW = GLA_HEADS * GLA_DK
GLA_WIDTH = GLA_HEADS * GLA_DV
GLA_RANK = 16
GLA_TAU = 16.0
GLA_CHUNK = 64
SWA_HEADS = 8
SWA_KV_HEADS = 2
SWA_GROUP = SWA_HEADS // SWA_KV_HEADS
SWA_HD = 64
SWA_WIDTH = SWA_HEADS * SWA_HD
SWA_KVW = SWA_KV_HEADS * SWA_HD
WINDOW = 128
BLOCK = 128
REL_BUCKETS = 32
REL_MAX_DIST = 128
CONV_WIDTH = D_MODEL
CONV_K = 3
DN_ALPHA = (2 * DEPTH) ** 0.25
DN_BETA = (8 * DEPTH) ** -0.25
LN_EPS = 1e-5
NORM_EPS = 1e-6
N_EVEN = (DEPTH + 1) // 2
N_ODD = DEPTH // 2
EVEN_SPLITS = (GLA_KW, GLA_KW, GLA_WIDTH, GLA_WIDTH, 2 * GLA_RANK, SWA_WIDTH, SWA_KVW, SWA_KVW, SWA_WIDTH)
EVEN_IN = sum(EVEN_SPLITS)
EVEN_MIX = GLA_WIDTH + SWA_WIDTH
ODD_IN = 4 * CONV_WIDTH

kernel_name = "hybrid_gla_swa_shortconv_encoder"


def _offsets(splits):
    out, acc = [], 0
    for s in splits[:-1]:
        acc += s
        out.append(acc)
    return out


def layer_norm(x, g, b):
    xf = x.astype(jnp.float32)
    mu = jnp.mean(xf, -1, keepdims=True)
    var = jnp.mean(jnp.square(xf - mu), -1, keepdims=True)
    y = (xf - mu) * lax.rsqrt(var + LN_EPS) * g.astype(jnp.float32) + b.astype(jnp.float32)
    return y.astype(x.dtype)


def _gla_scan(q, k, v, logd, strict):
    Bn, H, L, dk = q.shape
    dv = v.shape[-1]
    n = L // GLA_CHUNK

    def chunks(t):
        return t.reshape(Bn, H, n, GLA_CHUNK, t.shape[-1]).transpose(2, 0, 1, 3, 4)

    qc, kc, vc, gc = chunks(q), chunks(k), chunks(v), chunks(logd)
    idx = jnp.arange(GLA_CHUNK)
    mask = (idx[None, :] < idx[:, None]) if strict else (idx[None, :] <= idx[:, None])

    def step(S, inp):
        qi, ki, vi, gi = inp
        b = jnp.cumsum(gi, axis=2)
        inter = jnp.einsum('bhcd,bhde->bhce', qi * jnp.exp(b), S)
        diff = b[:, :, :, None, :] - b[:, :, None, :, :]
        decay = jnp.exp(jnp.where(mask[:, :, None], diff, -jnp.inf))
        scores = jnp.sum(qi[:, :, :, None, :] * ki[:, :, None, :, :] * decay, axis=-1)
        intra = jnp.einsum('bhij,bhje->bhie', scores, vi)
        b_last = b[:, :, -1:, :]
        S_new = jnp.exp(b_last[:, :, 0, :])[..., None] * S + jnp.einsum(
            'bhcd,bhce->bhde', ki * jnp.exp(b_last - b), vi)
        return S_new, inter + intra

    S0 = jnp.zeros((Bn, H, dk, dv), jnp.float32)
    _, out = lax.scan(step, S0, (qc, kc, vc, gc))
    return out.transpose(1, 2, 0, 3, 4).reshape(Bn, H, L, dv)


def gla_mixer(q, k, v, z, gdown, w_up_f, b_f, w_up_b, b_b, norm_g):
    Bn, L, _ = q.shape
    f32 = jnp.float32

    def heads(t, d):
        return t.astype(f32).reshape(Bn, L, GLA_HEADS, d).transpose(0, 2, 1, 3)

    qh = heads(q, GLA_DK) * (GLA_DK ** -0.5)
    kh = heads(k, GLA_DK)
    vh = heads(v, GLA_DV)
    gd_f, gd_b = gdown[..., :GLA_RANK], gdown[..., GLA_RANK:]
    logd_f = heads(jax.nn.log_sigmoid((gd_f @ w_up_f + b_f).astype(f32)) / GLA_TAU, GLA_DK)
    logd_b = heads(jax.nn.log_sigmoid((gd_b @ w_up_b + b_b).astype(f32)) / GLA_TAU, GLA_DK)
    o_f = _gla_scan(qh, kh, vh, logd_f, strict=False)
    flip = lambda t: t[:, :, ::-1]
    o_b = flip(_gla_scan(flip(qh), flip(kh), flip(vh), flip(logd_b), strict=True))
    o = o_f + o_b
    o = o * lax.rsqrt(jnp.mean(jnp.square(o), -1, keepdims=True) + NORM_EPS) * norm_g.astype(f32)
    o = o.transpose(0, 2, 1, 3).reshape(Bn, L, GLA_WIDTH).astype(v.dtype)
    return o * jax.nn.silu(z)


def _rel_buckets():
    i = np.arange(BLOCK)[:, None]
    j = np.arange(3 * BLOCK)[None, :]
    rel = j - BLOCK - i
    half = REL_BUCKETS // 2
    max_exact = half // 2
    n = np.abs(rel)
    large = max_exact + (np.log(np.maximum(n, 1) / max_exact) / np.log(REL_MAX_DIST / max_exact)
                         * (half - max_exact)).astype(np.int32)
    large = np.minimum(large, half - 1)
    bucket = (rel > 0).astype(np.int32) * half + np.where(n < max_exact, n, large)
    return bucket.astype(np.int32), rel


def swa_mixer(q, k, v, z, rel_bias, sink):
    Bn, L, _ = q.shape
    nb = L // BLOCK
    qh = q.reshape(Bn, nb, BLOCK, SWA_KV_HEADS, SWA_GROUP, SWA_HD) * (SWA_HD ** -0.5)

    def banded(t):
        tp = jnp.pad(t.reshape(Bn, L, SWA_KV_HEADS, SWA_HD), ((0, 0), (BLOCK, BLOCK), (0, 0), (0, 0)))
        tp = tp.reshape(Bn, nb + 2, BLOCK, SWA_KV_HEADS, SWA_HD)
        return jnp.concatenate([tp[:, :-2], tp[:, 1:-1], tp[:, 2:]], axis=2)

    kb, vb = banded(k), banded(v)
    bucket, rel = _rel_buckets()
    key_pos = np.arange(nb)[:, None] * BLOCK - BLOCK + np.arange(3 * BLOCK)[None, :]
    valid = (key_pos >= 0) & (key_pos < L)
    mask = jnp.asarray((np.abs(rel) <= WINDOW)[None] & valid[:, None, :])[:, None, None]
    bias = rel_bias.astype(jnp.float32)[bucket]
    bias = bias.transpose(2, 0, 1).reshape(SWA_KV_HEADS, SWA_GROUP, BLOCK, 3 * BLOCK)

    s = jnp.einsum('bnqkgd,bnskd->bnkgqs', qh, kb).astype(jnp.float32) + bias
    s = jnp.where(mask, s, -jnp.inf)
    sk = sink.astype(jnp.float32).reshape(SWA_KV_HEADS, SWA_GROUP)[:, :, None, None]
    m = jnp.maximum(jnp.max(s, -1, keepdims=True), sk)
    e = jnp.exp(s - m)
    p = e / (jnp.sum(e, -1, keepdims=True) + jnp.exp(sk - m))
    o = jnp.einsum('bnkgqs,bnskd->bnqkgd', p.astype(v.dtype), vb).reshape(Bn, L, SWA_WIDTH)
    return o * jax.nn.silu(z)


def even_sublayer(x, w_in, w_up_f, b_f, w_up_b, b_b, norm_g, sink, rel_bias, w_out):
    u = x @ w_in
    qa, ka, va, za, gd, qb, kb, vb, zb = jnp.split(u, _offsets(EVEN_SPLITS), axis=-1)
    ya = gla_mixer(qa, ka, va, za, gd, w_up_f, b_f, w_up_b, b_b, norm_g)
    yb = swa_mixer(qb, kb, vb, zb, rel_bias, sink)
    return jnp.concatenate([ya, yb], axis=-1) @ w_out


def odd_sublayer(x, w_in, conv_w, w_out):
    u = x @ w_in
    bg, cg, h, z = jnp.split(u, 4, axis=-1)
    t = jnp.pad(cg * h, ((0, 0), (1, 1), (0, 0)))
    conv = conv_w[0] * t[:, :-2] + conv_w[1] * t[:, 1:-1] + conv_w[2] * t[:, 2:]
    return (jax.nn.silu(z) * bg * conv) @ w_out


def trunk(x, w_in_even, gla_w_up_fwd, gla_b_fwd, gla_w_up_bwd, gla_b_bwd, gla_norm_g, swa_sink,
          rel_bias, w_out_even, w_in_odd, conv_w, w_out_odd, ln_g, ln_b):
    for l in range(DEPTH):
        i = l // 2
        if l % 2 == 0:
            sub = even_sublayer(x, w_in_even[i], gla_w_up_fwd[i], gla_b_fwd[i], gla_w_up_bwd[i], gla_b_bwd[i],
                                gla_norm_g[i], swa_sink[i], rel_bias, w_out_even[i])
        else:
            sub = odd_sublayer(x, w_in_odd[i], conv_w[i], w_out_odd[i])
        x = layer_norm(DN_ALPHA * x + sub, ln_g[l], ln_b[l])
    return x


def setup_inputs(seed: int = 0) -> dict:
    key = jax.random.key(seed)
    ks = jax.random.split(key, 20)
    nrm = lambda k, shape, s: jax.random.normal(k, shape, jnp.float32) * s
    return {
        "x_prompt": nrm(ks[0], (BATCH, SEQ, D_MODEL), 1.0),
        "x_sample": nrm(ks[1], (DEC_BATCH, DEC_SEQ, D_MODEL), 1.0),
        "w_in_even": nrm(ks[2], (N_EVEN, D_MODEL, EVEN_IN), D_MODEL ** -0.5),
        "gla_w_up_fwd": nrm(ks[3], (N_EVEN, GLA_RANK, GLA_KW), GLA_RANK ** -0.5),
        "gla_b_fwd": nrm(ks[4], (N_EVEN, GLA_KW), 0.01),
        "gla_w_up_bwd": nrm(ks[5], (N_EVEN, GLA_RANK, GLA_KW), GLA_RANK ** -0.5),
        "gla_b_bwd": nrm(ks[6], (N_EVEN, GLA_KW), 0.01),
        "gla_norm_g": 1.0 + nrm(ks[7], (N_EVEN, GLA_DV), 0.01),
        "swa_sink": nrm(ks[8], (N_EVEN, SWA_HEADS), 0.5),
        "rel_bias": nrm(ks[9], (REL_BUCKETS, SWA_HEADS), 0.5),
        "w_out_even": nrm(ks[10], (N_EVEN, EVEN_MIX, D_MODEL), EVEN_MIX ** -0.5 * DN_BETA),
        "w_in_odd": nrm(ks[11], (N_ODD, D_MODEL, ODD_IN), D_MODEL ** -0.5),
        "conv_w": nrm(ks[12], (N_ODD, CONV_K, CONV_WIDTH), CONV_K ** -0.5),
        "w_out_odd": nrm(ks[13], (N_ODD, CONV_WIDTH, D_MODEL), CONV_WIDTH ** -0.5 * DN_BETA),
        "ln_g": 1.0 + nrm(ks[14], (DEPTH, D_MODEL), 0.01),
        "ln_b": nrm(ks[15], (DEPTH, D_MODEL), 0.01),
    }


def reference(x_prompt, x_sample, w_in_even, gla_w_up_fwd, gla_b_fwd, gla_w_up_bwd, gla_b_bwd, gla_norm_g,
              swa_sink, rel_bias, w_out_even, w_in_odd, conv_w, w_out_odd, ln_g, ln_b):
    y_prompt = trunk(x_prompt, w_in_even, gla_w_up_fwd, gla_b_fwd, gla_w_up_bwd, gla_b_bwd, gla_norm_g,
                     swa_sink, rel_bias, w_out_even, w_in_odd, conv_w, w_out_odd, ln_g, ln_b)
    y_sample = trunk(x_sample, w_in_even, gla_w_up_fwd, gla_b_fwd, gla_w_up_bwd, gla_b_bwd, gla_norm_g,
                     swa_sink, rel_bias, w_out_even, w_in_odd, conv_w, w_out_odd, ln_g, ln_b)
    return (y_prompt, y_sample)
```

```python
import numpy as np
from contextlib import ExitStack
import concourse.bass as bass
import concourse.mybir as mybir
from concourse.bass_utils import run_bass_kernel_spmd

F32 = mybir.dt.float32
BF16 = mybir.dt.bfloat16
AF = mybir.ActivationFunctionType
ALU = mybir.AluOpType

D = 1024
ALPHA = 4 ** 0.25
LN_EPS = 1e-5
NORM_EPS = 1e-6
NEG = -30000.0

QA0, KA0, QB0, KB0, GD0 = 0, 512, 1024, 1536, 1664
KT0, VT0, ZA0, ZB0, VB0 = 1728, 2240, 2752, 3264, 3776
NC0 = 3904


class Buf:
    def __init__(self, name):
        self.name = name
        self.w = None
        self.r = {}


class Sem:
    def __init__(self, nc, stack, name):
        self.h = stack.enter_context(nc.semaphore(name))
        self.cnt = 0
        self.name = name


class Eng:
    def __init__(self, name, e, sem):
        self.name, self.e, self.sem = name, e, sem
        self.waited = {}


class FW:
    def __init__(self, nc, stack):
        self.nc = nc
        self.stack = stack
        self.engs = {}
        for name, e in (("pe", nc.tensor), ("act", nc.scalar), ("dve", nc.vector), ("pool", nc.gpsimd), ("sp", nc.sync)):
            self.engs[name] = Eng(name, e, Sem(nc, stack, "s_" + name))
        self.dsems = []
        self.n_instr = 0

    def dsem(self, name):
        s = Sem(self.nc, self.stack, name)
        self.dsems.append(s)
        return s

    def _waits(self, eng, reads, writes):
        need = {}

        def add(p):
            if p is None:
                return
            s, c = p
            if need.get(s, 0) < c:
                need[s] = c
        for b in reads:
            add(b.w)
        for b in writes:
            add(b.w)
            for s, c in b.r.items():
                add((s, c))
        for s, c in need.items():
            if s is eng.sem and eng.name == "pe":
                continue
            if eng.waited.get(s, 0) >= c:
                continue
            eng.e.wait_ge(s.h, c)
            eng.waited[s] = c

    def _mark(self, tok, reads, writes):
        s, c = tok
        for b in reads:
            b.r[s] = c
        for b in writes:
            b.w = tok
            b.r = {}

    def op(self, engname, fn, reads=(), writes=()):
        eng = self.engs[engname]
        self._waits(eng, reads, writes)
        ins = fn(eng.e)
        eng.sem.cnt += 1
        ins.then_inc(eng.sem.h, 1)
        self.n_instr += 1
        tok = (eng.sem, eng.sem.cnt)
        self._mark(tok, reads, writes)
        return tok

    def dma(self, engname, out, in_, sem, reads=(), writes=()):
        eng = self.engs[engname]
        self._waits(eng, reads, writes)
        ins = eng.e.dma_start(out=out, in_=in_)
        sem.cnt += 16
        ins.then_inc(sem.h, 16)
        self.n_instr += 1
        tok = (sem, sem.cnt)
        self._mark(tok, reads, writes)
        return tok

    def barrier(self):
        sems = [e.sem for e in self.engs.values()] + self.dsems
        for eng in self.engs.values():
            for s in sems:
                if s is eng.sem or s.cnt == 0:
                    continue
                if eng.waited.get(s, 0) >= s.cnt:
                    continue
                eng.e.wait_ge(s.h, s.cnt)
                eng.waited[s] = s.cnt


class TB:
    def __init__(self, t, name):
        self.t = t
        self.b = Buf(name)


def sched(gens):
    gens = list(gens)
    while gens:
        for g_ in list(gens):
            try:
                next(g_)
            except StopIteration:
                gens.remove(g_)


def build(nP, nS):
    JOBS = [dict(name="P", n=nP, NO=3 * nP, j=0, woff=0), dict(name="S", n=nS, NO=nS, j=1, woff=2 * 3 * nP)]
    NWT = 2 * (3 * nP + nS)
    nc = bass.Bass("TRN2", target_bir_lowering=False)

    def din(name, shape):
        return nc.dram_tensor(name, shape, F32, kind="ExternalInput").ap()
    for J in JOBS:
        J["x_d"] = din("xext" + J["name"], [(J["n"] + 4) * 128, D])
        J["xo_d"] = din("xoth" + J["name"], [J["NO"] * 128, D])
        J["y_d"] = nc.dram_tensor("y" + J["name"], [J["n"] * 128, D], F32, kind="ExternalOutput").ap()
        J["ob_d"] = nc.dram_tensor("ob_scr" + J["name"], [(J["n"] + 4) * 128, 512], F32).ap()
        J["x1_d"] = nc.dram_tensor("x1_scr" + J["name"], [(J["n"] + 4) * 128, D], F32).ap()
        J["qkv_d"] = nc.dram_tensor("qkv_scr" + J["name"], [(J["n"] + 4) * 128, 4, 512], BF16).ap()
    w0_d = din("w0", [D, NC0])
    wo0_d = din("wo0", [D, D])
    w1_d = din("w1", [D, 4 * D])
    wo1_d = din("wo1", [D, D])
    wup_d = din("wup", [64, 512])
    bias_d = din("biasT", [128, 8 * 3 * 128])
    sink_d = din("sink", [128, 8])
    gn_d = din("gn", [128, 512])
    lng_d = din("lng", [128, 2 * D])
    lnb_d = din("lnb", [128, 2 * D])
    cw_d = din("convw", [128, 24])
    fl_d = din("flags", [128, 4])
    wt_d = din("wts", [128, NWT])

    with ExitStack() as gst:
        fw = FW(nc, gst)
        op, dma = fw.op, fw.dma
        uid = [0]

        def sbt(st, name, shape, dt):
            uid[0] += 1
            return TB(st.enter_context(nc.sbuf_tensor("sb%d_%s" % (uid[0], name), shape, dt)), name)

        def pst(st, name, shape, dt):
            return TB(st.enter_context(nc.psum_tensor("ps_" + name, shape, dt)), name)

        ident = sbt(gst, "ident", [128, 128], BF16)
        mU = sbt(gst, "mU", [128, 128], F32)
        mL = sbt(gst, "mL", [128, 128], F32)
        mUs = sbt(gst, "mUs", [128, 128], F32)
        mLs = sbt(gst, "mLs", [128, 128], F32)
        m01U = sbt(gst, "m01U", [128, 512], BF16)
        m01Ls = sbt(gst, "m01Ls", [128, 512], BF16)
        wup = sbt(gst, "wup", [64, 512], BF16)
        rfl = sbt(gst, "rfl", [128, 4], F32)
        negm = sbt(gst, "negm", [128, 4], F32)
        wts = sbt(gst, "wts", [128, NWT], F32)
        m16c = sbt(gst, "m16c", [128, 1], F32)
        SinF = [sbt(gst, "SinF%d" % i, [128, 512], F32) for i in range(2)]
        Pdec = sbt(gst, "Pdec", [128, 4], F32)
        S = sbt(gst, "S", [128, 512], F32)
        Sbf = sbt(gst, "Sbf", [128, 512], BF16)
        gda = sbt(gst, "gda", [64, 128], BF16)
        cn = [0]

        def dma_c(out, in_, writes):
            cn[0] += 1
            return dma("sp", out, in_, fw.dsem("dc%d" % cn[0]), writes=writes)
        dx = [fw.dsem("dx%d" % i) for i in range(6)]
        dst = [fw.dsem("dst%d" % i) for i in range(2)]
        dob = [fw.dsem("dob%d" % i) for i in range(2)]
        dw = [fw.dsem("dw%d" % i) for i in range(2)]
        dqs = [fw.dsem("dqs%d" % i) for i in range(2)]
        dql = [fw.dsem("dql%d" % i) for i in range(2)]

        def mask(tb, val, cm, base, nrep=1):
            pat = [[-cm, 128]] if nrep == 1 else [[0, nrep], [-cm, 128]]
            ap = tb.t[:] if nrep == 1 else tb.t[:].rearrange("p (a b) -> p a b", a=nrep)
            op("pool", lambda e: e.memset(tb.t[:], val), writes=[tb.b])
            op("pool", lambda e: e.affine_select(out=ap, in_=ap, pattern=pat, compare_op=ALU.is_ge, fill=0.0,
                                                 base=base, channel_multiplier=cm), reads=[tb.b], writes=[tb.b])
        mask(mU, -0.0625, -1, 0)
        mask(mL, -0.0625, 1, 0)
        mask(mUs, -0.0625, -1, -1)
        mask(mLs, -0.0625, 1, -1)
        mask(m01U, 1.0, -1, 0, 4)
        mask(m01Ls, 1.0, 1, -1, 4)
        with ExitStack() as st0:
            tmpf = sbt(st0, "tmpf", [128, 512], F32)
            op("pool", lambda e: e.memset(tmpf.t[:, 0:128], 1.0), writes=[tmpf.b])
            op("pool", lambda e: e.affine_select(out=tmpf.t[:, 0:128], in_=tmpf.t[:, 0:128], pattern=[[-1, 128]],
                                                 compare_op=ALU.is_equal, fill=0.0, base=0, channel_multiplier=1),
               reads=[tmpf.b], writes=[tmpf.b])
            op("dve", lambda e: e.tensor_copy(out=ident.t[:], in_=tmpf.t[:, 0:128]), reads=[tmpf.b], writes=[ident.b])
            dma_c(tmpf.t[0:64, :], wup_d[:, :], [tmpf.b])
            op("dve", lambda e: e.tensor_copy(out=wup.t[:], in_=tmpf.t[0:64, :]), reads=[tmpf.b], writes=[wup.b])
            dma_c(rfl.t[:], fl_d[:, :], [rfl.b])
            dma_c(wts.t[:], wt_d[:, :], [wts.b])
            op("pool", lambda e: e.memset(m16c.t[:], -0.0625), writes=[m16c.b])
            op("dve", lambda e: e.tensor_scalar(out=negm.t[:], in0=rfl.t[:], scalar1=-1.0, scalar2=-NEG, op0=ALU.add, op1=ALU.mult),
               reads=[rfl.b], writes=[negm.b])
            op("pool", lambda e: e.memset(gda.t[:], 1.0), writes=[gda.b])
            fw.barrier()

        pT = pst(gst, "pT", [128, 8, 128], BF16)
        banks = [pst(gst, "pb%d" % i, [128, 512], F32) for i in range(7)]

        class Rot:
            def __init__(self, lst):
                self.l, self.i = lst, 0

            def __call__(self):
                b = self.l[self.i % len(self.l)]
                self.i += 1
                return b

        wst_n = [0]

        def load_weight(stg, dst, src_d, col0, ncols, dcol0):
            src = src_d.rearrange("(c p) n -> p c n", p=128)
            c = 0
            while c < ncols:
                n = min(512, ncols - c)
                i = wst_n[0] % 2
                wst_n[0] += 1
                s = stg[i]
                dma("sp", s.t[:, :, 0:n], src[:, :, col0 + c:col0 + c + n], dw[i], writes=[s.b])
                eng = ("act", "dve")[wst_n[0] % 2]
                dd = dst.t[:, :, dcol0 + c:dcol0 + c + n]
                if eng == "act":
                    op("act", lambda e: e.copy(out=dd, in_=s.t[:, :, 0:n]), reads=[s.b], writes=[dst.b])
                else:
                    op(eng, lambda e: e.tensor_copy(out=dd, in_=s.t[:, :, 0:n]), reads=[s.b], writes=[dst.b])
                c += n

        def proj_tm(bankf, W, xT, col0, n):
            pb = bankf()
            for kc in range(8):
                op("pe", lambda e: e.matmul(pb.t[:, 0:n], lhsT=xT.t[:, kc, :], rhs=W.t[:, kc, col0:col0 + n],
                                            start=(kc == 0), stop=(kc == 7)), reads=[xT.b, W.b], writes=[pb.b])
            return pb

        def proj_fm(bankf, W, xT, col0, nch, m=128):
            pb = bankf()
            for ch in range(nch):
                for kc in range(8):
                    op("pe", lambda e: e.matmul(pb.t[0:m, ch * 128:(ch + 1) * 128],
                                                lhsT=W.t[:, kc, col0 + ch * m:col0 + (ch + 1) * m], rhs=xT.t[:, kc, :],
                                                start=(kc == 0), stop=(kc == 7)), reads=[xT.b, W.b], writes=[pb.b])
            return pb

        def front(fwd, bankf, W, xsl, xb, xT, lg, dec):
            r0 = 0 if fwd else 32
            mFM, mTM = (mU, mLs) if fwd else (mL, mUs)
            for c in range(8):
                op("pe", lambda e: e.transpose(out=pT.t[:, c, :], in_=xb.t[:, c * 128:(c + 1) * 128], identity=ident.t[:]),
                   reads=[xb.b, ident.b], writes=[pT.b])
            op("act", lambda e: e.copy(out=xT.t[:], in_=pT.t[:]), reads=[pT.b], writes=[xT.b])
            yield
            pg = proj_fm(bankf, W, xT, GD0, 1, m=64)
            yield
            op("act", lambda e: e.copy(out=gda.t[r0:r0 + 16, :], in_=pg.t[r0:r0 + 16, 0:128]), reads=[pg.b], writes=[gda.b])
            yield
            pl = bankf()
            op("pe", lambda e: e.matmul(pl.t[:, :], lhsT=gda.t[r0:r0 + 32, :], rhs=wup.t[r0:r0 + 32, :], start=True, stop=True),
               reads=[gda.b, wup.b], writes=[pl.b])
            yield
            op("act", lambda e: e.activation(out=lg.t[:], in_=pl.t[:], func=AF.Exp, scale=-1.0), reads=[pl.b], writes=[lg.b])
            op("act", lambda e: e.activation(out=lg.t[:], in_=lg.t[:], func=AF.Ln, bias=1.0), reads=[lg.b], writes=[lg.b])
            yield
            pc = bankf()
            for h in range(4):
                hs = slice(h * 128, (h + 1) * 128)
                op("pe", lambda e: e.matmul(pc.t[:, hs], lhsT=lg.t[:, hs], rhs=mFM.t[:], start=True, stop=True),
                   reads=[lg.b, mFM.b], writes=[pc.b])
            yield
            pp = bankf()
            op("pe", lambda e: e.matmul(pp.t[:, :], lhsT=mTM.t[:], rhs=lg.t[:], start=True, stop=True),
               reads=[lg.b, mTM.b], writes=[pp.b])
            yield
            ec, enc, epre = dec
            op("act", lambda e: e.activation(out=ec.t[:], in_=pc.t[:], func=AF.Exp), reads=[pc.b], writes=[ec.b])
            op("act", lambda e: e.activation(out=enc.t[:], in_=pc.t[:], func=AF.Exp, scale=-1.0), reads=[pc.b], writes=[enc.b])
            yield
            op("act", lambda e: e.activation(out=epre.t[:], in_=pp.t[:], func=AF.Exp), reads=[pp.b], writes=[epre.b])
            yield

        def gla_main(fwd, bankf, W, xT, dec, g, o_evac, raw=None, pre=None):
            ec, enc, epre = dec
            m01, col = (m01U, 127) if fwd else (m01Ls, 0)
            qt, kt, kh, sT = g["qt"], g["kt"], g["kh"], g["sT"]
            if pre is None:
                rq, rk, rkt, v = raw
                for (c0, fm, dst_) in ((QA0, True, rq), (KA0, True, rk), (VT0, False, v)):
                    pb = proj_fm(bankf, W, xT, c0, 4) if fm else proj_tm(bankf, W, xT, c0, 512)
                    op("act", lambda e: e.copy(out=dst_.t[:], in_=pb.t[:]), reads=[pb.b], writes=[dst_.b])
                    yield
                for h in range(4):
                    op("pe", lambda e: e.transpose(out=pT.t[:, h, :], in_=rk.t[:, h * 128:(h + 1) * 128], identity=ident.t[:]),
                       reads=[rk.b, ident.b], writes=[pT.b])
                op("act", lambda e: e.copy(out=rkt.t[:].rearrange("p (h d) -> p h d", h=4), in_=pT.t[:, 0:4, :]), reads=[pT.b], writes=[rkt.b])
                yield
            else:
                rq, rk, rkt, v = pre
            op("dve", lambda e: e.scalar_tensor_tensor(out=qt.t[:], in0=rq.t[:], scalar=128 ** -0.5, in1=ec.t[:], op0=ALU.mult, op1=ALU.mult),
               reads=[rq.b, ec.b], writes=[qt.b])
            op("dve", lambda e: e.tensor_tensor(out=kt.t[:], in0=rk.t[:], in1=enc.t[:], op=ALU.mult), reads=[rk.b, enc.b], writes=[kt.b])
            yield
            op("dve", lambda e: e.tensor_tensor(out=kh.t[:], in0=rkt.t[:], in1=epre.t[:], op=ALU.mult), reads=[rkt.b, epre.b], writes=[kh.b])
            yield
            psc = bankf()
            for h in range(4):
                hs = slice(h * 128, (h + 1) * 128)
                op("pe", lambda e: e.matmul(psc.t[:, hs], lhsT=kt.t[:, hs], rhs=qt.t[:, hs], start=True, stop=True),
                   reads=[kt.b, qt.b], writes=[psc.b])
            op("dve", lambda e: e.tensor_tensor(out=sT.t[:], in0=psc.t[:], in1=m01.t[:], op=ALU.mult), reads=[psc.b, m01.b], writes=[sT.b])
            yield
            po = bankf()
            for h in range(4):
                hs = slice(h * 128, (h + 1) * 128)
                op("pe", lambda e: e.matmul(po.t[:, hs], lhsT=sT.t[:, hs], rhs=v.t[:, hs], start=True, stop=False),
                   reads=[sT.b, v.b], writes=[po.b])
                op("pe", lambda e: e.matmul(po.t[:, hs], lhsT=qt.t[:, hs], rhs=Sbf.t[:, hs], start=False, stop=True),
                   reads=[qt.b, Sbf.b], writes=[po.b])
            o_evac(po)
            yield
            pu = bankf()
            for h in range(4):
                hs = slice(h * 128, (h + 1) * 128)
                op("pe", lambda e: e.matmul(pu.t[:, hs], lhsT=kh.t[:, hs], rhs=v.t[:, hs], start=True, stop=True),
                   reads=[kh.b, v.b], writes=[pu.b])
            for h in range(4):
                hs = slice(h * 128, (h + 1) * 128)
                cc = h * 128 + col
                op("dve", lambda e: e.scalar_tensor_tensor(out=S.t[:, hs], in0=S.t[:, hs], scalar=ec.t[:, cc:cc + 1], in1=pu.t[:, hs],
                                                           op0=ALU.mult, op1=ALU.add), reads=[S.b, ec.b, pu.b], writes=[S.b])
            op("pool", lambda e: e.tensor_copy(out=Sbf.t[:], in_=S.t[:]), reads=[S.b], writes=[Sbf.b])
            yield

        def reset_state(zero):
            if zero:
                op("pool", lambda e: e.memset(S.t[:], 0.0), writes=[S.b])
            else:
                op("dve", lambda e: e.tensor_scalar(out=S.t[:], in0=S.t[:], scalar1=rfl.t[:, 0:1], scalar2=None, op0=ALU.mult),
                   reads=[S.b, rfl.b], writes=[S.b])
            op("pool", lambda e: e.tensor_copy(out=Sbf.t[:], in_=S.t[:]), reads=[S.b], writes=[Sbf.b])

        def silu_from_psum(pz, ez, out_tb):
            op("act", lambda e: e.activation(out=ez.t[:], in_=pz.t[:], func=AF.Exp, scale=-1.0), reads=[pz.b], writes=[ez.b])
            op("act", lambda e: e.activation(out=ez.t[:], in_=ez.t[:], func=AF.Ln, bias=1.0), reads=[ez.b], writes=[ez.b])
            op("act", lambda e: e.activation(out=ez.t[:], in_=ez.t[:], func=AF.Exp, scale=-1.0), reads=[ez.b], writes=[ez.b])
            op("dve", lambda e: e.tensor_tensor(out=out_tb.t[:], in0=pz.t[:], in1=ez.t[:], op=ALU.mult), reads=[pz.b, ez.b], writes=[out_tb.b])

        def layer_norm(junk, stat, lng, lnb, h1):
            op("act", lambda e: e.activation(out=junk.t[:], in_=h1.t[:], func=AF.Copy, accum_out=stat.t[:, 0:1]),
               reads=[h1.b], writes=[junk.b, stat.b])
            op("act", lambda e: e.activation(out=junk.t[:], in_=h1.t[:], func=AF.Square, accum_out=stat.t[:, 1:2]),
               reads=[h1.b], writes=[junk.b, stat.b])
            op("dve", lambda e: e.tensor_scalar(out=stat.t[:, 0:2], in0=stat.t[:, 0:2], scalar1=1.0 / D, scalar2=None, op0=ALU.mult),
               reads=[stat.b], writes=[stat.b])
            op("dve", lambda e: e.tensor_tensor(out=stat.t[:, 2:3], in0=stat.t[:, 0:1], in1=stat.t[:, 0:1], op=ALU.mult),
               reads=[stat.b], writes=[stat.b])
            op("dve", lambda e: e.tensor_tensor(out=stat.t[:, 2:3], in0=stat.t[:, 1:2], in1=stat.t[:, 2:3], op=ALU.subtract),
               reads=[stat.b], writes=[stat.b])
            op("act", lambda e: e.activation(out=stat.t[:, 3:4], in_=stat.t[:, 2:3], func=AF.Ln, bias=LN_EPS), reads=[stat.b], writes=[stat.b])
            op("act", lambda e: e.activation(out=stat.t[:, 3:4], in_=stat.t[:, 3:4], func=AF.Exp, scale=-0.5), reads=[stat.b], writes=[stat.b])
            op("dve", lambda e: e.tensor_scalar(out=h1.t[:], in0=h1.t[:], scalar1=stat.t[:, 0:1], scalar2=stat.t[:, 3:4],
                                                op0=ALU.subtract, op1=ALU.mult), reads=[h1.b, stat.b], writes=[h1.b])
            op("dve", lambda e: e.tensor_tensor(out=h1.t[:], in0=h1.t[:], in1=lng.t[:], op=ALU.mult), reads=[h1.b, lng.b], writes=[h1.b])
            op("pool", lambda e: e.tensor_tensor(out=h1.t[:], in0=h1.t[:], in1=lnb.t[:], op=ALU.add), reads=[h1.b, lnb.b], writes=[h1.b])

        def init_state(src):
            op("dve", lambda e: e.tensor_copy(out=S.t[:], in_=src.t[:]), reads=[src.b], writes=[S.b])
            op("pool", lambda e: e.tensor_copy(out=Sbf.t[:], in_=S.t[:]), reads=[S.b], writes=[Sbf.b])

        def gla_bufs(st):
            g = {}
            for n in ("qt", "kt", "kh", "sT"):
                g[n] = sbt(st, "g_" + n, [128, 512], BF16)
            return g

        def run_pipe(stages):
            smin = min(lo + lag for (_, lag, lo, hi) in stages)
            smax = max(hi - 1 + lag for (_, lag, lo, hi) in stages)
            for s_ in range(smin, smax + 1):
                sched([fn(s_ - lag) for (fn, lag, lo, hi) in stages if lo <= s_ - lag < hi])

        def run_pipe2(stages, NU):
            smin = min(lag for (_, lag, _) in stages)
            smax = NU - 1 + max(lag for (_, lag, _) in stages)
            for s_ in range(smin, smax + 1):
                sched([fn(s_ - lag) for (fn, lag, pred) in stages if 0 <= s_ - lag < NU and pred(s_ - lag)])

        UE = [(J, e) for J in JOBS for e in range(J["n"] + 4)]
        UA = [(J, e) for J in JOBS for e in range(J["n"] + 2, 0, -1)]
        UP = [(J, t) for J in JOBS for t in range(J["NO"])]

        def main_e(u):
            J, e = UE[u]
            return 1 <= e < J["n"] + 3

        def own_e(u):
            J, e = UE[u]
            return 2 <= e < J["n"] + 2

        stW = ExitStack()
        stW.__enter__()
        W0 = sbt(stW, "W0", [128, 8, NC0], BF16)
        WO0 = sbt(stW, "WO0", [128, 8, D], BF16)
        with ExitStack() as st2:
            stg = [sbt(st2, "stgA%d" % i, [128, 8, 512], F32) for i in range(2)]
            load_weight(stg, W0, w0_d, 0, NC0, 0)
            load_weight(stg, WO0, wo0_d, 0, D, 0)
            fw.barrier()
        with ExitStack() as st:
            WA = W0
            SinB = [sbt(st, "SinB%d" % i, [128, 512], F32) for i in range(2)]
            xs = [sbt(st, "xsA%d" % i, [128, D], F32) for i in range(3)]
            xb = [sbt(st, "xbA%d" % i, [128, D], BF16) for i in range(2)]
            xT = [sbt(st, "xTA%d" % i, [128, 8, 128], BF16) for i in range(2)]

            with ExitStack() as st2:
                lgf = sbt(st2, "lgf", [128, 512], F32)
                lgb = sbt(st2, "lgb", [128, 512], F32)
                ebs = [sbt(st2, "ebs%d" % i, [128, 512], F32) for i in range(2)]
                epr = [sbt(st2, "epr%d" % i, [128, 512], F32) for i in range(2)]
                Dfb = [sbt(st2, "Dfb%d" % i, [128, 8], F32) for i in range(2)]
                khf = sbt(st2, "khf", [128, 512], BF16)
                khb = sbt(st2, "khb", [128, 512], BF16)
                vpp = sbt(st2, "vpp", [128, 512], BF16)
                bankA, bankB = Rot(banks[0:4]), Rot(banks[4:7])
                if True:
                    def p_load(u):
                        J, t = UP[u]
                        xo_d = J["xo_d"]
                        dma("sp", xs[u % 3].t[:], xo_d[t * 128:(t + 1) * 128, :], dx[u % 3], writes=[xs[u % 3].b])
                        op("pool", lambda e: e.tensor_copy(out=xb[u % 2].t[:], in_=xs[u % 3].t[:]), reads=[xs[u % 3].b], writes=[xb[u % 2].b])
                        yield

                    def p_front(u):
                        J, t = UP[u]
                        woff = J["woff"]
                        xb_, xT_ = xb[u % 2], xT[u % 2]
                        wf = wts.t[:, woff + 2 * t:woff + 2 * t + 1]
                        wb = wts.t[:, woff + 2 * t + 1:woff + 2 * t + 2]
                        for c in range(8):
                            op("pe", lambda e: e.transpose(out=pT.t[:, c, :], in_=xb_.t[:, c * 128:(c + 1) * 128], identity=ident.t[:]),
                               reads=[xb_.b, ident.b], writes=[pT.b])
                        op("act", lambda e: e.copy(out=xT_.t[:], in_=pT.t[:]), reads=[pT.b], writes=[xT_.b])
                        yield
                        pg = proj_fm(bankA, WA, xT_, GD0, 1, m=64)
                        op("act", lambda e: e.copy(out=gda.t[0:16, :], in_=pg.t[0:16, 0:128]), reads=[pg.b], writes=[gda.b])
                        op("act", lambda e: e.copy(out=gda.t[32:48, :], in_=pg.t[32:48, 0:128]), reads=[pg.b], writes=[gda.b])
                        yield
                        plf = bankA()
                        op("pe", lambda e: e.matmul(plf.t[:, :], lhsT=gda.t[0:32, :], rhs=wup.t[0:32, :], start=True, stop=True),
                           reads=[gda.b, wup.b], writes=[plf.b])
                        plb = bankA()
                        op("pe", lambda e: e.matmul(plb.t[:, :], lhsT=gda.t[32:64, :], rhs=wup.t[32:64, :], start=True, stop=True),
                           reads=[gda.b, wup.b], writes=[plb.b])
                        for (pl, lg_) in ((plf, lgf), (plb, lgb)):
                            op("act", lambda e: e.activation(out=lg_.t[:], in_=pl.t[:], func=AF.Exp, scale=-1.0), reads=[pl.b], writes=[lg_.b])
                            op("act", lambda e: e.activation(out=lg_.t[:], in_=lg_.t[:], func=AF.Ln, bias=1.0), reads=[lg_.b], writes=[lg_.b])
                        yield
                        ppf = bankA()
                        op("pe", lambda e: e.matmul(ppf.t[:, :], lhsT=mLs.t[:], rhs=lgf.t[:], start=True, stop=True),
                           reads=[lgf.b, mLs.b], writes=[ppf.b])
                        op("act", lambda e: e.activation(out=ebs[u % 2].t[:], in_=ppf.t[:], func=AF.Exp), reads=[ppf.b], writes=[ebs[u % 2].b])
                        yield
                        ppb = bankA()
                        op("pe", lambda e: e.matmul(ppb.t[:, :], lhsT=mUs.t[:], rhs=lgb.t[:], start=True, stop=True),
                           reads=[lgb.b, mUs.b], writes=[ppb.b])
                        op("act", lambda e: e.activation(out=epr[u % 2].t[:], in_=ppb.t[:], func=AF.Exp), reads=[ppb.b], writes=[epr[u % 2].b])
                        yield
                        ptot = bankA()
                        for di, lg_ in enumerate((lgf, lgb)):
                            for h in range(4):
                                cc = di * 4 + h
                                op("pe", lambda e: e.matmul(ptot.t[:, cc:cc + 1], lhsT=lg_.t[:, h * 128:(h + 1) * 128], rhs=m16c.t[:, 0:1],
                                                            start=True, stop=True), reads=[lg_.b, m16c.b], writes=[ptot.b])
                        dd = Dfb[u % 2]
                        op("act", lambda e: e.activation(out=dd.t[:, 0:4], in_=ptot.t[:, 0:4], func=AF.Exp, scale=wf), reads=[ptot.b, wts.b], writes=[dd.b])
                        op("act", lambda e: e.activation(out=dd.t[:, 4:8], in_=ptot.t[:, 4:8], func=AF.Exp, scale=wb), reads=[ptot.b, wts.b], writes=[dd.b])
                        yield

                    def p_upd(u):
                        J, t = UP[u]
                        woff = J["woff"]
                        Sf, Ab = SinF[J["j"]], SinB[J["j"]]
                        if t == 0:
                            op("pool", lambda e: e.memset(Sf.t[:], 0.0), writes=[Sf.b])
                            op("pool", lambda e: e.memset(Ab.t[:], 0.0), writes=[Ab.b])
                            op("pool", lambda e: e.memset(Pdec.t[:], 1.0), writes=[Pdec.b])
                        xT_ = xT[u % 2]
                        wf = wts.t[:, woff + 2 * t:woff + 2 * t + 1]
                        wb = wts.t[:, woff + 2 * t + 1:woff + 2 * t + 2]
                        dd = Dfb[u % 2]
                        pkt = proj_tm(bankB, WA, xT_, KT0, 512)
                        op("dve", lambda e: e.scalar_tensor_tensor(out=khf.t[:], in0=pkt.t[:], scalar=wf, in1=ebs[u % 2].t[:], op0=ALU.mult, op1=ALU.mult),
                           reads=[pkt.b, wts.b, ebs[u % 2].b], writes=[khf.b])
                        op("dve", lambda e: e.scalar_tensor_tensor(out=khb.t[:], in0=pkt.t[:], scalar=wb, in1=epr[u % 2].t[:], op0=ALU.mult, op1=ALU.mult),
                           reads=[pkt.b, wts.b, epr[u % 2].b], writes=[khb.b])
                        yield
                        pvt = proj_tm(bankB, WA, xT_, VT0, 512)
                        op("act", lambda e: e.copy(out=vpp.t[:], in_=pvt.t[:]), reads=[pvt.b], writes=[vpp.b])
                        yield
                        puf = bankB()
                        for h in range(4):
                            hs = slice(h * 128, (h + 1) * 128)
                            op("pe", lambda e: e.matmul(puf.t[:, hs], lhsT=khf.t[:, hs], rhs=vpp.t[:, hs], start=True, stop=True),
                               reads=[khf.b, vpp.b], writes=[puf.b])
                        for h in range(4):
                            hs = slice(h * 128, (h + 1) * 128)
                            op("dve", lambda e: e.scalar_tensor_tensor(out=Sf.t[:, hs], in0=Sf.t[:, hs], scalar=dd.t[:, h:h + 1], in1=puf.t[:, hs],
                                                                       op0=ALU.mult, op1=ALU.add), reads=[Sf.b, dd.b, puf.b], writes=[Sf.b])
                        yield
                        pub = bankB()
                        for h in range(4):
                            hs = slice(h * 128, (h + 1) * 128)
                            op("pe", lambda e: e.matmul(pub.t[:, hs], lhsT=khb.t[:, hs], rhs=vpp.t[:, hs], start=True, stop=True),
                               reads=[khb.b, vpp.b], writes=[pub.b])
                        for h in range(4):
                            hs = slice(h * 128, (h + 1) * 128)
                            op("dve", lambda e: e.scalar_tensor_tensor(out=Ab.t[:, hs], in0=pub.t[:, hs], scalar=Pdec.t[:, h:h + 1], in1=Ab.t[:, hs],
                                                                       op0=ALU.mult, op1=ALU.add), reads=[Ab.b, Pdec.b, pub.b], writes=[Ab.b])
                        op("dve", lambda e: e.tensor_tensor(out=Pdec.t[:], in0=Pdec.t[:], in1=dd.t[:, 4:8], op=ALU.mult), reads=[Pdec.b, dd.b], writes=[Pdec.b])
                        yield

                    run_pipe2([(p_load, -2, lambda u: True), (p_front, -1, lambda u: True), (p_upd, 0, lambda u: True)], len(UP))
                fw.barrier()

            lg = sbt(st, "lgA", [128, 512], F32)
            decs = [[sbt(st, "decA%d_%d" % (i, j), [128, 512], F32) for j in range(3)] for i in range(2)]
            obs = [sbt(st, "obsA%d" % i, [128, 512], F32) for i in range(2)]
            raws = [[sbt(st, "rawA%d_%d" % (i, k), [128, 512], BF16) for k in range(4)] for i in range(2)]
            g = gla_bufs(st)
            bankA, bankB = Rot(banks[0:3]), Rot(banks[3:7])
            if True:
                def a_load(i):
                    J, t = UA[i]
                    x_d = J["x_d"]
                    dma("sp", xs[i % 3].t[:], x_d[t * 128:(t + 1) * 128, :], dx[i % 3], writes=[xs[i % 3].b])
                    op("pool", lambda e: e.tensor_copy(out=xb[i % 2].t[:], in_=xs[i % 3].t[:]), reads=[xs[i % 3].b], writes=[xb[i % 2].b])
                    yield

                def a_front(i):
                    yield from front(False, bankA, WA, xs[i % 3], xb[i % 2], xT[i % 2], lg, decs[i % 2])

                def a_main(i):
                    J, t = UA[i]
                    ob_d = J["ob_d"]
                    if t == J["n"] + 2:
                        init_state(SinB[J["j"]])
                    ob = obs[i % 2]

                    def o_evac(po):
                        op("act", lambda e: e.copy(out=ob.t[:], in_=po.t[:]), reads=[po.b], writes=[ob.b])
                        dma("sp", ob_d[t * 128:(t + 1) * 128, :], ob.t[:], dob[i % 2], reads=[ob.b])
                    rw = raws[i % 2]
                    yield from gla_main(False, bankB, WA, xT[i % 2], decs[i % 2], g, o_evac, raw=rw)
                    qkv_d = J["qkv_d"]
                    for k in range(4):
                        dma("sp", qkv_d[t * 128:(t + 1) * 128, k, :], rw[k].t[:], dqs[i % 2], reads=[rw[k].b])
                    for k in range(4):
                        rw[k].b.r[dqs[i % 2]] = dqs[i % 2].cnt
                    yield

                run_pipe2([(a_load, -2, lambda u: True), (a_front, -1, lambda u: True), (a_main, 0, lambda u: True)], len(UA))
            fw.barrier()

        with ExitStack() as st:
            WB = W0
            WO = WO0
            biasT = sbt(st, "biasT", [128, 8, 3, 128], F32)
            esink = sbt(st, "esink", [128, 8], F32)
            gn4 = sbt(st, "gn4", [128, 512], F32)
            lng = sbt(st, "lng", [128, D], F32)
            lnb = sbt(st, "lnb", [128, D], F32)
            dma_c(biasT.t[:].rearrange("p h b q -> p (h b q)"), bias_d[:, :], [biasT.b])
            dma_c(esink.t[:], sink_d[:, :], [esink.b])
            dma_c(gn4.t[:], gn_d[:, :], [gn4.b])
            dma_c(lng.t[:], lng_d[:, 0:D], [lng.b])
            dma_c(lnb.t[:], lnb_d[:, 0:D], [lnb.b])
            op("act", lambda e: e.activation(out=esink.t[:], in_=esink.t[:], func=AF.Exp), reads=[esink.b], writes=[esink.b])
            op("pool", lambda e: e.affine_select(out=biasT.t[:, :, 0, :], in_=biasT.t[:, :, 0, :], pattern=[[0, 8], [-1, 128]],
                                                 compare_op=ALU.is_ge, fill=NEG, base=0, channel_multiplier=1),
               reads=[biasT.b], writes=[biasT.b])
            op("pool", lambda e: e.affine_select(out=biasT.t[:, :, 2, :], in_=biasT.t[:, :, 2, :], pattern=[[0, 8], [1, 128]],
                                                 compare_op=ALU.is_ge, fill=NEG, base=0, channel_multiplier=-1),
               reads=[biasT.b], writes=[biasT.b])

            xs = [sbt(st, "xsB%d" % i, [128, D], F32) for i in range(5)]
            xb = [sbt(st, "xbB%d" % i, [128, D], BF16) for i in range(2)]
            lg = sbt(st, "lgB", [128, 512], F32)
            xT = [sbt(st, "xTB%d" % i, [128, 8, 128], BF16) for i in range(2)]
            decs = [[sbt(st, "decB%d_%d" % (i, j), [128, 512], F32) for j in range(3)] for i in range(2)]
            obl = [sbt(st, "oblB%d" % i, [128, 512], F32) for i in range(2)]
            raws = [[sbt(st, "rawB%d_%d" % (i, k), [128, 512], BF16) for k in range(4)] for i in range(2)]
            g = gla_bufs(st)
            o_sb = sbt(st, "o_sb", [128, 512], F32)
            ss = sbt(st, "ssB", [128, 8], F32)
            ez = sbt(st, "ezB", [128, 512], F32)
            gz = sbt(st, "gzB", [128, 512], F32)
            szb = [sbt(st, "szbB%d" % i, [128, 512], F32) for i in range(2)]
            ybuf = [sbt(st, "yB%d" % i, [128, D], BF16) for i in range(3)]
            qTB = [sbt(st, "qTB%d" % i, [128, 512], BF16) for i in range(2)]
            kTB = [sbt(st, "kTB%d" % i, [128, 128], BF16) for i in range(4)]
            vB = [sbt(st, "vB%d" % i, [128, 2, 65], BF16) for i in range(4)]
            sc = [sbt(st, "scB%d" % i, [128, 384], F32) for i in range(2)]
            pTs = [sbt(st, "pTsB%d" % i, [128, 384], BF16) for i in range(2)]
            den = sbt(st, "denB", [128, 8], F32)
            yT = sbt(st, "yTB", [128, 8, 128], BF16)
            h1s = [sbt(st, "h1B%d" % i, [128, D], F32) for i in range(2)]
            junk = sbt(st, "junkB", [128, D], BF16)
            sq = junk
            stat = sbt(st, "statB", [128, 4], F32)
            for v3 in vB:
                op("pool", lambda e: e.memset(v3.t[:], 1.0), writes=[v3.b])
            bankA, bankB, bankC = Rot(banks[0:2]), Rot(banks[2:4]), Rot(banks[5:7])

            if True:
                def b_load(u):
                    J, e_ = UE[u]
                    x_d, t = J["x_d"], u
                    dma("sp", xs[t % 5].t[:], x_d[e_ * 128:(e_ + 1) * 128, :], dx[t % 5], writes=[xs[t % 5].b])
                    op("pool", lambda e: e.tensor_copy(out=xb[t % 2].t[:], in_=xs[t % 5].t[:]), reads=[xs[t % 5].b], writes=[xb[t % 2].b])
                    yield

                def b_loadob(t):
                    J, e_ = UE[t]
                    ob_d = J["ob_d"]
                    dma("sp", obl[t % 2].t[:], ob_d[e_ * 128:(e_ + 1) * 128, :], dob[t % 2], writes=[obl[t % 2].b])
                    qkv_d, rw = J["qkv_d"], raws[t % 2]
                    for k in range(4):
                        dma("sp", rw[k].t[:], qkv_d[e_ * 128:(e_ + 1) * 128, k, :], dql[t % 2], writes=[rw[k].b])
                    for k in range(4):
                        rw[k].b.w = (dql[t % 2], dql[t % 2].cnt)
                    yield

                def b_front(t):
                    yield from front(True, bankA, WB, xs[t % 5], xb[t % 2], xT[t % 2], lg, decs[t % 2])
                    pkb = proj_fm(bankA, WB, xT[t % 2], KB0, 1)
                    op("act", lambda e: e.copy(out=kTB[t % 4].t[:], in_=pkb.t[:, 0:128]), reads=[pkb.b], writes=[kTB[t % 4].b])
                    yield
                    pvb = proj_tm(bankA, WB, xT[t % 2], VB0, 128)
                    vs = vB[t % 4]
                    op("act", lambda e: e.copy(out=vs.t[:, :, 0:64], in_=pvb.t[:, 0:128].rearrange("p (a b) -> p a b", a=2)),
                       reads=[pvb.b], writes=[vs.b])
                    yield

                def b_main(t):
                    J, e_ = UE[t]
                    par = t % 2
                    if e_ == 1:
                        init_state(SinF[J["j"]])
                    y = ybuf[t % 3]

                    def o_evac(po):
                        op("dve", lambda e: e.tensor_tensor(out=o_sb.t[:], in0=po.t[:], in1=obl[par].t[:], op=ALU.add),
                           reads=[po.b, obl[par].b], writes=[o_sb.b])
                    yield from gla_main(True, bankB, WB, xT[par], decs[par], g, o_evac, pre=raws[par])
                    for h in range(4):
                        hs = slice(h * 128, (h + 1) * 128)
                        op("act", lambda e: e.activation(out=sq.t[:, hs], in_=o_sb.t[:, hs], func=AF.Square, accum_out=ss.t[:, h:h + 1]),
                           reads=[o_sb.b], writes=[sq.b, ss.b])
                    op("dve", lambda e: e.tensor_scalar(out=ss.t[:, 0:4], in0=ss.t[:, 0:4], scalar1=1.0 / 128, scalar2=None, op0=ALU.mult),
                       reads=[ss.b], writes=[ss.b])
                    op("act", lambda e: e.activation(out=ss.t[:, 0:4], in_=ss.t[:, 0:4], func=AF.Ln, bias=NORM_EPS), reads=[ss.b], writes=[ss.b])
                    op("act", lambda e: e.activation(out=ss.t[:, 0:4], in_=ss.t[:, 0:4], func=AF.Exp, scale=-0.5), reads=[ss.b], writes=[ss.b])
                    yield
                    pz = proj_tm(bankB, WB, xT[par], ZA0, 512)
                    silu_from_psum(pz, ez, gz)
                    op("pool", lambda e: e.tensor_tensor(out=gz.t[:], in0=gz.t[:], in1=gn4.t[:], op=ALU.mult), reads=[gz.b, gn4.b], writes=[gz.b])
                    yield
                    for h in range(4):
                        hs = slice(h * 128, (h + 1) * 128)
                        op("dve", lambda e: e.scalar_tensor_tensor(out=y.t[:, hs], in0=o_sb.t[:, hs], scalar=ss.t[:, h:h + 1], in1=gz.t[:, hs],
                                                                   op0=ALU.mult, op1=ALU.mult), reads=[o_sb.b, ss.b, gz.b], writes=[y.b])
                    yield
                    pz2 = proj_tm(bankB, WB, xT[par], ZB0, 512)
                    silu_from_psum(pz2, ez, szb[par])
                    yield
                    pqb = proj_fm(bankB, WB, xT[par], QB0, 4)
                    op("act", lambda e: e.activation(out=qTB[par].t[:], in_=pqb.t[:], func=AF.Copy, scale=0.125), reads=[pqb.b], writes=[qTB[par].b])
                    yield

                def b_back(t):
                    J, e_ = UE[t]
                    n, jj = J["n"], J["j"]
                    par = t % 2
                    y = ybuf[t % 3]
                    blks = [(0, t - 1, 2 * jj if e_ == 2 else None), (1, t, None), (2, t + 1, 2 * jj + 1 if e_ == n + 1 else None)]
                    b0, b1 = 0, 3

                    def scores(h):
                        kv, c = h // 4, h % 4
                        rs = slice(kv * 64, (kv + 1) * 64)
                        pss = banks[5 + h % 2]
                        for (bi, tt, _) in blks:
                            op("pe", lambda e: e.matmul(pss.t[:, bi * 128:(bi + 1) * 128], lhsT=kTB[tt % 4].t[rs, :],
                                                        rhs=qTB[par].t[rs, c * 128:(c + 1) * 128], start=True, stop=True),
                               reads=[kTB[tt % 4].b, qTB[par].b], writes=[pss.b])
                        s_, p_ = sc[h % 2], pTs[h % 2]
                        op("dve", lambda e: e.tensor_tensor(out=s_.t[:, b0 * 128:b1 * 128], in0=pss.t[:, b0 * 128:b1 * 128],
                                                            in1=biasT.t[:, h, b0:b1, :].rearrange("p b q -> p (b q)"), op=ALU.add),
                           reads=[pss.b, biasT.b], writes=[s_.b])
                        for (bi, tt, fc) in blks:
                            if fc is not None:
                                op("dve", lambda e: e.tensor_scalar(out=s_.t[:, bi * 128:(bi + 1) * 128], in0=s_.t[:, bi * 128:(bi + 1) * 128],
                                                                    scalar1=negm.t[:, fc:fc + 1], scalar2=None, op0=ALU.add),
                                   reads=[s_.b, negm.b], writes=[s_.b])
                        op("act", lambda e: e.activation(out=p_.t[:, b0 * 128:b1 * 128], in_=s_.t[:, b0 * 128:b1 * 128], func=AF.Exp),
                           reads=[s_.b], writes=[p_.b])

                    for half in range(2):
                        pv = banks[4]
                        for hh in range(4):
                            h = half * 4 + hh
                            kv = h // 4
                            if h == 0:
                                scores(0)
                            if h + 1 < 8:
                                scores(h + 1)
                            yield
                            p_ = pTs[h % 2]
                            hc = hh * 65
                            for n_, (bi, tt, _) in enumerate(blks):
                                op("pe", lambda e: e.matmul(pv.t[:, hc:hc + 65], lhsT=p_.t[:, bi * 128:(bi + 1) * 128], rhs=vB[tt % 4].t[:, kv, :],
                                                            start=(n_ == 0), stop=(n_ == len(blks) - 1)),
                                   reads=[p_.b, vB[tt % 4].b], writes=[pv.b])
                        pvv = pv.t[:, 0:260].rearrange("p (h c) -> p h c", c=65)
                        op("dve", lambda e: e.tensor_tensor(out=den.t[:, half * 4:half * 4 + 4], in0=pvv[:, :, 64],
                                                            in1=esink.t[:, half * 4:half * 4 + 4], op=ALU.add),
                           reads=[pv.b, esink.b], writes=[den.b])
                        op("dve", lambda e: e.reciprocal(out=den.t[:, half * 4:half * 4 + 4], in_=den.t[:, half * 4:half * 4 + 4]),
                           reads=[den.b], writes=[den.b])
                        for hh in range(4):
                            h = half * 4 + hh
                            op("dve", lambda e: e.scalar_tensor_tensor(out=y.t[:, 512 + h * 64:512 + (h + 1) * 64], in0=pvv[:, hh, 0:64],
                                                                       scalar=den.t[:, h:h + 1], in1=szb[par].t[:, h * 64:(h + 1) * 64],
                                                                       op0=ALU.mult, op1=ALU.mult),
                               reads=[pv.b, den.b, szb[par].b], writes=[y.b])
                        yield

                def b_out(t):
                    J, e_ = UE[t]
                    x1_d = J["x1_d"]
                    par = t % 2
                    y = ybuf[t % 3]
                    for c in range(8):
                        op("pe", lambda e: e.transpose(out=pT.t[:, c, :], in_=y.t[:, c * 128:(c + 1) * 128], identity=ident.t[:]),
                           reads=[y.b, ident.b], writes=[pT.b])
                    op("act", lambda e: e.copy(out=yT.t[:], in_=pT.t[:]), reads=[pT.b], writes=[yT.b])
                    yield
                    h1 = h1s[par]
                    for nb in range(2):
                        pb = bankC()
                        for kc in range(8):
                            op("pe", lambda e: e.matmul(pb.t[:, :], lhsT=yT.t[:, kc, :], rhs=WO.t[:, kc, nb * 512:(nb + 1) * 512],
                                                        start=(kc == 0), stop=(kc == 7)), reads=[yT.b, WO.b], writes=[pb.b])
                        op("dve", lambda e: e.scalar_tensor_tensor(out=h1.t[:, nb * 512:(nb + 1) * 512], in0=xs[t % 5].t[:, nb * 512:(nb + 1) * 512],
                                                                   scalar=ALPHA, in1=pb.t[:, :], op0=ALU.mult, op1=ALU.add),
                           reads=[xs[t % 5].b, pb.b], writes=[h1.b])
                        yield
                    layer_norm(junk, stat, lng, lnb, h1)
                    dma("sp", x1_d[e_ * 128:(e_ + 1) * 128, :], h1.t[:], dst[par], reads=[h1.b])
                    yield

                run_pipe2([(b_load, -2, lambda u: True), (b_loadob, -1, main_e), (b_front, -1, lambda u: True),
                           (b_main, 0, main_e), (b_back, 1, main_e), (b_out, 2, main_e)], len(UE))
            fw.barrier()

        stW.close()
        with ExitStack() as st:
            W1 = sbt(st, "W1", [128, 8, 4 * D], BF16)
            WO = sbt(st, "WO1", [128, 8, D], BF16)
            with ExitStack() as st2:
                stg = [sbt(st2, "stgC%d" % i, [128, 8, 512], F32) for i in range(2)]
                load_weight(stg, W1, w1_d, 0, 4 * D, 0)
                load_weight(stg, WO, wo1_d, 0, D, 0)
                fw.barrier()
            lng = sbt(st, "lngC", [128, D], F32)
            lnb = sbt(st, "lnbC", [128, D], F32)
            cw = sbt(st, "cw", [128, 8, 3], F32)
            dma_c(lng.t[:], lng_d[:, D:2 * D], [lng.b])
            dma_c(lnb.t[:], lnb_d[:, D:2 * D], [lnb.b])
            dma_c(cw.t[:].rearrange("p c k -> p (c k)"), cw_d[:, :], [cw.b])
            xs = [sbt(st, "xsC%d" % i, [128, D], F32) for i in range(6)]
            xb = [sbt(st, "xbC%d" % i, [128, D], BF16) for i in range(2)]
            xT = [sbt(st, "xTC%d" % i, [128, 8, 128], BF16) for i in range(2)]
            TT = [sbt(st, "TT%d" % i, [128, 8, 130], F32) for i in range(3)]
            ub = [sbt(st, "ub%d" % i, [128, D], F32) for i in range(3)]
            hsb = sbt(st, "hsb", [128, 512], F32)
            ez = sbt(st, "ezC", [128, 512], F32)
            cv = sbt(st, "cvC", [128, D], F32)
            gTs = [sbt(st, "gTC%d" % i, [128, D], BF16) for i in range(2)]
            h1s = [sbt(st, "h1C%d" % i, [128, D], F32) for i in range(2)]
            junk = sbt(st, "junkC", [128, D], BF16)
            stat = sbt(st, "statC", [128, 4], F32)
            for T_ in TT:
                op("pool", lambda e: e.memset(T_.t[:], 0.0), writes=[T_.b])
            bankB, bankC = Rot(banks[0:5]), Rot(banks[5:7])

            if True:
                def c_load(t):
                    J, e_ = UE[t]
                    x1_d = J["x1_d"]
                    dma("sp", xs[t % 6].t[:], x1_d[e_ * 128:(e_ + 1) * 128, :], dx[t % 6], writes=[xs[t % 6].b])
                    op("pool", lambda e: e.tensor_copy(out=xb[t % 2].t[:], in_=xs[t % 6].t[:]), reads=[xs[t % 6].b], writes=[xb[t % 2].b])
                    yield

                def c_front(t):
                    xT_ = xT[t % 2]
                    xb_ = xb[t % 2]
                    for c in range(8):
                        op("pe", lambda e: e.transpose(out=pT.t[:, c, :], in_=xb_.t[:, c * 128:(c + 1) * 128], identity=ident.t[:]),
                           reads=[xb_.b, ident.b], writes=[pT.b])
                    op("act", lambda e: e.copy(out=xT_.t[:], in_=pT.t[:]), reads=[pT.b], writes=[xT_.b])
                    yield

                def c_main(t):
                    J, e_ = UE[t]
                    n, jj = J["n"], J["j"]
                    xT_ = xT[t % 2]
                    T_ = TT[t % 3]
                    u_ = ub[t % 3]
                    op("pool", lambda e: e.memset(T_.t[:, :, 0:1], 0.0), writes=[T_.b])
                    op("pool", lambda e: e.memset(T_.t[:, :, 129:130], 0.0), writes=[T_.b])
                    for half in range(2):
                        fs = slice(half * 512, (half + 1) * 512)
                        ph = proj_fm(bankB, W1, xT_, 2 * D + half * 512, 4)
                        op("act", lambda e: e.copy(out=hsb.t[:], in_=ph.t[:]), reads=[ph.b], writes=[hsb.b])
                        yield
                        pc_ = proj_fm(bankB, W1, xT_, D + half * 512, 4)
                        op("dve", lambda e: e.tensor_tensor(out=T_.t[:, half * 4:half * 4 + 4, 1:129],
                                                            in0=pc_.t[:].rearrange("p (c q) -> p c q", c=4),
                                                            in1=hsb.t[:].rearrange("p (c q) -> p c q", c=4), op=ALU.mult),
                           reads=[pc_.b, hsb.b], writes=[T_.b])
                        yield
                        pz = proj_fm(bankB, W1, xT_, 3 * D + half * 512, 4)
                        op("act", lambda e: e.activation(out=ez.t[:], in_=pz.t[:], func=AF.Exp, scale=-1.0), reads=[pz.b], writes=[ez.b])
                        op("act", lambda e: e.activation(out=ez.t[:], in_=ez.t[:], func=AF.Ln, bias=1.0), reads=[ez.b], writes=[ez.b])
                        op("act", lambda e: e.activation(out=ez.t[:], in_=ez.t[:], func=AF.Exp, scale=-1.0), reads=[ez.b], writes=[ez.b])
                        op("dve", lambda e: e.tensor_tensor(out=ez.t[:], in0=pz.t[:], in1=ez.t[:], op=ALU.mult), reads=[pz.b, ez.b], writes=[ez.b])
                        yield
                        pbg = proj_fm(bankB, W1, xT_, half * 512, 4)
                        op("dve", lambda e: e.tensor_tensor(out=u_.t[:, fs], in0=pbg.t[:], in1=ez.t[:], op=ALU.mult),
                           reads=[pbg.b, ez.b], writes=[u_.b])
                        yield
                    if e_ > 1:
                        Tp = TT[(t - 1) % 3]
                        if e_ == 2:
                            fc = 2 * jj
                            op("dve", lambda e: e.tensor_scalar(out=T_.t[:, :, 0:1], in0=Tp.t[:, :, 128:129], scalar1=rfl.t[:, fc:fc + 1], scalar2=None,
                                                                op0=ALU.mult), reads=[Tp.b, rfl.b], writes=[T_.b])
                        elif e_ == n + 2:
                            fc = 2 * jj + 1
                            op("dve", lambda e: e.tensor_scalar(out=Tp.t[:, :, 129:130], in0=T_.t[:, :, 1:2], scalar1=rfl.t[:, fc:fc + 1], scalar2=None,
                                                                op0=ALU.mult), reads=[T_.b, rfl.b], writes=[Tp.b])
                        else:
                            op("pool", lambda e: e.tensor_copy(out=T_.t[:, :, 0:1], in_=Tp.t[:, :, 128:129]), reads=[Tp.b], writes=[T_.b])
                            op("pool", lambda e: e.tensor_copy(out=Tp.t[:, :, 129:130], in_=T_.t[:, :, 1:2]), reads=[T_.b], writes=[Tp.b])
                    yield

                def c_back(t):
                    T_ = TT[t % 3]
                    u_ = ub[t % 3]
                    for c in range(8):
                        cs = slice(c * 128, (c + 1) * 128)
                        op("dve", lambda e: e.tensor_scalar(out=cv.t[:, cs], in0=T_.t[:, c, 0:128], scalar1=cw.t[:, c, 0:1], scalar2=None, op0=ALU.mult),
                           reads=[T_.b, cw.b], writes=[cv.b])
                        op("dve", lambda e: e.scalar_tensor_tensor(out=cv.t[:, cs], in0=T_.t[:, c, 1:129], scalar=cw.t[:, c, 1:2], in1=cv.t[:, cs],
                                                                   op0=ALU.mult, op1=ALU.add), reads=[T_.b, cw.b, cv.b], writes=[cv.b])
                        op("dve", lambda e: e.scalar_tensor_tensor(out=cv.t[:, cs], in0=T_.t[:, c, 2:130], scalar=cw.t[:, c, 2:3], in1=cv.t[:, cs],
                                                                   op0=ALU.mult, op1=ALU.add), reads=[T_.b, cw.b, cv.b], writes=[cv.b])
                        if c % 2 == 1:
                            yield
                    gT = gTs[t % 2]
                    op("dve", lambda e: e.tensor_tensor(out=gT.t[:], in0=cv.t[:], in1=u_.t[:], op=ALU.mult), reads=[cv.b, u_.b], writes=[gT.b])
                    yield

                def c_out(t):
                    J, e_ = UE[t]
                    y_d = J["y_d"]
                    gT = gTs[t % 2]
                    h1 = h1s[t % 2]
                    for nb in range(2):
                        pb = bankC()
                        for kc in range(8):
                            op("pe", lambda e: e.matmul(pb.t[:, :], lhsT=gT.t[:, kc * 128:(kc + 1) * 128], rhs=WO.t[:, kc, nb * 512:(nb + 1) * 512],
                                                        start=(kc == 0), stop=(kc == 7)), reads=[gT.b, WO.b], writes=[pb.b])
                        op("dve", lambda e: e.scalar_tensor_tensor(out=h1.t[:, nb * 512:(nb + 1) * 512], in0=xs[t % 6].t[:, nb * 512:(nb + 1) * 512],
                                                                   scalar=ALPHA, in1=pb.t[:, :], op0=ALU.mult, op1=ALU.add),
                           reads=[xs[t % 6].b, pb.b], writes=[h1.b])
                        yield
                    layer_norm(junk, stat, lng, lnb, h1)
                    dma("sp", y_d[(e_ - 2) * 128:(e_ - 1) * 128, :], h1.t[:], dst[t % 2], reads=[h1.b])
                    yield

                run_pipe2([(c_load, -2, main_e), (c_front, -1, main_e), (c_main, 0, main_e),
                           (c_back, 2, own_e), (c_out, 3, own_e)], len(UE))
            fw.barrier()
        for d_ in dst:
            nc.sync.wait_ge(d_.h, d_.cnt)
    return nc


def _rel_buckets():
    BLOCK, REL_BUCKETS, REL_MAX_DIST = 128, 32, 128
    i = np.arange(BLOCK)[:, None]
    j = np.arange(3 * BLOCK)[None, :]
    rel = j - BLOCK - i
    half = REL_BUCKETS // 2
    max_exact = half // 2
    n = np.abs(rel)
    large = max_exact + (np.log(np.maximum(n, 1) / max_exact) / np.log(REL_MAX_DIST / max_exact)
                         * (half - max_exact)).astype(np.int32)
    large = np.minimum(large, half - 1)
    bucket = (rel > 0).astype(np.int32) * half + np.where(n < max_exact, n, large)
    return bucket.astype(np.int32)


def pack_w0(w):
    qa, ka, va, za = w[:, 0:512], w[:, 512:1024], w[:, 1024:1536], w[:, 1536:2048]
    gd = w[:, 2048:2080]
    o = 2080
    qb, kb, vb, zb = w[:, o:o + 512], w[:, o + 512:o + 640], w[:, o + 640:o + 768], w[:, o + 768:o + 1280]
    out = np.zeros((D, NC0), np.float32)
    out[:, QA0:QA0 + 512] = qa
    out[:, KA0:KA0 + 512] = ka
    for c in range(4):
        out[:, QB0 + c * 128:QB0 + c * 128 + 64] = qb[:, c * 64:(c + 1) * 64]
        out[:, QB0 + c * 128 + 64:QB0 + (c + 1) * 128] = qb[:, (c + 4) * 64:(c + 5) * 64]
    out[:, KB0:KB0 + 128] = kb
    out[:, GD0:GD0 + 16] = gd[:, 0:16]
    out[:, GD0 + 32:GD0 + 48] = gd[:, 16:32]
    out[:, KT0:KT0 + 512] = ka
    out[:, VT0:VT0 + 512] = va
    out[:, ZA0:ZA0 + 512] = za
    out[:, ZB0:ZB0 + 512] = zb
    out[:, VB0:VB0 + 128] = vb
    return out


def make_shared(inp):
    f = lambda a: np.ascontiguousarray(np.asarray(a, dtype=np.float32))
    sh = {}
    sh["w0"] = pack_w0(f(inp["w_in_even"])[0])
    sh["wo0"] = f(inp["w_out_even"])[0]
    sh["w1"] = f(inp["w_in_odd"])[0]
    sh["wo1"] = f(inp["w_out_odd"])[0]
    wup = np.zeros((64, 512), np.float32)
    wup[0:16] = f(inp["gla_w_up_fwd"])[0]
    wup[16] = f(inp["gla_b_fwd"])[0]
    wup[32:48] = f(inp["gla_w_up_bwd"])[0]
    wup[48] = f(inp["gla_b_bwd"])[0]
    sh["wup"] = wup
    bucket = _rel_buckets()
    rb = f(inp["rel_bias"])
    bfull = rb[bucket]
    bT = bfull.reshape(128, 3, 128, 8).transpose(2, 3, 1, 0)
    sh["biasT"] = np.ascontiguousarray(bT).reshape(128, 8 * 3 * 128)
    sh["sink"] = np.ascontiguousarray(np.broadcast_to(f(inp["swa_sink"])[0][None, :], (128, 8)))
    sh["gn"] = np.ascontiguousarray(np.broadcast_to(np.tile(f(inp["gla_norm_g"])[0], 4)[None, :], (128, 512)))
    sh["lng"] = np.ascontiguousarray(np.broadcast_to(f(inp["ln_g"]).reshape(1, 2 * D), (128, 2 * D)))
    sh["lnb"] = np.ascontiguousarray(np.broadcast_to(f(inp["ln_b"]).reshape(1, 2 * D), (128, 2 * D)))
    cw = f(inp["conv_w"])[0]
    sh["convw"] = np.ascontiguousarray(cw.reshape(3, 8, 128).transpose(2, 1, 0)).reshape(128, 24)
    return sh


_NC_CACHE = {}


def _ext(xseq, a, n):
    L = xseq.shape[0]
    out = np.zeros(((n + 4) * 128, D), np.float32)
    lo, hi = (a - 2) * 128, (a + n + 2) * 128
    slo, shi = max(lo, 0), min(hi, L)
    out[slo - lo:shi - lo] = xseq[slo:shi]
    return out


def _oth(xseq, a, n, NO):
    Ls = xseq.shape[0] // 128
    pre = list(range(0, max(a - 1, 0)))
    suf = list(range(min(a + n + 1, Ls), Ls))
    out = np.zeros((NO * 128, D), np.float32)
    wts = np.zeros((NO, 2), np.float32)
    for k, t in enumerate(pre + suf):
        out[k * 128:(k + 1) * 128] = xseq[t * 128:(t + 1) * 128]
        wts[k, 0 if k < len(pre) else 1] = 1.0
    return out, wts


def run(xp, xs_, inp, n_cores=8):
    Bp, LP, _ = xp.shape
    Bs, LS, _ = xs_.shape
    assert Bp == 2 and Bs == 4 and LP % 512 == 0 and LS % 256 == 0
    nP, nS = LP // 128 // 4, LS // 128 // 2
    key = (nP, nS)
    if key not in _NC_CACHE:
        _NC_CACHE[key] = build(nP, nS)
    nc = _NC_CACHE[key]
    sh = make_shared(inp)
    in_maps = []
    for c in range(n_cores):
        m = dict(sh)
        aP, aS = (c % 4) * nP, (c % 2) * nS
        xP, xS = xp[c // 4], xs_[c // 2]
        m["xextP"] = _ext(xP, aP, nP)
        m["xextS"] = _ext(xS, aS, nS)
        m["xothP"], wP = _oth(xP, aP, nP, 3 * nP)
        m["xothS"], wS = _oth(xS, aS, nS, nS)
        wrow = np.concatenate([wP.reshape(-1), wS.reshape(-1)])[None, :]
        m["wts"] = np.ascontiguousarray(np.broadcast_to(wrow, (128, wrow.shape[1]))).astype(np.float32)
        fl = np.array([[float(aP > 0), float(aP + nP < 4 * nP), float(aS > 0), float(aS + nS < 2 * nS)]], np.float32)
        m["flags"] = np.ascontiguousarray(np.broadcast_to(fl, (128, 4)))
        in_maps.append(m)
    res = run_bass_kernel_spmd(nc, in_maps, core_ids=list(range(n_cores)))
    yp = np.zeros((Bp, LP, D), np.float32)
    ys = np.zeros((Bs, LS, D), np.float32)
    for c in range(n_cores):
        aP, aS = (c % 4) * nP, (c % 2) * nS
        yp[c // 4, aP * 128:(aP + nP) * 128] = res.results[c]["yP"]
        ys[c // 2, aS * 128:(aS + nS) * 128] = res.results[c]["yS"]
    return yp, ys


def kernel(x_prompt, x_sample, **w):
    xp = np.ascontiguousarray(np.asarray(x_prompt, dtype=np.float32))
    xs_ = np.ascontiguousarray(np.asarray(x_sample, dtype=np.float32))
    return run(xp, xs_, w)
```

```python
import numpy as np
from contextlib import ExitStack
import concourse.bass as bass
import concourse.mybir as mybir
from concourse.bass_utils import run_bass_kernel_spmd

F32 = mybir.dt.float32
BF16 = mybir.dt.bfloat16
AF = mybir.ActivationFunctionType
ALU = mybir.AluOpType

D = 1024
ALPHA = 4 ** 0.25
LN_EPS = 1e-5
NORM_EPS = 1e-6
NEG = -30000.0

QA0, KA0, QB0, KB0, GD0 = 0, 512, 1024, 1536, 1664
KT0, VT0, ZA0, ZB0, VB0 = 1728, 2240, 2752, 3264, 3776
NC0 = 3904


class Buf:
    def __init__(self, name):
        self.name = name
        self.w = None
        self.r = {}


class Sem:
    def __init__(self, nc, stack, name):
        self.h = stack.enter_context(nc.semaphore(name))
        self.cnt = 0
        self.name = name


class Eng:
    def __init__(self, name, e, sem):
        self.name, self.e, self.sem = name, e, sem
        self.waited = {}


class FW:
    def __init__(self, nc, stack):
        self.nc = nc
        self.stack = stack
        self.engs = {}
        for name, e in (("pe", nc.tensor), ("act", nc.scalar), ("dve", nc.vector), ("pool", nc.gpsimd), ("sp", nc.sync)):
            self.engs[name] = Eng(name, e, Sem(nc, stack, "s_" + name))
        self.dsems = []
        self.n_instr = 0

    def dsem(self, name):
        s = Sem(self.nc, self.stack, name)
        self.dsems.append(s)
        return s

    def _waits(self, eng, reads, writes):
        need = {}

        def add(p):
            if p is None:
                return
            s, c = p
            if need.get(s, 0) < c:
                need[s] = c
        for b in reads:
            add(b.w)
        for b in writes:
            add(b.w)
            for s, c in b.r.items():
                add((s, c))
        for s, c in need.items():
            if s is eng.sem and eng.name == "pe":
                continue
            if eng.waited.get(s, 0) >= c:
                continue
            eng.e.wait_ge(s.h, c)
            eng.waited[s] = c

    def _mark(self, tok, reads, writes):
        s, c = tok
        for b in reads:
            b.r[s] = c
        for b in writes:
            b.w = tok
            b.r = {}

    def op(self, engname, fn, reads=(), writes=()):
        eng = self.engs[engname]
        self._waits(eng, reads, writes)
        ins = fn(eng.e)
        eng.sem.cnt += 1
        ins.then_inc(eng.sem.h, 1)
        self.n_instr += 1
        tok = (eng.sem, eng.sem.cnt)
        self._mark(tok, reads, writes)
        return tok

    def dma(self, engname, out, in_, sem, reads=(), writes=()):
        eng = self.engs[engname]
        self._waits(eng, reads, writes)
        ins = eng.e.dma_start(out=out, in_=in_)
        sem.cnt += 16
        ins.then_inc(sem.h, 16)
        self.n_instr += 1
        tok = (sem, sem.cnt)
        self._mark(tok, reads, writes)
        return tok

    def barrier(self):
        sems = [e.sem for e in self.engs.values()] + self.dsems
        for eng in self.engs.values():
            for s in sems:
                if s is eng.sem or s.cnt == 0:
                    continue
                if eng.waited.get(s, 0) >= s.cnt:
                    continue
                eng.e.wait_ge(s.h, s.cnt)
                eng.waited[s] = s.cnt


class TB:
    def __init__(self, t, name):
        self.t = t
        self.b = Buf(name)


def sched(gens):
    gens = list(gens)
    while gens:
        for g_ in list(gens):
            try:
                next(g_)
            except StopIteration:
                gens.remove(g_)


def build(nP, nS):
    JOBS = [dict(name="P", n=nP, NO=3 * nP, j=0, woff=0), dict(name="S", n=nS, NO=nS, j=1, woff=2 * 3 * nP)]
    NWT = 2 * (3 * nP + nS)
    nc = bass.Bass("TRN2", target_bir_lowering=False)

    def din(name, shape):
        return nc.dram_tensor(name, shape, F32, kind="ExternalInput").ap()
    for J in JOBS:
        J["x_d"] = din("xext" + J["name"], [(J["n"] + 4) * 128, D])
        J["xo_d"] = din("xoth" + J["name"], [J["NO"] * 128, D])
        J["y_d"] = nc.dram_tensor("y" + J["name"], [J["n"] * 128, D], F32, kind="ExternalOutput").ap()
        J["ob_d"] = nc.dram_tensor("ob_scr" + J["name"], [(J["n"] + 4) * 128, 512], F32).ap()
        J["x1_d"] = nc.dram_tensor("x1_scr" + J["name"], [(J["n"] + 4) * 128, D], F32).ap()
        J["qkv_d"] = nc.dram_tensor("qkv_scr" + J["name"], [(J["n"] + 4) * 128, 4, 512], BF16).ap()
    w0_d = din("w0", [D, NC0])
    wo0_d = din("wo0", [D, D])
    w1_d = din("w1", [D, 4 * D])
    wo1_d = din("wo1", [D, D])
    wup_d = din("wup", [64, 512])
    bias_d = din("biasT", [128, 8 * 3 * 128])
    sink_d = din("sink", [128, 8])
    gn_d = din("gn", [128, 512])
    lng_d = din("lng", [128, 2 * D])
    lnb_d = din("lnb", [128, 2 * D])
    cw_d = din("convw", [128, 24])
    fl_d = din("flags", [128, 4])
    wt_d = din("wts", [128, NWT])

    with ExitStack() as gst:
        fw = FW(nc, gst)
        op, dma = fw.op, fw.dma
        uid = [0]

        def sbt(st, name, shape, dt):
            uid[0] += 1
            return TB(st.enter_context(nc.sbuf_tensor("sb%d_%s" % (uid[0], name), shape, dt)), name)

        def pst(st, name, shape, dt):
            return TB(st.enter_context(nc.psum_tensor("ps_" + name, shape, dt)), name)

        ident = sbt(gst, "ident", [128, 128], BF16)
        mU = sbt(gst, "mU", [128, 128], F32)
        mL = sbt(gst, "mL", [128, 128], F32)
        mUs = sbt(gst, "mUs", [128, 128], F32)
        mLs = sbt(gst, "mLs", [128, 128], F32)
        m01U = sbt(gst, "m01U", [128, 512], BF16)
        m01Ls = sbt(gst, "m01Ls", [128, 512], BF16)
        wup = sbt(gst, "wup", [64, 512], BF16)
        rfl = sbt(gst, "rfl", [128, 4], F32)
        negm = sbt(gst, "negm", [128, 4], F32)
        wts = sbt(gst, "wts", [128, NWT], F32)
        m16c = sbt(gst, "m16c", [128, 1], F32)
        SinF = [sbt(gst, "SinF%d" % i, [128, 512], F32) for i in range(2)]
        Pdec = sbt(gst, "Pdec", [128, 4], F32)
        S = sbt(gst, "S", [128, 512], F32)
        Sbf = sbt(gst, "Sbf", [128, 512], BF16)
        gda = sbt(gst, "gda", [64, 128], BF16)
        cn = [0]

        def dma_c(out, in_, writes):
            cn[0] += 1
            return dma("sp", out, in_, fw.dsem("dc%d" % cn[0]), writes=writes)
        dx = [fw.dsem("dx%d" % i) for i in range(6)]
        dst = [fw.dsem("dst%d" % i) for i in range(2)]
        dob = [fw.dsem("dob%d" % i) for i in range(2)]
        dw = [fw.dsem("dw%d" % i) for i in range(2)]
        dqs = [fw.dsem("dqs%d" % i) for i in range(2)]
        dql = [fw.dsem("dql%d" % i) for i in range(2)]

        def mask(tb, val, cm, base, nrep=1):
            pat = [[-cm, 128]] if nrep == 1 else [[0, nrep], [-cm, 128]]
            ap = tb.t[:] if nrep == 1 else tb.t[:].rearrange("p (a b) -> p a b", a=nrep)
            op("pool", lambda e: e.memset(tb.t[:], val), writes=[tb.b])
            op("pool", lambda e: e.affine_select(out=ap, in_=ap, pattern=pat, compare_op=ALU.is_ge, fill=0.0,
                                                 base=base, channel_multiplier=cm), reads=[tb.b], writes=[tb.b])
        mask(mU, -0.0625, -1, 0)
        mask(mL, -0.0625, 1, 0)
        mask(mUs, -0.0625, -1, -1)
        mask(mLs, -0.0625, 1, -1)
        mask(m01U, 1.0, -1, 0, 4)
        mask(m01Ls, 1.0, 1, -1, 4)
        with ExitStack() as st0:
            tmpf = sbt(st0, "tmpf", [128, 512], F32)
            op("pool", lambda e: e.memset(tmpf.t[:, 0:128], 1.0), writes=[tmpf.b])
            op("pool", lambda e: e.affine_select(out=tmpf.t[:, 0:128], in_=tmpf.t[:, 0:128], pattern=[[-1, 128]],
                                                 compare_op=ALU.is_equal, fill=0.0, base=0, channel_multiplier=1),
               reads=[tmpf.b], writes=[tmpf.b])
            op("dve", lambda e: e.tensor_copy(out=ident.t[:], in_=tmpf.t[:, 0:128]), reads=[tmpf.b], writes=[ident.b])
            dma_c(tmpf.t[0:64, :], wup_d[:, :], [tmpf.b])
            op("dve", lambda e: e.tensor_copy(out=wup.t[:], in_=tmpf.t[0:64, :]), reads=[tmpf.b], writes=[wup.b])
            dma_c(rfl.t[:], fl_d[:, :], [rfl.b])
            dma_c(wts.t[:], wt_d[:, :], [wts.b])
            op("pool", lambda e: e.memset(m16c.t[:], -0.0625), writes=[m16c.b])
            op("dve", lambda e: e.tensor_scalar(out=negm.t[:], in0=rfl.t[:], scalar1=-1.0, scalar2=-NEG, op0=ALU.add, op1=ALU.mult),
               reads=[rfl.b], writes=[negm.b])
            op("pool", lambda e: e.memset(gda.t[:], 1.0), writes=[gda.b])
            fw.barrier()

        pT = pst(gst, "pT", [128, 8, 128], BF16)
        banks = [pst(gst, "pb%d" % i, [128, 512], F32) for i in range(7)]

        class Rot:
            def __init__(self, lst):
                self.l, self.i = lst, 0

            def __call__(self):
                b = self.l[self.i % len(self.l)]
                self.i += 1
                return b

        wst_n = [0]

        def load_weight(stg, dst, src_d, col0, ncols, dcol0):
            src = src_d.rearrange("(c p) n -> p c n", p=128)
            c = 0
            while c < ncols:
                n = min(512, ncols - c)
                i = wst_n[0] % 2
                wst_n[0] += 1
                s = stg[i]
                dma("sp", s.t[:, :, 0:n], src[:, :, col0 + c:col0 + c + n], dw[i], writes=[s.b])
                eng = ("act", "dve")[wst_n[0] % 2]
                dd = dst.t[:, :, dcol0 + c:dcol0 + c + n]
                if eng == "act":
                    op("act", lambda e: e.copy(out=dd, in_=s.t[:, :, 0:n]), reads=[s.b], writes=[dst.b])
                else:
                    op(eng, lambda e: e.tensor_copy(out=dd, in_=s.t[:, :, 0:n]), reads=[s.b], writes=[dst.b])
                c += n

        def proj_tm(bankf, W, xT, col0, n):
            pb = bankf()
            for kc in range(8):
                op("pe", lambda e: e.matmul(pb.t[:, 0:n], lhsT=xT.t[:, kc, :], rhs=W.t[:, kc, col0:col0 + n],
                                            start=(kc == 0), stop=(kc == 7)), reads=[xT.b, W.b], writes=[pb.b])
            return pb

        def proj_fm(bankf, W, xT, col0, nch, m=128):
            pb = bankf()
            for ch in range(nch):
                for kc in range(8):
                    op("pe", lambda e: e.matmul(pb.t[0:m, ch * 128:(ch + 1) * 128],
                                                lhsT=W.t[:, kc, col0 + ch * m:col0 + (ch + 1) * m], rhs=xT.t[:, kc, :],
                                                start=(kc == 0), stop=(kc == 7)), reads=[xT.b, W.b], writes=[pb.b])
            return pb

        def front(fwd, bankf, W, xsl, xb, xT, lg, dec):
            r0 = 0 if fwd else 32
            mFM, mTM = (mU, mLs) if fwd else (mL, mUs)
            for c in range(8):
                op("pe", lambda e: e.transpose(out=pT.t[:, c, :], in_=xb.t[:, c * 128:(c + 1) * 128], identity=ident.t[:]),
                   reads=[xb.b, ident.b], writes=[pT.b])
            op("act", lambda e: e.copy(out=xT.t[:], in_=pT.t[:]), reads=[pT.b], writes=[xT.b])
            yield
            pg = proj_fm(bankf, W, xT, GD0, 1, m=64)
            yield
            op("act", lambda e: e.copy(out=gda.t[r0:r0 + 16, :], in_=pg.t[r0:r0 + 16, 0:128]), reads=[pg.b], writes=[gda.b])
            yield
            pl = bankf()
            op("pe", lambda e: e.matmul(pl.t[:, :], lhsT=gda.t[r0:r0 + 32, :], rhs=wup.t[r0:r0 + 32, :], start=True, stop=True),
               reads=[gda.b, wup.b], writes=[pl.b])
            yield
            op("act", lambda e: e.activation(out=lg.t[:], in_=pl.t[:], func=AF.Exp, scale=-1.0), reads=[pl.b], writes=[lg.b])
            op("act", lambda e: e.activation(out=lg.t[:], in_=lg.t[:], func=AF.Ln, bias=1.0), reads=[lg.b], writes=[lg.b])
            yield
            pc = bankf()
            for h in range(4):
                hs = slice(h * 128, (h + 1) * 128)
                op("pe", lambda e: e.matmul(pc.t[:, hs], lhsT=lg.t[:, hs], rhs=mFM.t[:], start=True, stop=True),
                   reads=[lg.b, mFM.b], writes=[pc.b])
            yield
            pp = bankf()
            op("pe", lambda e: e.matmul(pp.t[:, :], lhsT=mTM.t[:], rhs=lg.t[:], start=True, stop=True),
               reads=[lg.b, mTM.b], writes=[pp.b])
            yield
            ec, enc, epre = dec
            op("act", lambda e: e.activation(out=ec.t[:], in_=pc.t[:], func=AF.Exp), reads=[pc.b], writes=[ec.b])
            op("act", lambda e: e.activation(out=enc.t[:], in_=pc.t[:], func=AF.Exp, scale=-1.0), reads=[pc.b], writes=[enc.b])
            yield
            op("act", lambda e: e.activation(out=epre.t[:], in_=pp.t[:], func=AF.Exp), reads=[pp.b], writes=[epre.b])
            yield

        def gla_main(fwd, bankf, W, xT, dec, g, o_evac, raw=None, pre=None):
            ec, enc, epre = dec
            m01, col = (m01U, 127) if fwd else (m01Ls, 0)
            qt, kt, kh, sT = g["qt"], g["kt"], g["kh"], g["sT"]
            if pre is None:
                rq, rk, rkt, v = raw
                for (c0, fm, dst_) in ((QA0, True, rq), (KA0, True, rk), (VT0, False, v)):
                    pb = proj_fm(bankf, W, xT, c0, 4) if fm else proj_tm(bankf, W, xT, c0, 512)
                    op("act", lambda e: e.copy(out=dst_.t[:], in_=pb.t[:]), reads=[pb.b], writes=[dst_.b])
                    yield
                for h in range(4):
                    op("pe", lambda e: e.transpose(out=pT.t[:, h, :], in_=rk.t[:, h * 128:(h + 1) * 128], identity=ident.t[:]),
                       reads=[rk.b, ident.b], writes=[pT.b])
                op("act", lambda e: e.copy(out=rkt.t[:].rearrange("p (h d) -> p h d", h=4), in_=pT.t[:, 0:4, :]), reads=[pT.b], writes=[rkt.b])
                yield
            else:
                rq, rk, rkt, v = pre
            op("dve", lambda e: e.scalar_tensor_tensor(out=qt.t[:], in0=rq.t[:], scalar=128 ** -0.5, in1=ec.t[:], op0=ALU.mult, op1=ALU.mult),
               reads=[rq.b, ec.b], writes=[qt.b])
            op("dve", lambda e: e.tensor_tensor(out=kt.t[:], in0=rk.t[:], in1=enc.t[:], op=ALU.mult), reads=[rk.b, enc.b], writes=[kt.b])
            yield
            op("dve", lambda e: e.tensor_tensor(out=kh.t[:], in0=rkt.t[:], in1=epre.t[:], op=ALU.mult), reads=[rkt.b, epre.b], writes=[kh.b])
            yield
            psc = bankf()
            for h in range(4):
                hs = slice(h * 128, (h + 1) * 128)
                op("pe", lambda e: e.matmul(psc.t[:, hs], lhsT=kt.t[:, hs], rhs=qt.t[:, hs], start=True, stop=True),
                   reads=[kt.b, qt.b], writes=[psc.b])
            op("dve", lambda e: e.tensor_tensor(out=sT.t[:], in0=psc.t[:], in1=m01.t[:], op=ALU.mult), reads=[psc.b, m01.b], writes=[sT.b])
            yield
            po = bankf()
            for h in range(4):
                hs = slice(h * 128, (h + 1) * 128)
                op("pe", lambda e: e.matmul(po.t[:, hs], lhsT=sT.t[:, hs], rhs=v.t[:, hs], start=True, stop=False),
                   reads=[sT.b, v.b], writes=[po.b])
                op("pe", lambda e: e.matmul(po.t[:, hs], lhsT=qt.t[:, hs], rhs=Sbf.t[:, hs], start=False, stop=True),
                   reads=[qt.b, Sbf.b], writes=[po.b])
            o_evac(po)
            yield
            pu = bankf()
            for h in range(4):
                hs = slice(h * 128, (h + 1) * 128)
                op("pe", lambda e: e.matmul(pu.t[:, hs], lhsT=kh.t[:, hs], rhs=v.t[:, hs], start=True, stop=True),
                   reads=[kh.b, v.b], writes=[pu.b])
            for h in range(4):
                hs = slice(h * 128, (h + 1) * 128)
                cc = h * 128 + col
                op("dve", lambda e: e.scalar_tensor_tensor(out=S.t[:, hs], in0=S.t[:, hs], scalar=ec.t[:, cc:cc + 1], in1=pu.t[:, hs],
                                                           op0=ALU.mult, op1=ALU.add), reads=[S.b, ec.b, pu.b], writes=[S.b])
            op("pool", lambda e: e.tensor_copy(out=Sbf.t[:], in_=S.t[:]), reads=[S.b], writes=[Sbf.b])
            yield

        def reset_state(zero):
            if zero:
                op("pool", lambda e: e.memset(S.t[:], 0.0), writes=[S.b])
            else:
                op("dve", lambda e: e.tensor_scalar(out=S.t[:], in0=S.t[:], scalar1=rfl.t[:, 0:1], scalar2=None, op0=ALU.mult),
                   reads=[S.b, rfl.b], writes=[S.b])
            op("pool", lambda e: e.tensor_copy(out=Sbf.t[:], in_=S.t[:]), reads=[S.b], writes=[Sbf.b])

        def silu_from_psum(pz, ez, out_tb):
            op("act", lambda e: e.activation(out=ez.t[:], in_=pz.t[:], func=AF.Exp, scale=-1.0), reads=[pz.b], writes=[ez.b])
            op("act", lambda e: e.activation(out=ez.t[:], in_=ez.t[:], func=AF.Ln, bias=1.0), reads=[ez.b], writes=[ez.b])
            op("act", lambda e: e.activation(out=ez.t[:], in_=ez.t[:], func=AF.Exp, scale=-1.0), reads=[ez.b], writes=[ez.b])
            op("dve", lambda e: e.tensor_tensor(out=out_tb.t[:], in0=pz.t[:], in1=ez.t[:], op=ALU.mult), reads=[pz.b, ez.b], writes=[out_tb.b])

        def layer_norm(junk, stat, lng, lnb, h1):
            op("act", lambda e: e.activation(out=junk.t[:], in_=h1.t[:], func=AF.Copy, accum_out=stat.t[:, 0:1]),
               reads=[h1.b], writes=[junk.b, stat.b])
            op("act", lambda e: e.activation(out=junk.t[:], in_=h1.t[:], func=AF.Square, accum_out=stat.t[:, 1:2]),
               reads=[h1.b], writes=[junk.b, stat.b])
            op("dve", lambda e: e.tensor_scalar(out=stat.t[:, 0:2], in0=stat.t[:, 0:2], scalar1=1.0 / D, scalar2=None, op0=ALU.mult),
               reads=[stat.b], writes=[stat.b])
            op("dve", lambda e: e.tensor_tensor(out=stat.t[:, 2:3], in0=stat.t[:, 0:1], in1=stat.t[:, 0:1], op=ALU.mult),
               reads=[stat.b], writes=[stat.b])
            op("dve", lambda e: e.tensor_tensor(out=stat.t[:, 2:3], in0=stat.t[:, 1:2], in1=stat.t[:, 2:3], op=ALU.subtract),
               reads=[stat.b], writes=[stat.b])
            op("act", lambda e: e.activation(out=stat.t[:, 3:4], in_=stat.t[:, 2:3], func=AF.Ln, bias=LN_EPS), reads=[stat.b], writes=[stat.b])
            op("act", lambda e: e.activation(out=stat.t[:, 3:4], in_=stat.t[:, 3:4], func=AF.Exp, scale=-0.5), reads=[stat.b], writes=[stat.b])
            op("dve", lambda e: e.tensor_scalar(out=h1.t[:], in0=h1.t[:], scalar1=stat.t[:, 0:1], scalar2=stat.t[:, 3:4],
                                                op0=ALU.subtract, op1=ALU.mult), reads=[h1.b, stat.b], writes=[h1.b])
            op("dve", lambda e: e.tensor_tensor(out=h1.t[:], in0=h1.t[:], in1=lng.t[:], op=ALU.mult), reads=[h1.b, lng.b], writes=[h1.b])
            op("pool", lambda e: e.tensor_tensor(out=h1.t[:], in0=h1.t[:], in1=lnb.t[:], op=ALU.add), reads=[h1.b, lnb.b], writes=[h1.b])

        def init_state(src):
            op("dve", lambda e: e.tensor_copy(out=S.t[:], in_=src.t[:]), reads=[src.b], writes=[S.b])
            op("pool", lambda e: e.tensor_copy(out=Sbf.t[:], in_=S.t[:]), reads=[S.b], writes=[Sbf.b])

        def gla_bufs(st):
            g = {}
            for n in ("qt", "kt", "kh", "sT"):
                g[n] = sbt(st, "g_" + n, [128, 512], BF16)
            return g

        def run_pipe(stages):
            smin = min(lo + lag for (_, lag, lo, hi) in stages)
            smax = max(hi - 1 + lag for (_, lag, lo, hi) in stages)
            for s_ in range(smin, smax + 1):
                sched([fn(s_ - lag) for (fn, lag, lo, hi) in stages if lo <= s_ - lag < hi])

        def run_pipe2(stages, NU):
            smin = min(lag for (_, lag, _) in stages)
            smax = NU - 1 + max(lag for (_, lag, _) in stages)
            for s_ in range(smin, smax + 1):
                sched([fn(s_ - lag) for (fn, lag, pred) in stages if 0 <= s_ - lag < NU and pred(s_ - lag)])

        UE = [(J, e) for J in JOBS for e in range(J["n"] + 4)]
        UA = [(J, e) for J in JOBS for e in range(J["n"] + 2, 0, -1)]
        UP = [(J, t) for J in JOBS for t in range(J["NO"])]

        def main_e(u):
            J, e = UE[u]
            return 1 <= e < J["n"] + 3

        def own_e(u):
            J, e = UE[u]
            return 2 <= e < J["n"] + 2

        stW = ExitStack()
        stW.__enter__()
        W0 = sbt(stW, "W0", [128, 8, NC0], BF16)
        WO0 = sbt(stW, "WO0", [128, 8, D], BF16)
        with ExitStack() as st:
            stg = [sbt(st, "stgA%d" % i, [128, 8, 512], F32) for i in range(2)]
            W0e = TB(W0.t, "W0early")
            W0l = TB(W0.t, "W0late")
            load_weight(stg, W0e, w0_d, GD0, 64 + 1024, GD0)
            late_blocks = []
            for (dst_, src_, c0, n_) in ((W0l, w0_d, 0, GD0), (W0l, w0_d, ZA0, NC0 - ZA0), (WO0, wo0_d, 0, D)):
                c_ = 0
                while c_ < n_:
                    late_blocks.append((dst_, src_, c0 + c_, min(512, n_ - c_)))
                    c_ += 512
            WA = W0e
            SinB = [sbt(st, "SinB%d" % i, [128, 512], F32) for i in range(2)]
            xs = [sbt(st, "xsA%d" % i, [128, D], F32) for i in range(3)]
            xb = [sbt(st, "xbA%d" % i, [128, D], BF16) for i in range(2)]
            xT = [sbt(st, "xTA%d" % i, [128, 8, 128], BF16) for i in range(2)]

            with ExitStack() as st2:
                lgf = sbt(st2, "lgf", [128, 512], F32)
                lgb = sbt(st2, "lgb", [128, 512], F32)
                ebs = [sbt(st2, "ebs%d" % i, [128, 512], F32) for i in range(2)]
                epr = [sbt(st2, "epr%d" % i, [128, 512], F32) for i in range(2)]
                Dfb = [sbt(st2, "Dfb%d" % i, [128, 8], F32) for i in range(2)]
                khf = sbt(st2, "khf", [128, 512], BF16)
                khb = sbt(st2, "khb", [128, 512], BF16)
                vpp = sbt(st2, "vpp", [128, 512], BF16)
                bankA, bankB = Rot(banks[0:4]), Rot(banks[4:7])
                if True:
                    def p_load(u):
                        J, t = UP[u]
                        xo_d = J["xo_d"]
                        dma("sp", xs[u % 3].t[:], xo_d[t * 128:(t + 1) * 128, :], dx[u % 3], writes=[xs[u % 3].b])
                        op("pool", lambda e: e.tensor_copy(out=xb[u % 2].t[:], in_=xs[u % 3].t[:]), reads=[xs[u % 3].b], writes=[xb[u % 2].b])
                        yield

                    def p_front(u):
                        J, t = UP[u]
                        woff = J["woff"]
                        xb_, xT_ = xb[u % 2], xT[u % 2]
                        wf = wts.t[:, woff + 2 * t:woff + 2 * t + 1]
                        wb = wts.t[:, woff + 2 * t + 1:woff + 2 * t + 2]
                        for c in range(8):
                            op("pe", lambda e: e.transpose(out=pT.t[:, c, :], in_=xb_.t[:, c * 128:(c + 1) * 128], identity=ident.t[:]),
                               reads=[xb_.b, ident.b], writes=[pT.b])
                        op("act", lambda e: e.copy(out=xT_.t[:], in_=pT.t[:]), reads=[pT.b], writes=[xT_.b])
                        yield
                        pg = proj_fm(bankA, WA, xT_, GD0, 1, m=64)
                        op("act", lambda e: e.copy(out=gda.t[0:16, :], in_=pg.t[0:16, 0:128]), reads=[pg.b], writes=[gda.b])
                        op("act", lambda e: e.copy(out=gda.t[32:48, :], in_=pg.t[32:48, 0:128]), reads=[pg.b], writes=[gda.b])
                        yield
                        plf = bankA()
                        op("pe", lambda e: e.matmul(plf.t[:, :], lhsT=gda.t[0:32, :], rhs=wup.t[0:32, :], start=True, stop=True),
                           reads=[gda.b, wup.b], writes=[plf.b])
                        plb = bankA()
                        op("pe", lambda e: e.matmul(plb.t[:, :], lhsT=gda.t[32:64, :], rhs=wup.t[32:64, :], start=True, stop=True),
                           reads=[gda.b, wup.b], writes=[plb.b])
                        for (pl, lg_) in ((plf, lgf), (plb, lgb)):
                            op("act", lambda e: e.activation(out=lg_.t[:], in_=pl.t[:], func=AF.Exp, scale=-1.0), reads=[pl.b], writes=[lg_.b])
                            op("act", lambda e: e.activation(out=lg_.t[:], in_=lg_.t[:], func=AF.Ln, bias=1.0), reads=[lg_.b], writes=[lg_.b])
                        yield
                        ppf = bankA()
                        op("pe", lambda e: e.matmul(ppf.t[:, :], lhsT=mLs.t[:], rhs=lgf.t[:], start=True, stop=True),
                           reads=[lgf.b, mLs.b], writes=[ppf.b])
                        op("act", lambda e: e.activation(out=ebs[u % 2].t[:], in_=ppf.t[:], func=AF.Exp), reads=[ppf.b], writes=[ebs[u % 2].b])
                        yield
                        ppb = bankA()
                        op("pe", lambda e: e.matmul(ppb.t[:, :], lhsT=mUs.t[:], rhs=lgb.t[:], start=True, stop=True),
                           reads=[lgb.b, mUs.b], writes=[ppb.b])
                        op("act", lambda e: e.activation(out=epr[u % 2].t[:], in_=ppb.t[:], func=AF.Exp), reads=[ppb.b], writes=[epr[u % 2].b])
                        yield
                        ptot = bankA()
                        for di, lg_ in enumerate((lgf, lgb)):
                            for h in range(4):
                                cc = di * 4 + h
                                op("pe", lambda e: e.matmul(ptot.t[:, cc:cc + 1], lhsT=lg_.t[:, h * 128:(h + 1) * 128], rhs=m16c.t[:, 0:1],
                                                            start=True, stop=True), reads=[lg_.b, m16c.b], writes=[ptot.b])
                        dd = Dfb[u % 2]
                        op("act", lambda e: e.activation(out=dd.t[:, 0:4], in_=ptot.t[:, 0:4], func=AF.Exp, scale=wf), reads=[ptot.b, wts.b], writes=[dd.b])
                        op("act", lambda e: e.activation(out=dd.t[:, 4:8], in_=ptot.t[:, 4:8], func=AF.Exp, scale=wb), reads=[ptot.b, wts.b], writes=[dd.b])
                        yield

                    def p_upd(u):
                        J, t = UP[u]
                        woff = J["woff"]
                        Sf, Ab = SinF[J["j"]], SinB[J["j"]]
                        if t == 0:
                            op("pool", lambda e: e.memset(Sf.t[:], 0.0), writes=[Sf.b])
                            op("pool", lambda e: e.memset(Ab.t[:], 0.0), writes=[Ab.b])
                            op("pool", lambda e: e.memset(Pdec.t[:], 1.0), writes=[Pdec.b])
                        xT_ = xT[u % 2]
                        wf = wts.t[:, woff + 2 * t:woff + 2 * t + 1]
                        wb = wts.t[:, woff + 2 * t + 1:woff + 2 * t + 2]
                        dd = Dfb[u % 2]
                        pkt = proj_tm(bankB, WA, xT_, KT0, 512)
                        op("dve", lambda e: e.scalar_tensor_tensor(out=khf.t[:], in0=pkt.t[:], scalar=wf, in1=ebs[u % 2].t[:], op0=ALU.mult, op1=ALU.mult),
                           reads=[pkt.b, wts.b, ebs[u % 2].b], writes=[khf.b])
                        op("dve", lambda e: e.scalar_tensor_tensor(out=khb.t[:], in0=pkt.t[:], scalar=wb, in1=epr[u % 2].t[:], op0=ALU.mult, op1=ALU.mult),
                           reads=[pkt.b, wts.b, epr[u % 2].b], writes=[khb.b])
                        yield
                        pvt = proj_tm(bankB, WA, xT_, VT0, 512)
                        op("act", lambda e: e.copy(out=vpp.t[:], in_=pvt.t[:]), reads=[pvt.b], writes=[vpp.b])
                        yield
                        puf = bankB()
                        for h in range(4):
                            hs = slice(h * 128, (h + 1) * 128)
                            op("pe", lambda e: e.matmul(puf.t[:, hs], lhsT=khf.t[:, hs], rhs=vpp.t[:, hs], start=True, stop=True),
                               reads=[khf.b, vpp.b], writes=[puf.b])
                        for h in range(4):
                            hs = slice(h * 128, (h + 1) * 128)
                            op("dve", lambda e: e.scalar_tensor_tensor(out=Sf.t[:, hs], in0=Sf.t[:, hs], scalar=dd.t[:, h:h + 1], in1=puf.t[:, hs],
                                                                       op0=ALU.mult, op1=ALU.add), reads=[Sf.b, dd.b, puf.b], writes=[Sf.b])
                        yield
                        pub = bankB()
                        for h in range(4):
                            hs = slice(h * 128, (h + 1) * 128)
                            op("pe", lambda e: e.matmul(pub.t[:, hs], lhsT=khb.t[:, hs], rhs=vpp.t[:, hs], start=True, stop=True),
                               reads=[khb.b, vpp.b], writes=[pub.b])
                        for h in range(4):
                            hs = slice(h * 128, (h + 1) * 128)
                            op("dve", lambda e: e.scalar_tensor_tensor(out=Ab.t[:, hs], in0=pub.t[:, hs], scalar=Pdec.t[:, h:h + 1], in1=Ab.t[:, hs],
                                                                       op0=ALU.mult, op1=ALU.add), reads=[Ab.b, Pdec.b, pub.b], writes=[Ab.b])
                        op("dve", lambda e: e.tensor_tensor(out=Pdec.t[:], in0=Pdec.t[:], in1=dd.t[:, 4:8], op=ALU.mult), reads=[Pdec.b, dd.b], writes=[Pdec.b])
                        yield

                    def w_late(u):
                        dst_, src_, c0, n_ = late_blocks[u]
                        load_weight(stg, dst_, src_, c0, n_, c0)
                        yield

                    assert len(late_blocks) <= len(UP)
                    run_pipe2([(p_load, -2, lambda u: True), (p_front, -1, lambda u: True), (p_upd, 0, lambda u: True),
                               (w_late, 0, lambda u: u < len(late_blocks))], len(UP))
                fw.barrier()

            WA = W0
            lg = sbt(st, "lgA", [128, 512], F32)
            decs = [[sbt(st, "decA%d_%d" % (i, j), [128, 512], F32) for j in range(3)] for i in range(2)]
            obs = [sbt(st, "obsA%d" % i, [128, 512], F32) for i in range(2)]
            raws = [[sbt(st, "rawA%d_%d" % (i, k), [128, 512], BF16) for k in range(4)] for i in range(2)]
            g = gla_bufs(st)
            bankA, bankB = Rot(banks[0:3]), Rot(banks[3:7])
            if True:
                def a_load(i):
                    J, t = UA[i]
                    x_d = J["x_d"]
                    dma("sp", xs[i % 3].t[:], x_d[t * 128:(t + 1) * 128, :], dx[i % 3], writes=[xs[i % 3].b])
                    op("pool", lambda e: e.tensor_copy(out=xb[i % 2].t[:], in_=xs[i % 3].t[:]), reads=[xs[i % 3].b], writes=[xb[i % 2].b])
                    yield

                def a_front(i):
                    yield from front(False, bankA, WA, xs[i % 3], xb[i % 2], xT[i % 2], lg, decs[i % 2])

                def a_main(i):
                    J, t = UA[i]
                    ob_d = J["ob_d"]
                    if t == J["n"] + 2:
                        init_state(SinB[J["j"]])
                    ob = obs[i % 2]

                    def o_evac(po):
                        op("act", lambda e: e.copy(out=ob.t[:], in_=po.t[:]), reads=[po.b], writes=[ob.b])
                        dma("sp", ob_d[t * 128:(t + 1) * 128, :], ob.t[:], dob[i % 2], reads=[ob.b])
                    rw = raws[i % 2]
                    yield from gla_main(False, bankB, WA, xT[i % 2], decs[i % 2], g, o_evac, raw=rw)
                    qkv_d = J["qkv_d"]
                    for k in range(4):
                        dma("sp", qkv_d[t * 128:(t + 1) * 128, k, :], rw[k].t[:], dqs[i % 2], reads=[rw[k].b])
                    for k in range(4):
                        rw[k].b.r[dqs[i % 2]] = dqs[i % 2].cnt
                    yield

                run_pipe2([(a_load, -2, lambda u: True), (a_front, -1, lambda u: True), (a_main, 0, lambda u: True)], len(UA))
            fw.barrier()

        with ExitStack() as st:
            WB = W0
            WO = WO0
            biasT = sbt(st, "biasT", [128, 8, 3, 128], F32)
            esink = sbt(st, "esink", [128, 8], F32)
            gn4 = sbt(st, "gn4", [128, 512], F32)
            lng = sbt(st, "lng", [128, D], F32)
            lnb = sbt(st, "lnb", [128, D], F32)
            dma_c(biasT.t[:].rearrange("p h b q -> p (h b q)"), bias_d[:, :], [biasT.b])
            dma_c(esink.t[:], sink_d[:, :], [esink.b])
            dma_c(gn4.t[:], gn_d[:, :], [gn4.b])
            dma_c(lng.t[:], lng_d[:, 0:D], [lng.b])
            dma_c(lnb.t[:], lnb_d[:, 0:D], [lnb.b])
            op("act", lambda e: e.activation(out=esink.t[:], in_=esink.t[:], func=AF.Exp), reads=[esink.b], writes=[esink.b])
            op("pool", lambda e: e.affine_select(out=biasT.t[:, :, 0, :], in_=biasT.t[:, :, 0, :], pattern=[[0, 8], [-1, 128]],
                                                 compare_op=ALU.is_ge, fill=NEG, base=0, channel_multiplier=1),
               reads=[biasT.b], writes=[biasT.b])
            op("pool", lambda e: e.affine_select(out=biasT.t[:, :, 2, :], in_=biasT.t[:, :, 2, :], pattern=[[0, 8], [1, 128]],
                                                 compare_op=ALU.is_ge, fill=NEG, base=0, channel_multiplier=-1),
               reads=[biasT.b], writes=[biasT.b])

            xs = [sbt(st, "xsB%d" % i, [128, D], F32) for i in range(5)]
            xb = [sbt(st, "xbB%d" % i, [128, D], BF16) for i in range(2)]
            lg = sbt(st, "lgB", [128, 512], F32)
            xT = [sbt(st, "xTB%d" % i, [128, 8, 128], BF16) for i in range(2)]
            decs = [[sbt(st, "decB%d_%d" % (i, j), [128, 512], F32) for j in range(3)] for i in range(2)]
            obl = [sbt(st, "oblB%d" % i, [128, 512], F32) for i in range(2)]
            raws = [[sbt(st, "rawB%d_%d" % (i, k), [128, 512], BF16) for k in range(4)] for i in range(2)]
            g = gla_bufs(st)
            o_sb = sbt(st, "o_sb", [128, 512], F32)
            ss = sbt(st, "ssB", [128, 8], F32)
            ez = sbt(st, "ezB", [128, 512], F32)
            gz = sbt(st, "gzB", [128, 512], F32)
            szb = [sbt(st, "szbB%d" % i, [128, 512], F32) for i in range(2)]
            ybuf = [sbt(st, "yB%d" % i, [128, D], BF16) for i in range(3)]
            qTB = [sbt(st, "qTB%d" % i, [128, 512], BF16) for i in range(2)]
            kTB = [sbt(st, "kTB%d" % i, [128, 128], BF16) for i in range(4)]
            vB = [sbt(st, "vB%d" % i, [128, 2, 65], BF16) for i in range(4)]
            sc = [sbt(st, "scB%d" % i, [128, 384], F32) for i in range(2)]
            pTs = [sbt(st, "pTsB%d" % i, [128, 384], BF16) for i in range(2)]
            den = sbt(st, "denB", [128, 8], F32)
            yT = sbt(st, "yTB", [128, 8, 128], BF16)
            h1s = [sbt(st, "h1B%d" % i, [128, D], F32) for i in range(2)]
            junk = sbt(st, "junkB", [128, D], BF16)
            sq = junk
            stat = sbt(st, "statB", [128, 4], F32)
            for v3 in vB:
                op("pool", lambda e: e.memset(v3.t[:], 1.0), writes=[v3.b])
            bankA, bankB, bankC = Rot(banks[0:2]), Rot(banks[2:4]), Rot(banks[5:7])

            if True:
                def b_load(u):
                    J, e_ = UE[u]
                    x_d, t = J["x_d"], u
                    dma("sp", xs[t % 5].t[:], x_d[e_ * 128:(e_ + 1) * 128, :], dx[t % 5], writes=[xs[t % 5].b])
                    op("pool", lambda e: e.tensor_copy(out=xb[t % 2].t[:], in_=xs[t % 5].t[:]), reads=[xs[t % 5].b], writes=[xb[t % 2].b])
                    yield

                def b_loadob(t):
                    J, e_ = UE[t]
                    ob_d = J["ob_d"]
                    dma("sp", obl[t % 2].t[:], ob_d[e_ * 128:(e_ + 1) * 128, :], dob[t % 2], writes=[obl[t % 2].b])
                    qkv_d, rw = J["qkv_d"], raws[t % 2]
                    for k in range(4):
                        dma("sp", rw[k].t[:], qkv_d[e_ * 128:(e_ + 1) * 128, k, :], dql[t % 2], writes=[rw[k].b])
                    for k in range(4):
                        rw[k].b.w = (dql[t % 2], dql[t % 2].cnt)
                    yield

                def b_front(t):
                    yield from front(True, bankA, WB, xs[t % 5], xb[t % 2], xT[t % 2], lg, decs[t % 2])
                    pkb = proj_fm(bankA, WB, xT[t % 2], KB0, 1)
                    op("act", lambda e: e.copy(out=kTB[t % 4].t[:], in_=pkb.t[:, 0:128]), reads=[pkb.b], writes=[kTB[t % 4].b])
                    yield
                    pvb = proj_tm(bankA, WB, xT[t % 2], VB0, 128)
                    vs = vB[t % 4]
                    op("act", lambda e: e.copy(out=vs.t[:, :, 0:64], in_=pvb.t[:, 0:128].rearrange("p (a b) -> p a b", a=2)),
                       reads=[pvb.b], writes=[vs.b])
                    yield

                def b_main(t):
                    J, e_ = UE[t]
                    par = t % 2
                    if e_ == 1:
                        init_state(SinF[J["j"]])
                    y = ybuf[t % 3]

                    def o_evac(po):
                        op("dve", lambda e: e.tensor_tensor(out=o_sb.t[:], in0=po.t[:], in1=obl[par].t[:], op=ALU.add),
                           reads=[po.b, obl[par].b], writes=[o_sb.b])
                    yield from gla_main(True, bankB, WB, xT[par], decs[par], g, o_evac, pre=raws[par])
                    for h in range(4):
                        hs = slice(h * 128, (h + 1) * 128)
                        op("act", lambda e: e.activation(out=sq.t[:, hs], in_=o_sb.t[:, hs], func=AF.Square, accum_out=ss.t[:, h:h + 1]),
                           reads=[o_sb.b], writes=[sq.b, ss.b])
                    op("dve", lambda e: e.tensor_scalar(out=ss.t[:, 0:4], in0=ss.t[:, 0:4], scalar1=1.0 / 128, scalar2=None, op0=ALU.mult),
                       reads=[ss.b], writes=[ss.b])
                    op("act", lambda e: e.activation(out=ss.t[:, 0:4], in_=ss.t[:, 0:4], func=AF.Ln, bias=NORM_EPS), reads=[ss.b], writes=[ss.b])
                    op("act", lambda e: e.activation(out=ss.t[:, 0:4], in_=ss.t[:, 0:4], func=AF.Exp, scale=-0.5), reads=[ss.b], writes=[ss.b])
                    yield
                    pz = proj_tm(bankB, WB, xT[par], ZA0, 512)
                    silu_from_psum(pz, ez, gz)
                    op("pool", lambda e: e.tensor_tensor(out=gz.t[:], in0=gz.t[:], in1=gn4.t[:], op=ALU.mult), reads=[gz.b, gn4.b], writes=[gz.b])
                    yield
                    for h in range(4):
                        hs = slice(h * 128, (h + 1) * 128)
                        op("dve", lambda e: e.scalar_tensor_tensor(out=y.t[:, hs], in0=o_sb.t[:, hs], scalar=ss.t[:, h:h + 1], in1=gz.t[:, hs],
                                                                   op0=ALU.mult, op1=ALU.mult), reads=[o_sb.b, ss.b, gz.b], writes=[y.b])
                    yield
                    pz2 = proj_tm(bankB, WB, xT[par], ZB0, 512)
                    silu_from_psum(pz2, ez, szb[par])
                    yield
                    pqb = proj_fm(bankB, WB, xT[par], QB0, 4)
                    op("act", lambda e: e.activation(out=qTB[par].t[:], in_=pqb.t[:], func=AF.Copy, scale=0.125), reads=[pqb.b], writes=[qTB[par].b])
                    yield

                def b_back(t):
                    J, e_ = UE[t]
                    n, jj = J["n"], J["j"]
                    par = t % 2
                    y = ybuf[t % 3]
                    blks = [(0, t - 1, 2 * jj if e_ == 2 else None), (1, t, None), (2, t + 1, 2 * jj + 1 if e_ == n + 1 else None)]
                    b0, b1 = 0, 3

                    def scores(h):
                        kv, c = h // 4, h % 4
                        rs = slice(kv * 64, (kv + 1) * 64)
                        pss = banks[5 + h % 2]
                        for (bi, tt, _) in blks:
                            op("pe", lambda e: e.matmul(pss.t[:, bi * 128:(bi + 1) * 128], lhsT=kTB[tt % 4].t[rs, :],
                                                        rhs=qTB[par].t[rs, c * 128:(c + 1) * 128], start=True, stop=True),
                               reads=[kTB[tt % 4].b, qTB[par].b], writes=[pss.b])
                        s_, p_ = sc[h % 2], pTs[h % 2]
                        op("dve", lambda e: e.tensor_tensor(out=s_.t[:, b0 * 128:b1 * 128], in0=pss.t[:, b0 * 128:b1 * 128],
                                                            in1=biasT.t[:, h, b0:b1, :].rearrange("p b q -> p (b q)"), op=ALU.add),
                           reads=[pss.b, biasT.b], writes=[s_.b])
                        for (bi, tt, fc) in blks:
                            if fc is not None:
                                op("dve", lambda e: e.tensor_scalar(out=s_.t[:, bi * 128:(bi + 1) * 128], in0=s_.t[:, bi * 128:(bi + 1) * 128],
                                                                    scalar1=negm.t[:, fc:fc + 1], scalar2=None, op0=ALU.add),
                                   reads=[s_.b, negm.b], writes=[s_.b])
                        op("act", lambda e: e.activation(out=p_.t[:, b0 * 128:b1 * 128], in_=s_.t[:, b0 * 128:b1 * 128], func=AF.Exp),
                           reads=[s_.b], writes=[p_.b])

                    for half in range(2):
                        pv = banks[4]
                        for hh in range(4):
                            h = half * 4 + hh
                            kv = h // 4
                            if h == 0:
                                scores(0)
                            if h + 1 < 8:
                                scores(h + 1)
                            yield
                            p_ = pTs[h % 2]
                            hc = hh * 65
                            for n_, (bi, tt, _) in enumerate(blks):
                                op("pe", lambda e: e.matmul(pv.t[:, hc:hc + 65], lhsT=p_.t[:, bi * 128:(bi + 1) * 128], rhs=vB[tt % 4].t[:, kv, :],
                                                            start=(n_ == 0), stop=(n_ == len(blks) - 1)),
                                   reads=[p_.b, vB[tt % 4].b], writes=[pv.b])
                        pvv = pv.t[:, 0:260].rearrange("p (h c) -> p h c", c=65)
                        op("dve", lambda e: e.tensor_tensor(out=den.t[:, half * 4:half * 4 + 4], in0=pvv[:, :, 64],
                                                            in1=esink.t[:, half * 4:half * 4 + 4], op=ALU.add),
                           reads=[pv.b, esink.b], writes=[den.b])
                        op("dve", lambda e: e.reciprocal(out=den.t[:, half * 4:half * 4 + 4], in_=den.t[:, half * 4:half * 4 + 4]),
                           reads=[den.b], writes=[den.b])
                        for hh in range(4):
                            h = half * 4 + hh
                            op("dve", lambda e: e.scalar_tensor_tensor(out=y.t[:, 512 + h * 64:512 + (h + 1) * 64], in0=pvv[:, hh, 0:64],
                                                                       scalar=den.t[:, h:h + 1], in1=szb[par].t[:, h * 64:(h + 1) * 64],
                                                                       op0=ALU.mult, op1=ALU.mult),
                               reads=[pv.b, den.b, szb[par].b], writes=[y.b])
                        yield

                def b_out(t):
                    J, e_ = UE[t]
                    x1_d = J["x1_d"]
                    par = t % 2
                    y = ybuf[t % 3]
                    for c in range(8):
                        op("pe", lambda e: e.transpose(out=pT.t[:, c, :], in_=y.t[:, c * 128:(c + 1) * 128], identity=ident.t[:]),
                           reads=[y.b, ident.b], writes=[pT.b])
                    op("act", lambda e: e.copy(out=yT.t[:], in_=pT.t[:]), reads=[pT.b], writes=[yT.b])
                    yield
                    h1 = h1s[par]
                    for nb in range(2):
                        pb = bankC()
                        for kc in range(8):
                            op("pe", lambda e: e.matmul(pb.t[:, :], lhsT=yT.t[:, kc, :], rhs=WO.t[:, kc, nb * 512:(nb + 1) * 512],
                                                        start=(kc == 0), stop=(kc == 7)), reads=[yT.b, WO.b], writes=[pb.b])
                        op("dve", lambda e: e.scalar_tensor_tensor(out=h1.t[:, nb * 512:(nb + 1) * 512], in0=xs[t % 5].t[:, nb * 512:(nb + 1) * 512],
                                                                   scalar=ALPHA, in1=pb.t[:, :], op0=ALU.mult, op1=ALU.add),
                           reads=[xs[t % 5].b, pb.b], writes=[h1.b])
                        yield
                    layer_norm(junk, stat, lng, lnb, h1)
                    dma("sp", x1_d[e_ * 128:(e_ + 1) * 128, :], h1.t[:], dst[par], reads=[h1.b])
                    yield

                run_pipe2([(b_load, -2, lambda u: True), (b_loadob, -1, main_e), (b_front, -1, lambda u: True),
                           (b_main, 0, main_e), (b_back, 1, main_e), (b_out, 2, main_e)], len(UE))
            fw.barrier()

        stW.close()
        with ExitStack() as st:
            W1 = sbt(st, "W1", [128, 8, 4 * D], BF16)
            WO = sbt(st, "WO1", [128, 8, D], BF16)
            with ExitStack() as st2:
                stg = [sbt(st2, "stgC%d" % i, [128, 8, 512], F32) for i in range(2)]
                load_weight(stg, W1, w1_d, 0, 4 * D, 0)
                load_weight(stg, WO, wo1_d, 0, D, 0)
                fw.barrier()
            lng = sbt(st, "lngC", [128, D], F32)
            lnb = sbt(st, "lnbC", [128, D], F32)
            cw = sbt(st, "cw", [128, 8, 3], F32)
            dma_c(lng.t[:], lng_d[:, D:2 * D], [lng.b])
            dma_c(lnb.t[:], lnb_d[:, D:2 * D], [lnb.b])
            dma_c(cw.t[:].rearrange("p c k -> p (c k)"), cw_d[:, :], [cw.b])
            xs = [sbt(st, "xsC%d" % i, [128, D], F32) for i in range(6)]
            xb = [sbt(st, "xbC%d" % i, [128, D], BF16) for i in range(2)]
            xT = [sbt(st, "xTC%d" % i, [128, 8, 128], BF16) for i in range(2)]
            TT = [sbt(st, "TT%d" % i, [128, 8, 130], F32) for i in range(3)]
            ub = [sbt(st, "ub%d" % i, [128, D], F32) for i in range(3)]
            hsb = sbt(st, "hsb", [128, 512], F32)
            ez = sbt(st, "ezC", [128, 512], F32)
            cv = sbt(st, "cvC", [128, D], F32)
            gTs = [sbt(st, "gTC%d" % i, [128, D], BF16) for i in range(2)]
            h1s = [sbt(st, "h1C%d" % i, [128, D], F32) for i in range(2)]
            junk = sbt(st, "junkC", [128, D], BF16)
            stat = sbt(st, "statC", [128, 4], F32)
            for T_ in TT:
                op("pool", lambda e: e.memset(T_.t[:], 0.0), writes=[T_.b])
            bankB, bankC = Rot(banks[0:5]), Rot(banks[5:7])

            if True:
                def c_load(t):
                    J, e_ = UE[t]
                    x1_d = J["x1_d"]
                    dma("sp", xs[t % 6].t[:], x1_d[e_ * 128:(e_ + 1) * 128, :], dx[t % 6], writes=[xs[t % 6].b])
                    op("pool", lambda e: e.tensor_copy(out=xb[t % 2].t[:], in_=xs[t % 6].t[:]), reads=[xs[t % 6].b], writes=[xb[t % 2].b])
                    yield

                def c_front(t):
                    xT_ = xT[t % 2]
                    xb_ = xb[t % 2]
                    for c in range(8):
                        op("pe", lambda e: e.transpose(out=pT.t[:, c, :], in_=xb_.t[:, c * 128:(c + 1) * 128], identity=ident.t[:]),
                           reads=[xb_.b, ident.b], writes=[pT.b])
                    op("act", lambda e: e.copy(out=xT_.t[:], in_=pT.t[:]), reads=[pT.b], writes=[xT_.b])
                    yield

                def c_main(t):
                    J, e_ = UE[t]
                    n, jj = J["n"], J["j"]
                    xT_ = xT[t % 2]
                    T_ = TT[t % 3]
                    u_ = ub[t % 3]
                    op("pool", lambda e: e.memset(T_.t[:, :, 0:1], 0.0), writes=[T_.b])
                    op("pool", lambda e: e.memset(T_.t[:, :, 129:130], 0.0), writes=[T_.b])
                    for half in range(2):
                        fs = slice(half * 512, (half + 1) * 512)
                        ph = proj_fm(bankB, W1, xT_, 2 * D + half * 512, 4)
                        op("act", lambda e: e.copy(out=hsb.t[:], in_=ph.t[:]), reads=[ph.b], writes=[hsb.b])
                        yield
                        pc_ = proj_fm(bankB, W1, xT_, D + half * 512, 4)
                        op("dve", lambda e: e.tensor_tensor(out=T_.t[:, half * 4:half * 4 + 4, 1:129],
                                                            in0=pc_.t[:].rearrange("p (c q) -> p c q", c=4),
                                                            in1=hsb.t[:].rearrange("p (c q) -> p c q", c=4), op=ALU.mult),
                           reads=[pc_.b, hsb.b], writes=[T_.b])
                        yield
                        pz = proj_fm(bankB, W1, xT_, 3 * D + half * 512, 4)
                        op("act", lambda e: e.activation(out=ez.t[:], in_=pz.t[:], func=AF.Exp, scale=-1.0), reads=[pz.b], writes=[ez.b])
                        op("act", lambda e: e.activation(out=ez.t[:], in_=ez.t[:], func=AF.Ln, bias=1.0), reads=[ez.b], writes=[ez.b])
                        op("act", lambda e: e.activation(out=ez.t[:], in_=ez.t[:], func=AF.Exp, scale=-1.0), reads=[ez.b], writes=[ez.b])
                        op("dve", lambda e: e.tensor_tensor(out=ez.t[:], in0=pz.t[:], in1=ez.t[:], op=ALU.mult), reads=[pz.b, ez.b], writes=[ez.b])
                        yield
                        pbg = proj_fm(bankB, W1, xT_, half * 512, 4)
                        op("dve", lambda e: e.tensor_tensor(out=u_.t[:, fs], in0=pbg.t[:], in1=ez.t[:], op=ALU.mult),
                           reads=[pbg.b, ez.b], writes=[u_.b])
                        yield
                    if e_ > 1:
                        Tp = TT[(t - 1) % 3]
                        if e_ == 2:
                            fc = 2 * jj
                            op("dve", lambda e: e.tensor_scalar(out=T_.t[:, :, 0:1], in0=Tp.t[:, :, 128:129], scalar1=rfl.t[:, fc:fc + 1], scalar2=None,
                                                                op0=ALU.mult), reads=[Tp.b, rfl.b], writes=[T_.b])
                        elif e_ == n + 2:
                            fc = 2 * jj + 1
                            op("dve", lambda e: e.tensor_scalar(out=Tp.t[:, :, 129:130], in0=T_.t[:, :, 1:2], scalar1=rfl.t[:, fc:fc + 1], scalar2=None,
                                                                op0=ALU.mult), reads=[T_.b, rfl.b], writes=[Tp.b])
                        else:
                            op("pool", lambda e: e.tensor_copy(out=T_.t[:, :, 0:1], in_=Tp.t[:, :, 128:129]), reads=[Tp.b], writes=[T_.b])
                            op("pool", lambda e: e.tensor_copy(out=Tp.t[:, :, 129:130], in_=T_.t[:, :, 1:2]), reads=[T_.b], writes=[Tp.b])
                    yield

                def c_back(t):
                    T_ = TT[t % 3]
                    u_ = ub[t % 3]
                    for c in range(8):
                        cs = slice(c * 128, (c + 1) * 128)
                        op("dve", lambda e: e.tensor_scalar(out=cv.t[:, cs], in0=T_.t[:, c, 0:128], scalar1=cw.t[:, c, 0:1], scalar2=None, op0=ALU.mult),
                           reads=[T_.b, cw.b], writes=[cv.b])
                        op("dve", lambda e: e.scalar_tensor_tensor(out=cv.t[:, cs], in0=T_.t[:, c, 1:129], scalar=cw.t[:, c, 1:2], in1=cv.t[:, cs],
                                                                   op0=ALU.mult, op1=ALU.add), reads=[T_.b, cw.b, cv.b], writes=[cv.b])
                        op("dve", lambda e: e.scalar_tensor_tensor(out=cv.t[:, cs], in0=T_.t[:, c, 2:130], scalar=cw.t[:, c, 2:3], in1=cv.t[:, cs],
                                                                   op0=ALU.mult, op1=ALU.add), reads=[T_.b, cw.b, cv.b], writes=[cv.b])
                        if c % 2 == 1:
                            yield
                    gT = gTs[t % 2]
                    op("dve", lambda e: e.tensor_tensor(out=gT.t[:], in0=cv.t[:], in1=u_.t[:], op=ALU.mult), reads=[cv.b, u_.b], writes=[gT.b])
                    yield

                def c_out(t):
                    J, e_ = UE[t]
                    y_d = J["y_d"]
                    gT = gTs[t % 2]
                    h1 = h1s[t % 2]
                    for nb in range(2):
                        pb = bankC()
                        for kc in range(8):
                            op("pe", lambda e: e.matmul(pb.t[:, :], lhsT=gT.t[:, kc * 128:(kc + 1) * 128], rhs=WO.t[:, kc, nb * 512:(nb + 1) * 512],
                                                        start=(kc == 0), stop=(kc == 7)), reads=[gT.b, WO.b], writes=[pb.b])
                        op("dve", lambda e: e.scalar_tensor_tensor(out=h1.t[:, nb * 512:(nb + 1) * 512], in0=xs[t % 6].t[:, nb * 512:(nb + 1) * 512],
                                                                   scalar=ALPHA, in1=pb.t[:, :], op0=ALU.mult, op1=ALU.add),
                           reads=[xs[t % 6].b, pb.b], writes=[h1.b])
                        yield
                    layer_norm(junk, stat, lng, lnb, h1)
                    dma("sp", y_d[(e_ - 2) * 128:(e_ - 1) * 128, :], h1.t[:], dst[t % 2], reads=[h1.b])
                    yield

                run_pipe2([(c_load, -2, main_e), (c_front, -1, main_e), (c_main, 0, main_e),
                           (c_back, 2, own_e), (c_out, 3, own_e)], len(UE))
            fw.barrier()
        for d_ in dst:
            nc.sync.wait_ge(d_.h, d_.cnt)
    return nc


def _rel_buckets():
    BLOCK, REL_BUCKETS, REL_MAX_DIST = 128, 32, 128
    i = np.arange(BLOCK)[:, None]
    j = np.arange(3 * BLOCK)[None, :]
    rel = j - BLOCK - i
    half = REL_BUCKETS // 2
    max_exact = half // 2
    n = np.abs(rel)
    large = max_exact + (np.log(np.maximum(n, 1) / max_exact) / np.log(REL_MAX_DIST / max_exact)
                         * (half - max_exact)).astype(np.int32)
    large = np.minimum(large, half - 1)
    bucket = (rel > 0).astype(np.int32) * half + np.where(n < max_exact, n, large)
    return bucket.astype(np.int32)


def pack_w0(w):
    qa, ka, va, za = w[:, 0:512], w[:, 512:1024], w[:, 1024:1536], w[:, 1536:2048]
    gd = w[:, 2048:2080]
    o = 2080
    qb, kb, vb, zb = w[:, o:o + 512], w[:, o + 512:o + 640], w[:, o + 640:o + 768], w[:, o + 768:o + 1280]
    out = np.zeros((D, NC0), np.float32)
    out[:, QA0:QA0 + 512] = qa
    out[:, KA0:KA0 + 512] = ka
    for c in range(4):
        out[:, QB0 + c * 128:QB0 + c * 128 + 64] = qb[:, c * 64:(c + 1) * 64]
        out[:, QB0 + c * 128 + 64:QB0 + (c + 1) * 128] = qb[:, (c + 4) * 64:(c + 5) * 64]
    out[:, KB0:KB0 + 128] = kb
    out[:, GD0:GD0 + 16] = gd[:, 0:16]
    out[:, GD0 + 32:GD0 + 48] = gd[:, 16:32]
    out[:, KT0:KT0 + 512] = ka
    out[:, VT0:VT0 + 512] = va
    out[:, ZA0:ZA0 + 512] = za
    out[:, ZB0:ZB0 + 512] = zb
    out[:, VB0:VB0 + 128] = vb
    return out


def make_shared(inp):
    f = lambda a: np.ascontiguousarray(np.asarray(a, dtype=np.float32))
    sh = {}
    sh["w0"] = pack_w0(f(inp["w_in_even"])[0])
    sh["wo0"] = f(inp["w_out_even"])[0]
    sh["w1"] = f(inp["w_in_odd"])[0]
    sh["wo1"] = f(inp["w_out_odd"])[0]
    wup = np.zeros((64, 512), np.float32)
    wup[0:16] = f(inp["gla_w_up_fwd"])[0]
    wup[16] = f(inp["gla_b_fwd"])[0]
    wup[32:48] = f(inp["gla_w_up_bwd"])[0]
    wup[48] = f(inp["gla_b_bwd"])[0]
    sh["wup"] = wup
    bucket = _rel_buckets()
    rb = f(inp["rel_bias"])
    bfull = rb[bucket]
    bT = bfull.reshape(128, 3, 128, 8).transpose(2, 3, 1, 0)
    sh["biasT"] = np.ascontiguousarray(bT).reshape(128, 8 * 3 * 128)
    sh["sink"] = np.ascontiguousarray(np.broadcast_to(f(inp["swa_sink"])[0][None, :], (128, 8)))
    sh["gn"] = np.ascontiguousarray(np.broadcast_to(np.tile(f(inp["gla_norm_g"])[0], 4)[None, :], (128, 512)))
    sh["lng"] = np.ascontiguousarray(np.broadcast_to(f(inp["ln_g"]).reshape(1, 2 * D), (128, 2 * D)))
    sh["lnb"] = np.ascontiguousarray(np.broadcast_to(f(inp["ln_b"]).reshape(1, 2 * D), (128, 2 * D)))
    cw = f(inp["conv_w"])[0]
    sh["convw"] = np.ascontiguousarray(cw.reshape(3, 8, 128).transpose(2, 1, 0)).reshape(128, 24)
    return sh


_NC_CACHE = {}


def _ext(xseq, a, n):
    L = xseq.shape[0]
    out = np.zeros(((n + 4) * 128, D), np.float32)
    lo, hi = (a - 2) * 128, (a + n + 2) * 128
    slo, shi = max(lo, 0), min(hi, L)
    out[slo - lo:shi - lo] = xseq[slo:shi]
    return out


def _oth(xseq, a, n, NO):
    Ls = xseq.shape[0] // 128
    pre = list(range(0, max(a - 1, 0)))
    suf = list(range(min(a + n + 1, Ls), Ls))
    out = np.zeros((NO * 128, D), np.float32)
    wts = np.zeros((NO, 2), np.float32)
    for k, t in enumerate(pre + suf):
        out[k * 128:(k + 1) * 128] = xseq[t * 128:(t + 1) * 128]
        wts[k, 0 if k < len(pre) else 1] = 1.0
    return out, wts


def run(xp, xs_, inp, n_cores=8):
    Bp, LP, _ = xp.shape
    Bs, LS, _ = xs_.shape
    assert Bp == 2 and Bs == 4 and LP % 512 == 0 and LS % 256 == 0
    nP, nS = LP // 128 // 4, LS // 128 // 2
    key = (nP, nS)
    if key not in _NC_CACHE:
        _NC_CACHE[key] = build(nP, nS)
    nc = _NC_CACHE[key]
    sh = make_shared(inp)
    in_maps = []
    for c in range(n_cores):
        m = dict(sh)
        aP, aS = (c % 4) * nP, (c % 2) * nS
        xP, xS = xp[c // 4], xs_[c // 2]
        m["xextP"] = _ext(xP, aP, nP)
        m["xextS"] = _ext(xS, aS, nS)
        m["xothP"], wP = _oth(xP, aP, nP, 3 * nP)
        m["xothS"], wS = _oth(xS, aS, nS, nS)
        wrow = np.concatenate([wP.reshape(-1), wS.reshape(-1)])[None, :]
        m["wts"] = np.ascontiguousarray(np.broadcast_to(wrow, (128, wrow.shape[1]))).astype(np.float32)
        fl = np.array([[float(aP > 0), float(aP + nP < 4 * nP), float(aS > 0), float(aS + nS < 2 * nS)]], np.float32)
        m["flags"] = np.ascontiguousarray(np.broadcast_to(fl, (128, 4)))
        in_maps.append(m)
    res = run_bass_kernel_spmd(nc, in_maps, core_ids=list(range(n_cores)))
    yp = np.zeros((Bp, LP, D), np.float32)
    ys = np.zeros((Bs, LS, D), np.float32)
    for c in range(n_cores):
        aP, aS = (c % 4) * nP, (c % 2) * nS
        yp[c // 4, aP * 128:(aP + nP) * 128] = res.results[c]["yP"]
        ys[c // 2, aS * 128:(aS + nS) * 128] = res.results[c]["yS"]
    return yp, ys


def kernel(x_prompt, x_sample, **w):
    xp = np.ascontiguousarray(np.asarray(x_prompt, dtype=np.float32))
    xs_ = np.ascontiguousarray(np.asarray(x_sample, dtype=np.float32))
    return run(xp, xs_, w)
```

```python
import numpy as np
from contextlib import ExitStack
import concourse.bass as bass
import concourse.mybir as mybir
from concourse.bass_utils import run_bass_kernel_spmd

F32 = mybir.dt.float32
BF16 = mybir.dt.bfloat16
AF = mybir.ActivationFunctionType
ALU = mybir.AluOpType

D = 1024
ALPHA = 4 ** 0.25
LN_EPS = 1e-5
NORM_EPS = 1e-6
NEG = -30000.0

QA0, KA0, QB0, KB0, GD0 = 0, 512, 1024, 1536, 1664
KT0, VT0, ZA0, ZB0, VB0 = 1728, 2240, 2752, 3264, 3776
NC0 = 3904


class Buf:
    def __init__(self, name):
        self.name = name
        self.w = None
        self.r = {}


class Sem:
    def __init__(self, nc, stack, name):
        self.h = stack.enter_context(nc.semaphore(name))
        self.cnt = 0
        self.name = name


class Eng:
    def __init__(self, name, e, sem):
        self.name, self.e, self.sem = name, e, sem
        self.waited = {}


class FW:
    def __init__(self, nc, stack):
        self.nc = nc
        self.stack = stack
        self.engs = {}
        for name, e in (("pe", nc.tensor), ("act", nc.scalar), ("dve", nc.vector), ("pool", nc.gpsimd), ("sp", nc.sync)):
            self.engs[name] = Eng(name, e, Sem(nc, stack, "s_" + name))
        self.dsems = []
        self.n_instr = 0

    def dsem(self, name):
        s = Sem(self.nc, self.stack, name)
        self.dsems.append(s)
        return s

    def _waits(self, eng, reads, writes):
        need = {}

        def add(p):
            if p is None:
                return
            s, c = p
            if need.get(s, 0) < c:
                need[s] = c
        for b in reads:
            add(b.w)
        for b in writes:
            add(b.w)
            for s, c in b.r.items():
                add((s, c))
        for s, c in need.items():
            if s is eng.sem and eng.name == "pe":
                continue
            if eng.waited.get(s, 0) >= c:
                continue
            eng.e.wait_ge(s.h, c)
            eng.waited[s] = c

    def _mark(self, tok, reads, writes):
        s, c = tok
        for b in reads:
            b.r[s] = c
        for b in writes:
            b.w = tok
            b.r = {}

    def op(self, engname, fn, reads=(), writes=()):
        eng = self.engs[engname]
        self._waits(eng, reads, writes)
        ins = fn(eng.e)
        eng.sem.cnt += 1
        ins.then_inc(eng.sem.h, 1)
        self.n_instr += 1
        tok = (eng.sem, eng.sem.cnt)
        self._mark(tok, reads, writes)
        return tok

    def dma(self, engname, out, in_, sem, reads=(), writes=()):
        eng = self.engs[engname]
        self._waits(eng, reads, writes)
        ins = eng.e.dma_start(out=out, in_=in_)
        sem.cnt += 16
        ins.then_inc(sem.h, 16)
        self.n_instr += 1
        tok = (sem, sem.cnt)
        self._mark(tok, reads, writes)
        return tok

    def barrier(self):
        sems = [e.sem for e in self.engs.values()] + self.dsems
        for eng in self.engs.values():
            for s in sems:
                if s is eng.sem or s.cnt == 0:
                    continue
                if eng.waited.get(s, 0) >= s.cnt:
                    continue
                eng.e.wait_ge(s.h, s.cnt)
                eng.waited[s] = s.cnt


class TB:
    def __init__(self, t, name):
        self.t = t
        self.b = Buf(name)


def sched(gens):
    gens = list(gens)
    while gens:
        for g_ in list(gens):
            try:
                next(g_)
            except StopIteration:
                gens.remove(g_)


def build(nP, nS):
    JOBS = [dict(name="P", n=nP, NO=3 * nP, j=0, woff=0), dict(name="S", n=nS, NO=nS, j=1, woff=2 * 3 * nP)]
    NWT = 2 * (3 * nP + nS)
    nc = bass.Bass("TRN2", target_bir_lowering=False)

    def din(name, shape):
        return nc.dram_tensor(name, shape, F32, kind="ExternalInput").ap()
    for J in JOBS:
        J["x_d"] = din("xext" + J["name"], [(J["n"] + 4) * 128, D])
        J["xo_d"] = din("xoth" + J["name"], [J["NO"] * 128, D])
        J["y_d"] = nc.dram_tensor("y" + J["name"], [J["n"] * 128, D], F32, kind="ExternalOutput").ap()
        J["ob_d"] = nc.dram_tensor("ob_scr" + J["name"], [(J["n"] + 4) * 128, 512], F32).ap()
        J["x1_d"] = nc.dram_tensor("x1_scr" + J["name"], [(J["n"] + 4) * 128, D], F32).ap()
        J["qkv_d"] = nc.dram_tensor("qkv_scr" + J["name"], [(J["n"] + 4) * 128, 4, 512], BF16).ap()
    w0_d = din("w0", [D, NC0])
    wo0_d = din("wo0", [D, D])
    w1_d = din("w1", [D, 4 * D])
    wo1_d = din("wo1", [D, D])
    wup_d = din("wup", [64, 512])
    bias_d = din("biasT", [128, 8 * 3 * 128])
    sink_d = din("sink", [128, 8])
    gn_d = din("gn", [128, 512])
    lng_d = din("lng", [128, 2 * D])
    lnb_d = din("lnb", [128, 2 * D])
    cw_d = din("convw", [128, 24])
    fl_d = din("flags", [128, 4])
    wt_d = din("wts", [128, NWT])

    with ExitStack() as gst:
        fw = FW(nc, gst)
        op, dma = fw.op, fw.dma
        uid = [0]

        def sbt(st, name, shape, dt):
            uid[0] += 1
            return TB(st.enter_context(nc.sbuf_tensor("sb%d_%s" % (uid[0], name), shape, dt)), name)

        def pst(st, name, shape, dt):
            return TB(st.enter_context(nc.psum_tensor("ps_" + name, shape, dt)), name)

        ident = sbt(gst, "ident", [128, 128], BF16)
        mU = sbt(gst, "mU", [128, 128], F32)
        mL = sbt(gst, "mL", [128, 128], F32)
        mUs = sbt(gst, "mUs", [128, 128], F32)
        mLs = sbt(gst, "mLs", [128, 128], F32)
        m01U = sbt(gst, "m01U", [128, 512], BF16)
        m01Ls = sbt(gst, "m01Ls", [128, 512], BF16)
        wup = sbt(gst, "wup", [64, 512], BF16)
        rfl = sbt(gst, "rfl", [128, 4], F32)
        negm = sbt(gst, "negm", [128, 4], F32)
        wts = sbt(gst, "wts", [128, NWT], F32)
        m16c = sbt(gst, "m16c", [128, 1], F32)
        SinF = [sbt(gst, "SinF%d" % i, [128, 512], F32) for i in range(2)]
        Pdec = sbt(gst, "Pdec", [128, 4], F32)
        S = sbt(gst, "S", [128, 512], F32)
        Sbf = sbt(gst, "Sbf", [128, 512], BF16)
        gda = sbt(gst, "gda", [64, 128], BF16)
        cn = [0]

        def dma_c(out, in_, writes):
            cn[0] += 1
            return dma("sp", out, in_, fw.dsem("dc%d" % cn[0]), writes=writes)
        dx = [fw.dsem("dx%d" % i) for i in range(6)]
        dst = [fw.dsem("dst%d" % i) for i in range(2)]
        dob = [fw.dsem("dob%d" % i) for i in range(2)]
        dw = [fw.dsem("dw%d" % i) for i in range(2)]
        dqs = [fw.dsem("dqs%d" % i) for i in range(2)]
        dql = [fw.dsem("dql%d" % i) for i in range(2)]

        def mask(tb, val, cm, base, nrep=1):
            pat = [[-cm, 128]] if nrep == 1 else [[0, nrep], [-cm, 128]]
            ap = tb.t[:] if nrep == 1 else tb.t[:].rearrange("p (a b) -> p a b", a=nrep)
            op("pool", lambda e: e.memset(tb.t[:], val), writes=[tb.b])
            op("pool", lambda e: e.affine_select(out=ap, in_=ap, pattern=pat, compare_op=ALU.is_ge, fill=0.0,
                                                 base=base, channel_multiplier=cm), reads=[tb.b], writes=[tb.b])
        mask(mU, -0.0625, -1, 0)
        mask(mL, -0.0625, 1, 0)
        mask(mUs, -0.0625, -1, -1)
        mask(mLs, -0.0625, 1, -1)
        mask(m01U, 1.0, -1, 0, 4)
        mask(m01Ls, 1.0, 1, -1, 4)
        with ExitStack() as st0:
            tmpf = sbt(st0, "tmpf", [128, 512], F32)
            op("pool", lambda e: e.memset(tmpf.t[:, 0:128], 1.0), writes=[tmpf.b])
            op("pool", lambda e: e.affine_select(out=tmpf.t[:, 0:128], in_=tmpf.t[:, 0:128], pattern=[[-1, 128]],
                                                 compare_op=ALU.is_equal, fill=0.0, base=0, channel_multiplier=1),
               reads=[tmpf.b], writes=[tmpf.b])
            op("dve", lambda e: e.tensor_copy(out=ident.t[:], in_=tmpf.t[:, 0:128]), reads=[tmpf.b], writes=[ident.b])
            dma_c(tmpf.t[0:64, :], wup_d[:, :], [tmpf.b])
            op("dve", lambda e: e.tensor_copy(out=wup.t[:], in_=tmpf.t[0:64, :]), reads=[tmpf.b], writes=[wup.b])
            dma_c(rfl.t[:], fl_d[:, :], [rfl.b])
            dma_c(wts.t[:], wt_d[:, :], [wts.b])
            op("pool", lambda e: e.memset(m16c.t[:], -0.0625), writes=[m16c.b])
            op("dve", lambda e: e.tensor_scalar(out=negm.t[:], in0=rfl.t[:], scalar1=-1.0, scalar2=-NEG, op0=ALU.add, op1=ALU.mult),
               reads=[rfl.b], writes=[negm.b])
            op("pool", lambda e: e.memset(gda.t[:], 1.0), writes=[gda.b])
            fw.barrier()

        pT = pst(gst, "pT", [128, 8, 128], BF16)
        banks = [pst(gst, "pb%d" % i, [128, 512], F32) for i in range(7)]

        class Rot:
            def __init__(self, lst):
                self.l, self.i = lst, 0

            def __call__(self):
                b = self.l[self.i % len(self.l)]
                self.i += 1
                return b

        wst_n = [0]

        def load_weight(stg, dst, src_d, col0, ncols, dcol0):
            src = src_d.rearrange("(c p) n -> p c n", p=128)
            c = 0
            while c < ncols:
                n = min(512, ncols - c)
                i = wst_n[0] % 2
                wst_n[0] += 1
                s = stg[i]
                dma("sp", s.t[:, :, 0:n], src[:, :, col0 + c:col0 + c + n], dw[i], writes=[s.b])
                eng = ("act", "dve")[wst_n[0] % 2]
                dd = dst.t[:, :, dcol0 + c:dcol0 + c + n]
                if eng == "act":
                    op("act", lambda e: e.copy(out=dd, in_=s.t[:, :, 0:n]), reads=[s.b], writes=[dst.b])
                else:
                    op(eng, lambda e: e.tensor_copy(out=dd, in_=s.t[:, :, 0:n]), reads=[s.b], writes=[dst.b])
                c += n

        def proj_tm(bankf, W, xT, col0, n):
            pb = bankf()
            for kc in range(8):
                op("pe", lambda e: e.matmul(pb.t[:, 0:n], lhsT=xT.t[:, kc, :], rhs=W.t[:, kc, col0:col0 + n],
                                            start=(kc == 0), stop=(kc == 7)), reads=[xT.b, W.b], writes=[pb.b])
            return pb

        def proj_fm(bankf, W, xT, col0, nch, m=128):
            pb = bankf()
            for ch in range(nch):
                for kc in range(8):
                    op("pe", lambda e: e.matmul(pb.t[0:m, ch * 128:(ch + 1) * 128],
                                                lhsT=W.t[:, kc, col0 + ch * m:col0 + (ch + 1) * m], rhs=xT.t[:, kc, :],
                                                start=(kc == 0), stop=(kc == 7)), reads=[xT.b, W.b], writes=[pb.b])
            return pb

        def front(fwd, bankf, W, xsl, xb, xT, lg, dec):
            r0 = 0 if fwd else 32
            mFM, mTM = (mU, mLs) if fwd else (mL, mUs)
            for c in range(8):
                op("pe", lambda e: e.transpose(out=pT.t[:, c, :], in_=xb.t[:, c * 128:(c + 1) * 128], identity=ident.t[:]),
                   reads=[xb.b, ident.b], writes=[pT.b])
            op("act", lambda e: e.copy(out=xT.t[:], in_=pT.t[:]), reads=[pT.b], writes=[xT.b])
            yield
            pg = proj_fm(bankf, W, xT, GD0, 1, m=64)
            yield
            op("act", lambda e: e.copy(out=gda.t[r0:r0 + 16, :], in_=pg.t[r0:r0 + 16, 0:128]), reads=[pg.b], writes=[gda.b])
            yield
            pl = bankf()
            op("pe", lambda e: e.matmul(pl.t[:, :], lhsT=gda.t[r0:r0 + 32, :], rhs=wup.t[r0:r0 + 32, :], start=True, stop=True),
               reads=[gda.b, wup.b], writes=[pl.b])
            yield
            op("act", lambda e: e.activation(out=lg.t[:], in_=pl.t[:], func=AF.Exp, scale=-1.0), reads=[pl.b], writes=[lg.b])
            op("act", lambda e: e.activation(out=lg.t[:], in_=lg.t[:], func=AF.Ln, bias=1.0), reads=[lg.b], writes=[lg.b])
            yield
            pc = bankf()
            for h in range(4):
                hs = slice(h * 128, (h + 1) * 128)
                op("pe", lambda e: e.matmul(pc.t[:, hs], lhsT=lg.t[:, hs], rhs=mFM.t[:], start=True, stop=True),
                   reads=[lg.b, mFM.b], writes=[pc.b])
            yield
            pp = bankf()
            op("pe", lambda e: e.matmul(pp.t[:, :], lhsT=mTM.t[:], rhs=lg.t[:], start=True, stop=True),
               reads=[lg.b, mTM.b], writes=[pp.b])
            yield
            ec, enc, epre = dec
            op("act", lambda e: e.activation(out=ec.t[:], in_=pc.t[:], func=AF.Exp), reads=[pc.b], writes=[ec.b])
            op("act", lambda e: e.activation(out=enc.t[:], in_=pc.t[:], func=AF.Exp, scale=-1.0), reads=[pc.b], writes=[enc.b])
            yield
            op("act", lambda e: e.activation(out=epre.t[:], in_=pp.t[:], func=AF.Exp), reads=[pp.b], writes=[epre.b])
            yield

        def gla_main(fwd, bankf, W, xT, dec, g, o_evac, raw=None, pre=None):
            ec, enc, epre = dec
            m01, col = (m01U, 127) if fwd else (m01Ls, 0)
            qt, kt, kh, sT = g["qt"], g["kt"], g["kh"], g["sT"]
            if pre is None:
                rq, rk, rkt, v = raw
                for (c0, fm, dst_) in ((QA0, True, rq), (KA0, True, rk), (VT0, False, v)):
                    pb = proj_fm(bankf, W, xT, c0, 4) if fm else proj_tm(bankf, W, xT, c0, 512)
                    op("act", lambda e: e.copy(out=dst_.t[:], in_=pb.t[:]), reads=[pb.b], writes=[dst_.b])
                    yield
                for h in range(4):
                    op("pe", lambda e: e.transpose(out=pT.t[:, h, :], in_=rk.t[:, h * 128:(h + 1) * 128], identity=ident.t[:]),
                       reads=[rk.b, ident.b], writes=[pT.b])
                op("act", lambda e: e.copy(out=rkt.t[:].rearrange("p (h d) -> p h d", h=4), in_=pT.t[:, 0:4, :]), reads=[pT.b], writes=[rkt.b])
                yield
            else:
                rq, rk, rkt, v = pre
            op("dve", lambda e: e.scalar_tensor_tensor(out=qt.t[:], in0=rq.t[:], scalar=128 ** -0.5, in1=ec.t[:], op0=ALU.mult, op1=ALU.mult),
               reads=[rq.b, ec.b], writes=[qt.b])
            op("dve", lambda e: e.tensor_tensor(out=kt.t[:], in0=rk.t[:], in1=enc.t[:], op=ALU.mult), reads=[rk.b, enc.b], writes=[kt.b])
            yield
            op("dve", lambda e: e.tensor_tensor(out=kh.t[:], in0=rkt.t[:], in1=epre.t[:], op=ALU.mult), reads=[rkt.b, epre.b], writes=[kh.b])
            yield
            psc = bankf()
            for h in range(4):
                hs = slice(h * 128, (h + 1) * 128)
                op("pe", lambda e: e.matmul(psc.t[:, hs], lhsT=kt.t[:, hs], rhs=qt.t[:, hs], start=True, stop=True),
                   reads=[kt.b, qt.b], writes=[psc.b])
            op("dve", lambda e: e.tensor_tensor(out=sT.t[:], in0=psc.t[:], in1=m01.t[:], op=ALU.mult), reads=[psc.b, m01.b], writes=[sT.b])
            yield
            po = bankf()
            for h in range(4):
                hs = slice(h * 128, (h + 1) * 128)
                op("pe", lambda e: e.matmul(po.t[:, hs], lhsT=sT.t[:, hs], rhs=v.t[:, hs], start=True, stop=False),
                   reads=[sT.b, v.b], writes=[po.b])
                op("pe", lambda e: e.matmul(po.t[:, hs], lhsT=qt.t[:, hs], rhs=Sbf.t[:, hs], start=False, stop=True),
                   reads=[qt.b, Sbf.b], writes=[po.b])
            o_evac(po)
            yield
            pu = bankf()
            for h in range(4):
                hs = slice(h * 128, (h + 1) * 128)
                op("pe", lambda e: e.matmul(pu.t[:, hs], lhsT=kh.t[:, hs], rhs=v.t[:, hs], start=True, stop=True),
                   reads=[kh.b, v.b], writes=[pu.b])
            for h in range(4):
                hs = slice(h * 128, (h + 1) * 128)
                cc = h * 128 + col
                op("dve", lambda e: e.scalar_tensor_tensor(out=S.t[:, hs], in0=S.t[:, hs], scalar=ec.t[:, cc:cc + 1], in1=pu.t[:, hs],
                                                           op0=ALU.mult, op1=ALU.add), reads=[S.b, ec.b, pu.b], writes=[S.b])
            op("pool", lambda e: e.tensor_copy(out=Sbf.t[:], in_=S.t[:]), reads=[S.b], writes=[Sbf.b])
            yield

        def reset_state(zero):
            if zero:
                op("pool", lambda e: e.memset(S.t[:], 0.0), writes=[S.b])
            else:
                op("dve", lambda e: e.tensor_scalar(out=S.t[:], in0=S.t[:], scalar1=rfl.t[:, 0:1], scalar2=None, op0=ALU.mult),
                   reads=[S.b, rfl.b], writes=[S.b])
            op("pool", lambda e: e.tensor_copy(out=Sbf.t[:], in_=S.t[:]), reads=[S.b], writes=[Sbf.b])

        def silu_from_psum(pz, ez, out_tb):
            op("act", lambda e: e.activation(out=ez.t[:], in_=pz.t[:], func=AF.Exp, scale=-1.0), reads=[pz.b], writes=[ez.b])
            op("act", lambda e: e.activation(out=ez.t[:], in_=ez.t[:], func=AF.Ln, bias=1.0), reads=[ez.b], writes=[ez.b])
            op("act", lambda e: e.activation(out=ez.t[:], in_=ez.t[:], func=AF.Exp, scale=-1.0), reads=[ez.b], writes=[ez.b])
            op("dve", lambda e: e.tensor_tensor(out=out_tb.t[:], in0=pz.t[:], in1=ez.t[:], op=ALU.mult), reads=[pz.b, ez.b], writes=[out_tb.b])

        def layer_norm(junk, stat, lng, lnb, h1):
            op("act", lambda e: e.activation(out=junk.t[:], in_=h1.t[:], func=AF.Copy, accum_out=stat.t[:, 0:1]),
               reads=[h1.b], writes=[junk.b, stat.b])
            op("act", lambda e: e.activation(out=junk.t[:], in_=h1.t[:], func=AF.Square, accum_out=stat.t[:, 1:2]),
               reads=[h1.b], writes=[junk.b, stat.b])
            op("dve", lambda e: e.tensor_scalar(out=stat.t[:, 0:2], in0=stat.t[:, 0:2], scalar1=1.0 / D, scalar2=None, op0=ALU.mult),
               reads=[stat.b], writes=[stat.b])
            op("dve", lambda e: e.tensor_tensor(out=stat.t[:, 2:3], in0=stat.t[:, 0:1], in1=stat.t[:, 0:1], op=ALU.mult),
               reads=[stat.b], writes=[stat.b])
            op("dve", lambda e: e.tensor_tensor(out=stat.t[:, 2:3], in0=stat.t[:, 1:2], in1=stat.t[:, 2:3], op=ALU.subtract),
               reads=[stat.b], writes=[stat.b])
            op("act", lambda e: e.activation(out=stat.t[:, 3:4], in_=stat.t[:, 2:3], func=AF.Ln, bias=LN_EPS), reads=[stat.b], writes=[stat.b])
            op("act", lambda e: e.activation(out=stat.t[:, 3:4], in_=stat.t[:, 3:4], func=AF.Exp, scale=-0.5), reads=[stat.b], writes=[stat.b])
            op("dve", lambda e: e.tensor_scalar(out=h1.t[:], in0=h1.t[:], scalar1=stat.t[:, 0:1], scalar2=stat.t[:, 3:4],
                                                op0=ALU.subtract, op1=ALU.mult), reads=[h1.b, stat.b], writes=[h1.b])
            op("dve", lambda e: e.tensor_tensor(out=h1.t[:], in0=h1.t[:], in1=lng.t[:], op=ALU.mult), reads=[h1.b, lng.b], writes=[h1.b])
            op("pool", lambda e: e.tensor_tensor(out=h1.t[:], in0=h1.t[:], in1=lnb.t[:], op=ALU.add), reads=[h1.b, lnb.b], writes=[h1.b])

        def init_state(src):
            op("dve", lambda e: e.tensor_copy(out=S.t[:], in_=src.t[:]), reads=[src.b], writes=[S.b])
            op("pool", lambda e: e.tensor_copy(out=Sbf.t[:], in_=S.t[:]), reads=[S.b], writes=[Sbf.b])

        def gla_bufs(st):
            g = {}
            for n in ("qt", "kt", "kh", "sT"):
                g[n] = sbt(st, "g_" + n, [128, 512], BF16)
            return g

        def run_pipe(stages):
            smin = min(lo + lag for (_, lag, lo, hi) in stages)
            smax = max(hi - 1 + lag for (_, lag, lo, hi) in stages)
            for s_ in range(smin, smax + 1):
                sched([fn(s_ - lag) for (fn, lag, lo, hi) in stages if lo <= s_ - lag < hi])

        def run_pipe2(stages, NU):
            smin = min(lag for (_, lag, _) in stages)
            smax = NU - 1 + max(lag for (_, lag, _) in stages)
            for s_ in range(smin, smax + 1):
                sched([fn(s_ - lag) for (fn, lag, pred) in stages if 0 <= s_ - lag < NU and pred(s_ - lag)])

        UE = [(J, e) for J in JOBS for e in range(J["n"] + 4)]
        UA = [(J, e) for J in JOBS for e in range(J["n"] + 2, 0, -1)]
        UP = [(J, t) for J in JOBS for t in range(J["NO"])]

        def main_e(u):
            J, e = UE[u]
            return 1 <= e < J["n"] + 3

        def own_e(u):
            J, e = UE[u]
            return 2 <= e < J["n"] + 2

        stW = ExitStack()
        stW.__enter__()
        W0 = sbt(stW, "W0", [128, 8, NC0], BF16)
        WO0 = sbt(stW, "WO0", [128, 8, D], BF16)
        with ExitStack() as st:
            stg = [sbt(st, "stgA%d" % i, [128, 8, 512], F32) for i in range(2)]
            W0e = TB(W0.t, "W0early")
            W0l = TB(W0.t, "W0late")
            load_weight(stg, W0e, w0_d, GD0, 64 + 1024, GD0)
            late_blocks = []
            for (dst_, src_, c0, n_) in ((W0l, w0_d, 0, GD0), (W0l, w0_d, ZA0, NC0 - ZA0), (WO0, wo0_d, 0, D)):
                c_ = 0
                while c_ < n_:
                    late_blocks.append((dst_, src_, c0 + c_, min(512, n_ - c_)))
                    c_ += 512
            WA = W0e
            SinB = [sbt(st, "SinB%d" % i, [128, 512], F32) for i in range(2)]
            xs = [sbt(st, "xsA%d" % i, [128, D], F32) for i in range(3)]
            xb = [sbt(st, "xbA%d" % i, [128, D], BF16) for i in range(2)]
            xT = [sbt(st, "xTA%d" % i, [128, 8, 128], BF16) for i in range(2)]

            with ExitStack() as st2:
                lgf = sbt(st2, "lgf", [128, 512], F32)
                lgb = sbt(st2, "lgb", [128, 512], F32)
                ebs = [sbt(st2, "ebs%d" % i, [128, 512], F32) for i in range(2)]
                epr = [sbt(st2, "epr%d" % i, [128, 512], F32) for i in range(2)]
                Dfb = [sbt(st2, "Dfb%d" % i, [128, 8], F32) for i in range(2)]
                khf = sbt(st2, "khf", [128, 512], BF16)
                khb = sbt(st2, "khb", [128, 512], BF16)
                vpp = sbt(st2, "vpp", [128, 512], BF16)
                bankA, bankB = Rot(banks[0:4]), Rot(banks[4:7])
                if True:
                    def p_load(u):
                        J, t = UP[u]
                        xo_d = J["xo_d"]
                        dma("sp", xs[u % 3].t[:], xo_d[t * 128:(t + 1) * 128, :], dx[u % 3], writes=[xs[u % 3].b])
                        op("pool", lambda e: e.tensor_copy(out=xb[u % 2].t[:], in_=xs[u % 3].t[:]), reads=[xs[u % 3].b], writes=[xb[u % 2].b])
                        yield

                    def p_front(u):
                        J, t = UP[u]
                        woff = J["woff"]
                        xb_, xT_ = xb[u % 2], xT[u % 2]
                        wf = wts.t[:, woff + 2 * t:woff + 2 * t + 1]
                        wb = wts.t[:, woff + 2 * t + 1:woff + 2 * t + 2]
                        for c in range(8):
                            op("pe", lambda e: e.transpose(out=pT.t[:, c, :], in_=xb_.t[:, c * 128:(c + 1) * 128], identity=ident.t[:]),
                               reads=[xb_.b, ident.b], writes=[pT.b])
                        op("act", lambda e: e.copy(out=xT_.t[:], in_=pT.t[:]), reads=[pT.b], writes=[xT_.b])
                        yield
                        pg = proj_fm(bankA, WA, xT_, GD0, 1, m=64)
                        op("act", lambda e: e.copy(out=gda.t[0:16, :], in_=pg.t[0:16, 0:128]), reads=[pg.b], writes=[gda.b])
                        op("act", lambda e: e.copy(out=gda.t[32:48, :], in_=pg.t[32:48, 0:128]), reads=[pg.b], writes=[gda.b])
                        yield
                        plf = bankA()
                        op("pe", lambda e: e.matmul(plf.t[:, :], lhsT=gda.t[0:32, :], rhs=wup.t[0:32, :], start=True, stop=True),
                           reads=[gda.b, wup.b], writes=[plf.b])
                        plb = bankA()
                        op("pe", lambda e: e.matmul(plb.t[:, :], lhsT=gda.t[32:64, :], rhs=wup.t[32:64, :], start=True, stop=True),
                           reads=[gda.b, wup.b], writes=[plb.b])
                        for (pl, lg_) in ((plf, lgf), (plb, lgb)):
                            op("act", lambda e: e.activation(out=lg_.t[:], in_=pl.t[:], func=AF.Exp, scale=-1.0), reads=[pl.b], writes=[lg_.b])
                            op("act", lambda e: e.activation(out=lg_.t[:], in_=lg_.t[:], func=AF.Ln, bias=1.0), reads=[lg_.b], writes=[lg_.b])
                        yield
                        ppf = bankA()
                        op("pe", lambda e: e.matmul(ppf.t[:, :], lhsT=mLs.t[:], rhs=lgf.t[:], start=True, stop=True),
                           reads=[lgf.b, mLs.b], writes=[ppf.b])
                        op("act", lambda e: e.activation(out=ebs[u % 2].t[:], in_=ppf.t[:], func=AF.Exp), reads=[ppf.b], writes=[ebs[u % 2].b])
                        yield
                        ppb = bankA()
                        op("pe", lambda e: e.matmul(ppb.t[:, :], lhsT=mUs.t[:], rhs=lgb.t[:], start=True, stop=True),
                           reads=[lgb.b, mUs.b], writes=[ppb.b])
                        op("act", lambda e: e.activation(out=epr[u % 2].t[:], in_=ppb.t[:], func=AF.Exp), reads=[ppb.b], writes=[epr[u % 2].b])
                        yield
                        ptot = bankA()
                        for di, lg_ in enumerate((lgf, lgb)):
                            for h in range(4):
                                cc = di * 4 + h
                                op("pe", lambda e: e.matmul(ptot.t[:, cc:cc + 1], lhsT=lg_.t[:, h * 128:(h + 1) * 128], rhs=m16c.t[:, 0:1],
                                                            start=True, stop=True), reads=[lg_.b, m16c.b], writes=[ptot.b])
                        dd = Dfb[u % 2]
                        op("act", lambda e: e.activation(out=dd.t[:, 0:4], in_=ptot.t[:, 0:4], func=AF.Exp, scale=wf), reads=[ptot.b, wts.b], writes=[dd.b])
                        op("act", lambda e: e.activation(out=dd.t[:, 4:8], in_=ptot.t[:, 4:8], func=AF.Exp, scale=wb), reads=[ptot.b, wts.b], writes=[dd.b])
                        yield

                    def p_upd(u):
                        J, t = UP[u]
                        woff = J["woff"]
                        Sf, Ab = SinF[J["j"]], SinB[J["j"]]
                        if t == 0:
                            op("pool", lambda e: e.memset(Sf.t[:], 0.0), writes=[Sf.b])
                            op("pool", lambda e: e.memset(Ab.t[:], 0.0), writes=[Ab.b])
                            op("pool", lambda e: e.memset(Pdec.t[:], 1.0), writes=[Pdec.b])
                        xT_ = xT[u % 2]
                        wf = wts.t[:, woff + 2 * t:woff + 2 * t + 1]
                        wb = wts.t[:, woff + 2 * t + 1:woff + 2 * t + 2]
                        dd = Dfb[u % 2]
                        pkt = proj_tm(bankB, WA, xT_, KT0, 512)
                        op("dve", lambda e: e.scalar_tensor_tensor(out=khf.t[:], in0=pkt.t[:], scalar=wf, in1=ebs[u % 2].t[:], op0=ALU.mult, op1=ALU.mult),
                           reads=[pkt.b, wts.b, ebs[u % 2].b], writes=[khf.b])
                        op("dve", lambda e: e.scalar_tensor_tensor(out=khb.t[:], in0=pkt.t[:], scalar=wb, in1=epr[u % 2].t[:], op0=ALU.mult, op1=ALU.mult),
                           reads=[pkt.b, wts.b, epr[u % 2].b], writes=[khb.b])
                        yield
                        pvt = proj_tm(bankB, WA, xT_, VT0, 512)
                        op("act", lambda e: e.copy(out=vpp.t[:], in_=pvt.t[:]), reads=[pvt.b], writes=[vpp.b])
                        yield
                        puf = bankB()
                        for h in range(4):
                            hs = slice(h * 128, (h + 1) * 128)
                            op("pe", lambda e: e.matmul(puf.t[:, hs], lhsT=khf.t[:, hs], rhs=vpp.t[:, hs], start=True, stop=True),
                               reads=[khf.b, vpp.b], writes=[puf.b])
                        for h in range(4):
                            hs = slice(h * 128, (h + 1) * 128)
                            op("dve", lambda e: e.scalar_tensor_tensor(out=Sf.t[:, hs], in0=Sf.t[:, hs], scalar=dd.t[:, h:h + 1], in1=puf.t[:, hs],
                                                                       op0=ALU.mult, op1=ALU.add), reads=[Sf.b, dd.b, puf.b], writes=[Sf.b])
                        yield
                        pub = bankB()
                        for h in range(4):
                            hs = slice(h * 128, (h + 1) * 128)
                            op("pe", lambda e: e.matmul(pub.t[:, hs], lhsT=khb.t[:, hs], rhs=vpp.t[:, hs], start=True, stop=True),
                               reads=[khb.b, vpp.b], writes=[pub.b])
                        for h in range(4):
                            hs = slice(h * 128, (h + 1) * 128)
                            op("dve", lambda e: e.scalar_tensor_tensor(out=Ab.t[:, hs], in0=pub.t[:, hs], scalar=Pdec.t[:, h:h + 1], in1=Ab.t[:, hs],
                                                                       op0=ALU.mult, op1=ALU.add), reads=[Ab.b, Pdec.b, pub.b], writes=[Ab.b])
                        op("dve", lambda e: e.tensor_tensor(out=Pdec.t[:], in0=Pdec.t[:], in1=dd.t[:, 4:8], op=ALU.mult), reads=[Pdec.b, dd.b], writes=[Pdec.b])
                        yield

                    def w_late(u):
                        dst_, src_, c0, n_ = late_blocks[u]
                        load_weight(stg, dst_, src_, c0, n_, c0)
                        yield

                    assert len(late_blocks) <= len(UP)
                    run_pipe2([(p_load, -2, lambda u: True), (p_front, -1, lambda u: True), (p_upd, 0, lambda u: True),
                               (w_late, 0, lambda u: u < len(late_blocks))], len(UP))
                fw.barrier()

            WA = W0
            lg = sbt(st, "lgA", [128, 512], F32)
            decs = [[sbt(st, "decA%d_%d" % (i, j), [128, 512], F32) for j in range(3)] for i in range(2)]
            obs = [sbt(st, "obsA%d" % i, [128, 512], F32) for i in range(2)]
            raws = [[sbt(st, "rawA%d_%d" % (i, k), [128, 512], BF16) for k in range(4)] for i in range(2)]
            g = gla_bufs(st)
            bankA, bankB = Rot(banks[0:3]), Rot(banks[3:7])
            if True:
                def a_load(i):
                    J, t = UA[i]
                    x_d = J["x_d"]
                    dma("sp", xs[i % 3].t[:], x_d[t * 128:(t + 1) * 128, :], dx[i % 3], writes=[xs[i % 3].b])
                    op("pool", lambda e: e.tensor_copy(out=xb[i % 2].t[:], in_=xs[i % 3].t[:]), reads=[xs[i % 3].b], writes=[xb[i % 2].b])
                    yield

                def a_front(i):
                    yield from front(False, bankA, WA, xs[i % 3], xb[i % 2], xT[i % 2], lg, decs[i % 2])

                def a_main(i):
                    J, t = UA[i]
                    ob_d = J["ob_d"]
                    if t == J["n"] + 2:
                        init_state(SinB[J["j"]])
                    ob = obs[i % 2]

                    def o_evac(po):
                        op("act", lambda e: e.copy(out=ob.t[:], in_=po.t[:]), reads=[po.b], writes=[ob.b])
                        dma("sp", ob_d[t * 128:(t + 1) * 128, :], ob.t[:], dob[i % 2], reads=[ob.b])
                    rw = raws[i % 2]
                    yield from gla_main(False, bankB, WA, xT[i % 2], decs[i % 2], g, o_evac, raw=rw)
                    qkv_d = J["qkv_d"]
                    for k in range(4):
                        dma("sp", qkv_d[t * 128:(t + 1) * 128, k, :], rw[k].t[:], dqs[i % 2], reads=[rw[k].b])
                    for k in range(4):
                        rw[k].b.r[dqs[i % 2]] = dqs[i % 2].cnt
                    yield

                run_pipe2([(a_load, -2, lambda u: True), (a_front, -1, lambda u: True), (a_main, 0, lambda u: True)], len(UA))
            fw.barrier()

        with ExitStack() as st:
            WB = W0
            WO = WO0
            biasT = sbt(st, "biasT", [128, 8, 3, 128], F32)
            esink = sbt(st, "esink", [128, 8], F32)
            gn4 = sbt(st, "gn4", [128, 512], F32)
            lng = sbt(st, "lng", [128, D], F32)
            lnb = sbt(st, "lnb", [128, D], F32)
            dma_c(biasT.t[:].rearrange("p h b q -> p (h b q)"), bias_d[:, :], [biasT.b])
            dma_c(esink.t[:], sink_d[:, :], [esink.b])
            dma_c(gn4.t[:], gn_d[:, :], [gn4.b])
            dma_c(lng.t[:], lng_d[:, 0:D], [lng.b])
            dma_c(lnb.t[:], lnb_d[:, 0:D], [lnb.b])
            op("act", lambda e: e.activation(out=esink.t[:], in_=esink.t[:], func=AF.Exp), reads=[esink.b], writes=[esink.b])
            op("pool", lambda e: e.affine_select(out=biasT.t[:, :, 0, :], in_=biasT.t[:, :, 0, :], pattern=[[0, 8], [-1, 128]],
                                                 compare_op=ALU.is_ge, fill=NEG, base=0, channel_multiplier=1),
               reads=[biasT.b], writes=[biasT.b])
            op("pool", lambda e: e.affine_select(out=biasT.t[:, :, 2, :], in_=biasT.t[:, :, 2, :], pattern=[[0, 8], [1, 128]],
                                                 compare_op=ALU.is_ge, fill=NEG, base=0, channel_multiplier=-1),
               reads=[biasT.b], writes=[biasT.b])

            xs = [sbt(st, "xsB%d" % i, [128, D], F32) for i in range(5)]
            xb = [sbt(st, "xbB%d" % i, [128, D], BF16) for i in range(2)]
            lg = sbt(st, "lgB", [128, 512], F32)
            xT = [sbt(st, "xTB%d" % i, [128, 8, 128], BF16) for i in range(2)]
            decs = [[sbt(st, "decB%d_%d" % (i, j), [128, 512], F32) for j in range(3)] for i in range(2)]
            obl = [sbt(st, "oblB%d" % i, [128, 512], F32) for i in range(2)]
            raws = [[sbt(st, "rawB%d_%d" % (i, k), [128, 512], BF16) for k in range(4)] for i in range(2)]
            g = gla_bufs(st)
            o_sb = sbt(st, "o_sb", [128, 512], F32)
            ss = sbt(st, "ssB", [128, 8], F32)
            ez = sbt(st, "ezB", [128, 512], F32)
            gz = sbt(st, "gzB", [128, 512], F32)
            szb = [sbt(st, "szbB%d" % i, [128, 512], F32) for i in range(2)]
            ybuf = [sbt(st, "yB%d" % i, [128, D], BF16) for i in range(3)]
            qTB = [sbt(st, "qTB%d" % i, [128, 512], BF16) for i in range(2)]
            kTB = [sbt(st, "kTB%d" % i, [128, 128], BF16) for i in range(4)]
            vB = [sbt(st, "vB%d" % i, [128, 2, 65], BF16) for i in range(4)]
            sc = [sbt(st, "scB%d" % i, [128, 384], F32) for i in range(2)]
            pTs = [sbt(st, "pTsB%d" % i, [128, 384], BF16) for i in range(2)]
            den = sbt(st, "denB", [128, 8], F32)
            yT = sbt(st, "yTB", [128, 8, 128], BF16)
            h1s = [sbt(st, "h1B%d" % i, [128, D], F32) for i in range(2)]
            junk = sbt(st, "junkB", [128, D], BF16)
            sq = junk
            stat = sbt(st, "statB", [128, 4], F32)
            for v3 in vB:
                op("pool", lambda e: e.memset(v3.t[:], 1.0), writes=[v3.b])
            bankA, bankB, bankC = Rot(banks[0:2]), Rot(banks[2:4]), Rot(banks[5:7])

            if True:
                def b_load(u):
                    J, e_ = UE[u]
                    x_d, t = J["x_d"], u
                    dma("sp", xs[t % 5].t[:], x_d[e_ * 128:(e_ + 1) * 128, :], dx[t % 5], writes=[xs[t % 5].b])
                    op("pool", lambda e: e.tensor_copy(out=xb[t % 2].t[:], in_=xs[t % 5].t[:]), reads=[xs[t % 5].b], writes=[xb[t % 2].b])
                    yield

                def b_loadob(t):
                    J, e_ = UE[t]
                    ob_d = J["ob_d"]
                    dma("sp", obl[t % 2].t[:], ob_d[e_ * 128:(e_ + 1) * 128, :], dob[t % 2], writes=[obl[t % 2].b])
                    qkv_d, rw = J["qkv_d"], raws[t % 2]
                    for k in range(4):
                        dma("sp", rw[k].t[:], qkv_d[e_ * 128:(e_ + 1) * 128, k, :], dql[t % 2], writes=[rw[k].b])
                    for k in range(4):
                        rw[k].b.w = (dql[t % 2], dql[t % 2].cnt)
                    yield

                def b_front(t):
                    yield from front(True, bankA, WB, xs[t % 5], xb[t % 2], xT[t % 2], lg, decs[t % 2])
                    pkb = proj_fm(bankA, WB, xT[t % 2], KB0, 1)
                    op("act", lambda e: e.copy(out=kTB[t % 4].t[:], in_=pkb.t[:, 0:128]), reads=[pkb.b], writes=[kTB[t % 4].b])
                    yield
                    pvb = proj_tm(bankA, WB, xT[t % 2], VB0, 128)
                    vs = vB[t % 4]
                    op("act", lambda e: e.copy(out=vs.t[:, :, 0:64], in_=pvb.t[:, 0:128].rearrange("p (a b) -> p a b", a=2)),
                       reads=[pvb.b], writes=[vs.b])
                    yield

                def b_main(t):
                    J, e_ = UE[t]
                    par = t % 2
                    if e_ == 1:
                        init_state(SinF[J["j"]])
                    y = ybuf[t % 3]

                    def o_evac(po):
                        op("dve", lambda e: e.tensor_tensor(out=o_sb.t[:], in0=po.t[:], in1=obl[par].t[:], op=ALU.add),
                           reads=[po.b, obl[par].b], writes=[o_sb.b])
                    yield from gla_main(True, bankB, WB, xT[par], decs[par], g, o_evac, pre=raws[par])
                    for h in range(4):
                        hs = slice(h * 128, (h + 1) * 128)
                        op("act", lambda e: e.activation(out=sq.t[:, hs], in_=o_sb.t[:, hs], func=AF.Square, accum_out=ss.t[:, h:h + 1]),
                           reads=[o_sb.b], writes=[sq.b, ss.b])
                    op("dve", lambda e: e.tensor_scalar(out=ss.t[:, 0:4], in0=ss.t[:, 0:4], scalar1=1.0 / 128, scalar2=None, op0=ALU.mult),
                       reads=[ss.b], writes=[ss.b])
                    op("act", lambda e: e.activation(out=ss.t[:, 0:4], in_=ss.t[:, 0:4], func=AF.Ln, bias=NORM_EPS), reads=[ss.b], writes=[ss.b])
                    op("act", lambda e: e.activation(out=ss.t[:, 0:4], in_=ss.t[:, 0:4], func=AF.Exp, scale=-0.5), reads=[ss.b], writes=[ss.b])
                    yield
                    pz = proj_tm(bankB, WB, xT[par], ZA0, 512)
                    silu_from_psum(pz, ez, gz)
                    op("pool", lambda e: e.tensor_tensor(out=gz.t[:], in0=gz.t[:], in1=gn4.t[:], op=ALU.mult), reads=[gz.b, gn4.b], writes=[gz.b])
                    yield
                    for h in range(4):
                        hs = slice(h * 128, (h + 1) * 128)
                        op("dve", lambda e: e.scalar_tensor_tensor(out=y.t[:, hs], in0=o_sb.t[:, hs], scalar=ss.t[:, h:h + 1], in1=gz.t[:, hs],
                                                                   op0=ALU.mult, op1=ALU.mult), reads=[o_sb.b, ss.b, gz.b], writes=[y.b])
                    yield
                    pz2 = proj_tm(bankB, WB, xT[par], ZB0, 512)
                    silu_from_psum(pz2, ez, szb[par])
                    yield
                    pqb = proj_fm(bankB, WB, xT[par], QB0, 4)
                    op("act", lambda e: e.activation(out=qTB[par].t[:], in_=pqb.t[:], func=AF.Copy, scale=0.125), reads=[pqb.b], writes=[qTB[par].b])
                    yield

                def b_back(t):
                    J, e_ = UE[t]
                    n, jj = J["n"], J["j"]
                    par = t % 2
                    y = ybuf[t % 3]
                    blks = [(0, t - 1, 2 * jj if e_ == 2 else None), (1, t, None), (2, t + 1, 2 * jj + 1 if e_ == n + 1 else None)]
                    b0, b1 = 0, 3

                    def scores(h):
                        kv, c = h // 4, h % 4
                        rs = slice(kv * 64, (kv + 1) * 64)
                        pss = banks[5 + h % 2]
                        for (bi, tt, _) in blks:
                            op("pe", lambda e: e.matmul(pss.t[:, bi * 128:(bi + 1) * 128], lhsT=kTB[tt % 4].t[rs, :],
                                                        rhs=qTB[par].t[rs, c * 128:(c + 1) * 128], start=True, stop=True),
                               reads=[kTB[tt % 4].b, qTB[par].b], writes=[pss.b])
                        s_, p_ = sc[h % 2], pTs[h % 2]
                        op("dve", lambda e: e.tensor_tensor(out=s_.t[:, b0 * 128:b1 * 128], in0=pss.t[:, b0 * 128:b1 * 128],
                                                            in1=biasT.t[:, h, b0:b1, :].rearrange("p b q -> p (b q)"), op=ALU.add),
                           reads=[pss.b, biasT.b], writes=[s_.b])
                        for (bi, tt, fc) in blks:
                            if fc is not None:
                                op("dve", lambda e: e.tensor_scalar(out=s_.t[:, bi * 128:(bi + 1) * 128], in0=s_.t[:, bi * 128:(bi + 1) * 128],
                                                                    scalar1=negm.t[:, fc:fc + 1], scalar2=None, op0=ALU.add),
                                   reads=[s_.b, negm.b], writes=[s_.b])
                        op("act", lambda e: e.activation(out=p_.t[:, b0 * 128:b1 * 128], in_=s_.t[:, b0 * 128:b1 * 128], func=AF.Exp),
                           reads=[s_.b], writes=[p_.b])

                    for half in range(2):
                        pv = banks[4]
                        for hh in range(4):
                            h = half * 4 + hh
                            kv = h // 4
                            if h == 0:
                                scores(0)
                            if h + 1 < 8:
                                scores(h + 1)
                            yield
                            p_ = pTs[h % 2]
                            hc = hh * 65
                            for n_, (bi, tt, _) in enumerate(blks):
                                op("pe", lambda e: e.matmul(pv.t[:, hc:hc + 65], lhsT=p_.t[:, bi * 128:(bi + 1) * 128], rhs=vB[tt % 4].t[:, kv, :],
                                                            start=(n_ == 0), stop=(n_ == len(blks) - 1)),
                                   reads=[p_.b, vB[tt % 4].b], writes=[pv.b])
                        pvv = pv.t[:, 0:260].rearrange("p (h c) -> p h c", c=65)
                        op("dve", lambda e: e.tensor_tensor(out=den.t[:, half * 4:half * 4 + 4], in0=pvv[:, :, 64],
                                                            in1=esink.t[:, half * 4:half * 4 + 4], op=ALU.add),
                           reads=[pv.b, esink.b], writes=[den.b])
                        op("dve", lambda e: e.reciprocal(out=den.t[:, half * 4:half * 4 + 4], in_=den.t[:, half * 4:half * 4 + 4]),
                           reads=[den.b], writes=[den.b])
                        for hh in range(4):
                            h = half * 4 + hh
                            op("dve", lambda e: e.scalar_tensor_tensor(out=y.t[:, 512 + h * 64:512 + (h + 1) * 64], in0=pvv[:, hh, 0:64],
                                                                       scalar=den.t[:, h:h + 1], in1=szb[par].t[:, h * 64:(h + 1) * 64],
                                                                       op0=ALU.mult, op1=ALU.mult),
                               reads=[pv.b, den.b, szb[par].b], writes=[y.b])
                        yield

                def b_out(t):
                    J, e_ = UE[t]
                    x1_d = J["x1_d"]
                    par = t % 2
                    y = ybuf[t % 3]
                    for c in range(8):
                        op("pe", lambda e: e.transpose(out=pT.t[:, c, :], in_=y.t[:, c * 128:(c + 1) * 128], identity=ident.t[:]),
                           reads=[y.b, ident.b], writes=[pT.b])
                    op("act", lambda e: e.copy(out=yT.t[:], in_=pT.t[:]), reads=[pT.b], writes=[yT.b])
                    yield
                    h1 = h1s[par]
                    for nb in range(2):
                        pb = bankC()
                        for kc in range(8):
                            op("pe", lambda e: e.matmul(pb.t[:, :], lhsT=yT.t[:, kc, :], rhs=WO.t[:, kc, nb * 512:(nb + 1) * 512],
                                                        start=(kc == 0), stop=(kc == 7)), reads=[yT.b, WO.b], writes=[pb.b])
                        op("dve", lambda e: e.scalar_tensor_tensor(out=h1.t[:, nb * 512:(nb + 1) * 512], in0=xs[t % 5].t[:, nb * 512:(nb + 1) * 512],
                                                                   scalar=ALPHA, in1=pb.t[:, :], op0=ALU.mult, op1=ALU.add),
                           reads=[xs[t % 5].b, pb.b], writes=[h1.b])
                        yield
                    layer_norm(junk, stat, lng, lnb, h1)
                    dma("sp", x1_d[e_ * 128:(e_ + 1) * 128, :], h1.t[:], dst[par], reads=[h1.b])
                    yield

                run_pipe2([(b_load, -2, lambda u: True), (b_loadob, -1, main_e), (b_front, -1, lambda u: True),
                           (b_main, 0, main_e), (b_back, 1, main_e), (b_out, 2, main_e)], len(UE))
            fw.barrier()

        stW.close()
        with ExitStack() as st:
            W1 = sbt(st, "W1", [128, 8, 4 * D], BF16)
            WO = sbt(st, "WO1", [128, 8, D], BF16)
            with ExitStack() as st2:
                stg = [sbt(st2, "stgC%d" % i, [128, 8, 512], F32) for i in range(2)]
                load_weight(stg, W1, w1_d, 0, 4 * D, 0)
                load_weight(stg, WO, wo1_d, 0, D, 0)
                fw.barrier()
            lng = sbt(st, "lngC", [128, D], F32)
            lnb = sbt(st, "lnbC", [128, D], F32)
            cw = sbt(st, "cw", [128, 8, 3], F32)
            dma_c(lng.t[:], lng_d[:, D:2 * D], [lng.b])
            dma_c(lnb.t[:], lnb_d[:, D:2 * D], [lnb.b])
            dma_c(cw.t[:].rearrange("p c k -> p (c k)"), cw_d[:, :], [cw.b])
            xs = [sbt(st, "xsC%d" % i, [128, D], F32) for i in range(6)]
            xb = [sbt(st, "xbC%d" % i, [128, D], BF16) for i in range(2)]
            xT = [sbt(st, "xTC%d" % i, [128, 8, 128], BF16) for i in range(2)]
            TT = [sbt(st, "TT%d" % i, [128, 8, 130], F32) for i in range(3)]
            ub = [sbt(st, "ub%d" % i, [128, D], F32) for i in range(3)]
            hsb = sbt(st, "hsb", [128, 512], F32)
            ez = sbt(st, "ezC", [128, 512], F32)
            cv = sbt(st, "cvC", [128, D], F32)
            gTs = [sbt(st, "gTC%d" % i, [128, D], BF16) for i in range(2)]
            h1s = [sbt(st, "h1C%d" % i, [128, D], F32) for i in range(2)]
            junk = sbt(st, "junkC", [128, D], BF16)
            stat = sbt(st, "statC", [128, 4], F32)
            for T_ in TT:
                op("pool", lambda e: e.memset(T_.t[:], 0.0), writes=[T_.b])
            cwe = [sbt(st, "cwe%d" % k, [128, 8, 128], F32) for k in range(3)]
            ctmp = sbt(st, "ctmp", [128, D], F32)
            for k in range(3):
                op("pool", lambda e: e.memset(cwe[k].t[:], 1.0), writes=[cwe[k].b])
                for c in range(8):
                    op("dve", lambda e: e.tensor_scalar(out=cwe[k].t[:, c, :], in0=cwe[k].t[:, c, :], scalar1=cw.t[:, c, k:k + 1], scalar2=None,
                                                        op0=ALU.mult), reads=[cwe[k].b, cw.b], writes=[cwe[k].b])
            bankB, bankC = Rot(banks[0:5]), Rot(banks[5:7])

            if True:
                def c_load(t):
                    J, e_ = UE[t]
                    x1_d = J["x1_d"]
                    dma("sp", xs[t % 6].t[:], x1_d[e_ * 128:(e_ + 1) * 128, :], dx[t % 6], writes=[xs[t % 6].b])
                    op("pool", lambda e: e.tensor_copy(out=xb[t % 2].t[:], in_=xs[t % 6].t[:]), reads=[xs[t % 6].b], writes=[xb[t % 2].b])
                    yield

                def c_front(t):
                    xT_ = xT[t % 2]
                    xb_ = xb[t % 2]
                    for c in range(8):
                        op("pe", lambda e: e.transpose(out=pT.t[:, c, :], in_=xb_.t[:, c * 128:(c + 1) * 128], identity=ident.t[:]),
                           reads=[xb_.b, ident.b], writes=[pT.b])
                    op("act", lambda e: e.copy(out=xT_.t[:], in_=pT.t[:]), reads=[pT.b], writes=[xT_.b])
                    yield

                def c_main(t):
                    J, e_ = UE[t]
                    n, jj = J["n"], J["j"]
                    xT_ = xT[t % 2]
                    T_ = TT[t % 3]
                    u_ = ub[t % 3]
                    op("pool", lambda e: e.memset(T_.t[:, :, 0:1], 0.0), writes=[T_.b])
                    op("pool", lambda e: e.memset(T_.t[:, :, 129:130], 0.0), writes=[T_.b])
                    for half in range(2):
                        fs = slice(half * 512, (half + 1) * 512)
                        ph = proj_fm(bankB, W1, xT_, 2 * D + half * 512, 4)
                        op("act", lambda e: e.copy(out=hsb.t[:], in_=ph.t[:]), reads=[ph.b], writes=[hsb.b])
                        yield
                        pc_ = proj_fm(bankB, W1, xT_, D + half * 512, 4)
                        op("dve", lambda e: e.tensor_tensor(out=T_.t[:, half * 4:half * 4 + 4, 1:129],
                                                            in0=pc_.t[:].rearrange("p (c q) -> p c q", c=4),
                                                            in1=hsb.t[:].rearrange("p (c q) -> p c q", c=4), op=ALU.mult),
                           reads=[pc_.b, hsb.b], writes=[T_.b])
                        yield
                        pz = proj_fm(bankB, W1, xT_, 3 * D + half * 512, 4)
                        op("act", lambda e: e.activation(out=ez.t[:], in_=pz.t[:], func=AF.Exp, scale=-1.0), reads=[pz.b], writes=[ez.b])
                        op("act", lambda e: e.activation(out=ez.t[:], in_=ez.t[:], func=AF.Ln, bias=1.0), reads=[ez.b], writes=[ez.b])
                        op("act", lambda e: e.activation(out=ez.t[:], in_=ez.t[:], func=AF.Exp, scale=-1.0), reads=[ez.b], writes=[ez.b])
                        op("dve", lambda e: e.tensor_tensor(out=ez.t[:], in0=pz.t[:], in1=ez.t[:], op=ALU.mult), reads=[pz.b, ez.b], writes=[ez.b])
                        yield
                        pbg = proj_fm(bankB, W1, xT_, half * 512, 4)
                        op("dve", lambda e: e.tensor_tensor(out=u_.t[:, fs], in0=pbg.t[:], in1=ez.t[:], op=ALU.mult),
                           reads=[pbg.b, ez.b], writes=[u_.b])
                        yield
                    if e_ > 1:
                        Tp = TT[(t - 1) % 3]
                        if e_ == 2:
                            fc = 2 * jj
                            op("dve", lambda e: e.tensor_scalar(out=T_.t[:, :, 0:1], in0=Tp.t[:, :, 128:129], scalar1=rfl.t[:, fc:fc + 1], scalar2=None,
                                                                op0=ALU.mult), reads=[Tp.b, rfl.b], writes=[T_.b])
                        elif e_ == n + 2:
                            fc = 2 * jj + 1
                            op("dve", lambda e: e.tensor_scalar(out=Tp.t[:, :, 129:130], in0=T_.t[:, :, 1:2], scalar1=rfl.t[:, fc:fc + 1], scalar2=None,
                                                                op0=ALU.mult), reads=[T_.b, rfl.b], writes=[Tp.b])
                        else:
                            op("pool", lambda e: e.tensor_copy(out=T_.t[:, :, 0:1], in_=Tp.t[:, :, 128:129]), reads=[Tp.b], writes=[T_.b])
                            op("pool", lambda e: e.tensor_copy(out=Tp.t[:, :, 129:130], in_=T_.t[:, :, 1:2]), reads=[T_.b], writes=[Tp.b])
                    yield

                def c_back(t):
                    T_ = TT[t % 3]
                    u_ = ub[t % 3]
                    cv3 = cv.t[:].rearrange("p (c q) -> p c q", c=8)
                    tm3 = ctmp.t[:].rearrange("p (c q) -> p c q", c=8)
                    op("dve", lambda e: e.tensor_tensor(out=cv3, in0=T_.t[:, :, 0:128], in1=cwe[0].t[:], op=ALU.mult),
                       reads=[T_.b, cwe[0].b], writes=[cv.b])
                    op("dve", lambda e: e.tensor_tensor(out=tm3, in0=T_.t[:, :, 1:129], in1=cwe[1].t[:], op=ALU.mult),
                       reads=[T_.b, cwe[1].b], writes=[ctmp.b])
                    yield
                    op("dve", lambda e: e.tensor_tensor(out=cv.t[:], in0=cv.t[:], in1=ctmp.t[:], op=ALU.add),
                       reads=[cv.b, ctmp.b], writes=[cv.b])
                    op("dve", lambda e: e.tensor_tensor(out=tm3, in0=T_.t[:, :, 2:130], in1=cwe[2].t[:], op=ALU.mult),
                       reads=[T_.b, cwe[2].b], writes=[ctmp.b])
                    yield
                    op("dve", lambda e: e.tensor_tensor(out=cv.t[:], in0=cv.t[:], in1=ctmp.t[:], op=ALU.add),
                       reads=[cv.b, ctmp.b], writes=[cv.b])
                    gT = gTs[t % 2]
                    op("dve", lambda e: e.tensor_tensor(out=gT.t[:], in0=cv.t[:], in1=u_.t[:], op=ALU.mult), reads=[cv.b, u_.b], writes=[gT.b])
                    yield

                def c_out(t):
                    J, e_ = UE[t]
                    y_d = J["y_d"]
                    gT = gTs[t % 2]
                    h1 = h1s[t % 2]
                    for nb in range(2):
                        pb = bankC()
                        for kc in range(8):
                            op("pe", lambda e: e.matmul(pb.t[:, :], lhsT=gT.t[:, kc * 128:(kc + 1) * 128], rhs=WO.t[:, kc, nb * 512:(nb + 1) * 512],
                                                        start=(kc == 0), stop=(kc == 7)), reads=[gT.b, WO.b], writes=[pb.b])
                        op("dve", lambda e: e.scalar_tensor_tensor(out=h1.t[:, nb * 512:(nb + 1) * 512], in0=xs[t % 6].t[:, nb * 512:(nb + 1) * 512],
                                                                   scalar=ALPHA, in1=pb.t[:, :], op0=ALU.mult, op1=ALU.add),
                           reads=[xs[t % 6].b, pb.b], writes=[h1.b])
                        yield
                    layer_norm(junk, stat, lng, lnb, h1)
                    dma("sp", y_d[(e_ - 2) * 128:(e_ - 1) * 128, :], h1.t[:], dst[t % 2], reads=[h1.b])
                    yield

                run_pipe2([(c_load, -2, main_e), (c_front, -1, main_e), (c_main, 0, main_e),
                           (c_back, 2, own_e), (c_out, 3, own_e)], len(UE))
            fw.barrier()
        for d_ in dst:
            nc.sync.wait_ge(d_.h, d_.cnt)
    return nc


def _rel_buckets():
    BLOCK, REL_BUCKETS, REL_MAX_DIST = 128, 32, 128
    i = np.arange(BLOCK)[:, None]
    j = np.arange(3 * BLOCK)[None, :]
    rel = j - BLOCK - i
    half = REL_BUCKETS // 2
    max_exact = half // 2
    n = np.abs(rel)
    large = max_exact + (np.log(np.maximum(n, 1) / max_exact) / np.log(REL_MAX_DIST / max_exact)
                         * (half - max_exact)).astype(np.int32)
    large = np.minimum(large, half - 1)
    bucket = (rel > 0).astype(np.int32) * half + np.where(n < max_exact, n, large)
    return bucket.astype(np.int32)


def pack_w0(w):
    qa, ka, va, za = w[:, 0:512], w[:, 512:1024], w[:, 1024:1536], w[:, 1536:2048]
    gd = w[:, 2048:2080]
    o = 2080
    qb, kb, vb, zb = w[:, o:o + 512], w[:, o + 512:o + 640], w[:, o + 640:o + 768], w[:, o + 768:o + 1280]
    out = np.zeros((D, NC0), np.float32)
    out[:, QA0:QA0 + 512] = qa
    out[:, KA0:KA0 + 512] = ka
    for c in range(4):
        out[:, QB0 + c * 128:QB0 + c * 128 + 64] = qb[:, c * 64:(c + 1) * 64]
        out[:, QB0 + c * 128 + 64:QB0 + (c + 1) * 128] = qb[:, (c + 4) * 64:(c + 5) * 64]
    out[:, KB0:KB0 + 128] = kb
    out[:, GD0:GD0 + 16] = gd[:, 0:16]
    out[:, GD0 + 32:GD0 + 48] = gd[:, 16:32]
    out[:, KT0:KT0 + 512] = ka
    out[:, VT0:VT0 + 512] = va
    out[:, ZA0:ZA0 + 512] = za
    out[:, ZB0:ZB0 + 512] = zb
    out[:, VB0:VB0 + 128] = vb
    return out


def make_shared(inp):
    f = lambda a: np.ascontiguousarray(np.asarray(a, dtype=np.float32))
    sh = {}
    sh["w0"] = pack_w0(f(inp["w_in_even"])[0])
    sh["wo0"] = f(inp["w_out_even"])[0]
    sh["w1"] = f(inp["w_in_odd"])[0]
    sh["wo1"] = f(inp["w_out_odd"])[0]
    wup = np.zeros((64, 512), np.float32)
    wup[0:16] = f(inp["gla_w_up_fwd"])[0]
    wup[16] = f(inp["gla_b_fwd"])[0]
    wup[32:48] = f(inp["gla_w_up_bwd"])[0]
    wup[48] = f(inp["gla_b_bwd"])[0]
    sh["wup"] = wup
    bucket = _rel_buckets()
    rb = f(inp["rel_bias"])
    bfull = rb[bucket]
    bT = bfull.reshape(128, 3, 128, 8).transpose(2, 3, 1, 0)
    sh["biasT"] = np.ascontiguousarray(bT).reshape(128, 8 * 3 * 128)
    sh["sink"] = np.ascontiguousarray(np.broadcast_to(f(inp["swa_sink"])[0][None, :], (128, 8)))
    sh["gn"] = np.ascontiguousarray(np.broadcast_to(np.tile(f(inp["gla_norm_g"])[0], 4)[None, :], (128, 512)))
    sh["lng"] = np.ascontiguousarray(np.broadcast_to(f(inp["ln_g"]).reshape(1, 2 * D), (128, 2 * D)))
    sh["lnb"] = np.ascontiguousarray(np.broadcast_to(f(inp["ln_b"]).reshape(1, 2 * D), (128, 2 * D)))
    cw = f(inp["conv_w"])[0]
    sh["convw"] = np.ascontiguousarray(cw.reshape(3, 8, 128).transpose(2, 1, 0)).reshape(128, 24)
    return sh


_NC_CACHE = {}


def _ext(xseq, a, n):
    L = xseq.shape[0]
    out = np.zeros(((n + 4) * 128, D), np.float32)
    lo, hi = (a - 2) * 128, (a + n + 2) * 128
    slo, shi = max(lo, 0), min(hi, L)
    out[slo - lo:shi - lo] = xseq[slo:shi]
    return out


def _oth(xseq, a, n, NO):
    Ls = xseq.shape[0] // 128
    pre = list(range(0, max(a - 1, 0)))
    suf = list(range(min(a + n + 1, Ls), Ls))
    out = np.zeros((NO * 128, D), np.float32)
    wts = np.zeros((NO, 2), np.float32)
    for k, t in enumerate(pre + suf):
        out[k * 128:(k + 1) * 128] = xseq[t * 128:(t + 1) * 128]
        wts[k, 0 if k < len(pre) else 1] = 1.0
    return out, wts


def run(xp, xs_, inp, n_cores=8):
    Bp, LP, _ = xp.shape
    Bs, LS, _ = xs_.shape
    assert Bp == 2 and Bs == 4 and LP % 512 == 0 and LS % 256 == 0
    nP, nS = LP // 128 // 4, LS // 128 // 2
    key = (nP, nS)
    if key not in _NC_CACHE:
        _NC_CACHE[key] = build(nP, nS)
    nc = _NC_CACHE[key]
    sh = make_shared(inp)
    in_maps = []
    for c in range(n_cores):
        m = dict(sh)
        aP, aS = (c % 4) * nP, (c % 2) * nS
        xP, xS = xp[c // 4], xs_[c // 2]
        m["xextP"] = _ext(xP, aP, nP)
        m["xextS"] = _ext(xS, aS, nS)
        m["xothP"], wP = _oth(xP, aP, nP, 3 * nP)
        m["xothS"], wS = _oth(xS, aS, nS, nS)
        wrow = np.concatenate([wP.reshape(-1), wS.reshape(-1)])[None, :]
        m["wts"] = np.ascontiguousarray(np.broadcast_to(wrow, (128, wrow.shape[1]))).astype(np.float32)
        fl = np.array([[float(aP > 0), float(aP + nP < 4 * nP), float(aS > 0), float(aS + nS < 2 * nS)]], np.float32)
        m["flags"] = np.ascontiguousarray(np.broadcast_to(fl, (128, 4)))
        in_maps.append(m)
    res = run_bass_kernel_spmd(nc, in_maps, core_ids=list(range(n_cores)))
    yp = np.zeros((Bp, LP, D), np.float32)
    ys = np.zeros((Bs, LS, D), np.float32)
    for c in range(n_cores):
        aP, aS = (c % 4) * nP, (c % 2) * nS
        yp[c // 4, aP * 128:(aP + nP) * 128] = res.results[c]["yP"]
        ys[c // 2, aS * 128:(aS + nS) * 128] = res.results[c]["yS"]
    return yp, ys


def kernel(x_prompt, x_sample, **w):
    xp = np.ascontiguousarray(np.asarray(x_prompt, dtype=np.float32))
    xs_ = np.ascontiguousarray(np.asarray(x_sample, dtype=np.float32))
    return run(xp, xs_, w)
```

```python
import numpy as np
from contextlib import ExitStack
import concourse.bass as bass
import concourse.mybir as mybir
from concourse.bass_utils import run_bass_kernel_spmd

F32 = mybir.dt.float32
BF16 = mybir.dt.bfloat16
AF = mybir.ActivationFunctionType
ALU = mybir.AluOpType

D = 1024
ALPHA = 4 ** 0.25
LN_EPS = 1e-5
NORM_EPS = 1e-6
NEG = -30000.0

QA0, KA0, QB0, KB0, GD0 = 0, 512, 1024, 1536, 1664
KT0, VT0, ZA0, ZB0, VB0 = 1728, 2240, 2752, 3264, 3776
NC0 = 3904


class Buf:
    def __init__(self, name):
        self.name = name
        self.w = None
        self.r = {}


class Sem:
    def __init__(self, nc, stack, name):
        self.h = stack.enter_context(nc.semaphore(name))
        self.cnt = 0
        self.name = name


class Eng:
    def __init__(self, name, e, sem):
        self.name, self.e, self.sem = name, e, sem
        self.waited = {}


class FW:
    def __init__(self, nc, stack):
        self.nc = nc
        self.stack = stack
        self.engs = {}
        for name, e in (("pe", nc.tensor), ("act", nc.scalar), ("dve", nc.vector), ("pool", nc.gpsimd), ("sp", nc.sync)):
            self.engs[name] = Eng(name, e, Sem(nc, stack, "s_" + name))
        self.dsems = []
        self.n_instr = 0

    def dsem(self, name):
        s = Sem(self.nc, self.stack, name)
        self.dsems.append(s)
        return s

    def _waits(self, eng, reads, writes):
        need = {}

        def add(p):
            if p is None:
                return
            s, c = p
            if need.get(s, 0) < c:
                need[s] = c
        for b in reads:
            add(b.w)
        for b in writes:
            add(b.w)
            for s, c in b.r.items():
                add((s, c))
        for s, c in need.items():
            if s is eng.sem and eng.name == "pe":
                continue
            if eng.waited.get(s, 0) >= c:
                continue
            eng.e.wait_ge(s.h, c)
            eng.waited[s] = c

    def _mark(self, tok, reads, writes):
        s, c = tok
        for b in reads:
            b.r[s] = c
        for b in writes:
            b.w = tok
            b.r = {}

    def op(self, engname, fn, reads=(), writes=()):
        eng = self.engs[engname]
        self._waits(eng, reads, writes)
        ins = fn(eng.e)
        eng.sem.cnt += 1
        ins.then_inc(eng.sem.h, 1)
        self.n_instr += 1
        tok = (eng.sem, eng.sem.cnt)
        self._mark(tok, reads, writes)
        return tok

    def dma(self, engname, out, in_, sem, reads=(), writes=()):
        eng = self.engs[engname]
        self._waits(eng, reads, writes)
        ins = eng.e.dma_start(out=out, in_=in_)
        sem.cnt += 16
        ins.then_inc(sem.h, 16)
        self.n_instr += 1
        tok = (sem, sem.cnt)
        self._mark(tok, reads, writes)
        return tok

    def barrier(self):
        sems = [e.sem for e in self.engs.values()] + self.dsems
        for eng in self.engs.values():
            for s in sems:
                if s is eng.sem or s.cnt == 0:
                    continue
                if eng.waited.get(s, 0) >= s.cnt:
                    continue
                eng.e.wait_ge(s.h, s.cnt)
                eng.waited[s] = s.cnt


class TB:
    def __init__(self, t, name):
        self.t = t
        self.b = Buf(name)


def sched(gens):
    gens = list(gens)
    while gens:
        for g_ in list(gens):
            try:
                next(g_)
            except StopIteration:
                gens.remove(g_)


def build(nP, nS):
    JOBS = [dict(name="P", n=nP, NO=3 * nP, j=0, woff=0), dict(name="S", n=nS, NO=nS, j=1, woff=2 * 3 * nP)]
    NWT = 2 * (3 * nP + nS)
    nc = bass.Bass("TRN2", target_bir_lowering=False)

    def din(name, shape):
        return nc.dram_tensor(name, shape, F32, kind="ExternalInput").ap()
    for J in JOBS:
        J["x_d"] = din("xext" + J["name"], [(J["n"] + 4) * 128, D])
        J["xo_d"] = din("xoth" + J["name"], [J["NO"] * 128, D])
        J["y_d"] = nc.dram_tensor("y" + J["name"], [J["n"] * 128, D], F32, kind="ExternalOutput").ap()
        J["ob_d"] = nc.dram_tensor("ob_scr" + J["name"], [(J["n"] + 4) * 128, 512], F32).ap()
        J["x1_d"] = nc.dram_tensor("x1_scr" + J["name"], [(J["n"] + 4) * 128, D], F32).ap()
        J["qkv_d"] = nc.dram_tensor("qkv_scr" + J["name"], [(J["n"] + 4) * 128, 4, 512], BF16).ap()
    w0_d = din("w0", [D, NC0])
    wo0_d = din("wo0", [D, D])
    w1_d = din("w1", [D, 4 * D])
    wo1_d = din("wo1", [D, D])
    wup_d = din("wup", [64, 512])
    bias_d = din("biasT", [128, 8 * 3 * 128])
    sink_d = din("sink", [128, 8])
    gn_d = din("gn", [128, 512])
    lng_d = din("lng", [128, 2 * D])
    lnb_d = din("lnb", [128, 2 * D])
    cw_d = din("convw", [128, 24])
    fl_d = din("flags", [128, 4])
    wt_d = din("wts", [128, NWT])

    with ExitStack() as gst:
        fw = FW(nc, gst)
        op, dma = fw.op, fw.dma
        uid = [0]

        def sbt(st, name, shape, dt):
            uid[0] += 1
            return TB(st.enter_context(nc.sbuf_tensor("sb%d_%s" % (uid[0], name), shape, dt)), name)

        def pst(st, name, shape, dt):
            return TB(st.enter_context(nc.psum_tensor("ps_" + name, shape, dt)), name)

        ident = sbt(gst, "ident", [128, 128], BF16)
        mU = sbt(gst, "mU", [128, 128], F32)
        mL = sbt(gst, "mL", [128, 128], F32)
        mUs = sbt(gst, "mUs", [128, 128], F32)
        mLs = sbt(gst, "mLs", [128, 128], F32)
        m01U = sbt(gst, "m01U", [128, 512], BF16)
        m01Ls = sbt(gst, "m01Ls", [128, 512], BF16)
        wup = sbt(gst, "wup", [64, 512], BF16)
        rfl = sbt(gst, "rfl", [128, 4], F32)
        negm = sbt(gst, "negm", [128, 4], F32)
        wts = sbt(gst, "wts", [128, NWT], F32)
        m16c = sbt(gst, "m16c", [128, 1], F32)
        SinF = [sbt(gst, "SinF%d" % i, [128, 512], F32) for i in range(2)]
        Pdec = sbt(gst, "Pdec", [128, 4], F32)
        S = sbt(gst, "S", [128, 512], F32)
        Sbf = sbt(gst, "Sbf", [128, 512], BF16)
        gda = sbt(gst, "gda", [64, 128], BF16)
        cn = [0]

        def dma_c(out, in_, writes):
            cn[0] += 1
            return dma("sp", out, in_, fw.dsem("dc%d" % cn[0]), writes=writes)
        dx = [fw.dsem("dx%d" % i) for i in range(6)]
        dst = [fw.dsem("dst%d" % i) for i in range(2)]
        dob = [fw.dsem("dob%d" % i) for i in range(2)]
        dw = [fw.dsem("dw%d" % i) for i in range(2)]
        dqs = [fw.dsem("dqs%d" % i) for i in range(2)]
        dql = [fw.dsem("dql%d" % i) for i in range(2)]

        def mask(tb, val, cm, base, nrep=1):
            pat = [[-cm, 128]] if nrep == 1 else [[0, nrep], [-cm, 128]]
            ap = tb.t[:] if nrep == 1 else tb.t[:].rearrange("p (a b) -> p a b", a=nrep)
            op("pool", lambda e: e.memset(tb.t[:], val), writes=[tb.b])
            op("pool", lambda e: e.affine_select(out=ap, in_=ap, pattern=pat, compare_op=ALU.is_ge, fill=0.0,
                                                 base=base, channel_multiplier=cm), reads=[tb.b], writes=[tb.b])
        mask(mU, -0.0625, -1, 0)
        mask(mL, -0.0625, 1, 0)
        mask(mUs, -0.0625, -1, -1)
        mask(mLs, -0.0625, 1, -1)
        mask(m01U, 1.0, -1, 0, 4)
        mask(m01Ls, 1.0, 1, -1, 4)
        with ExitStack() as st0:
            tmpf = sbt(st0, "tmpf", [128, 512], F32)
            op("pool", lambda e: e.memset(tmpf.t[:, 0:128], 1.0), writes=[tmpf.b])
            op("pool", lambda e: e.affine_select(out=tmpf.t[:, 0:128], in_=tmpf.t[:, 0:128], pattern=[[-1, 128]],
                                                 compare_op=ALU.is_equal, fill=0.0, base=0, channel_multiplier=1),
               reads=[tmpf.b], writes=[tmpf.b])
            op("dve", lambda e: e.tensor_copy(out=ident.t[:], in_=tmpf.t[:, 0:128]), reads=[tmpf.b], writes=[ident.b])
            dma_c(tmpf.t[0:64, :], wup_d[:, :], [tmpf.b])
            op("dve", lambda e: e.tensor_copy(out=wup.t[:], in_=tmpf.t[0:64, :]), reads=[tmpf.b], writes=[wup.b])
            dma_c(rfl.t[:], fl_d[:, :], [rfl.b])
            dma_c(wts.t[:], wt_d[:, :], [wts.b])
            op("pool", lambda e: e.memset(m16c.t[:], -0.0625), writes=[m16c.b])
            op("dve", lambda e: e.tensor_scalar(out=negm.t[:], in0=rfl.t[:], scalar1=-1.0, scalar2=-NEG, op0=ALU.add, op1=ALU.mult),
               reads=[rfl.b], writes=[negm.b])
            op("pool", lambda e: e.memset(gda.t[:], 1.0), writes=[gda.b])
            fw.barrier()

        pT = pst(gst, "pT", [128, 8, 128], BF16)
        banks = [pst(gst, "pb%d" % i, [128, 512], F32) for i in range(7)]

        class Rot:
            def __init__(self, lst):
                self.l, self.i = lst, 0

            def __call__(self):
                b = self.l[self.i % len(self.l)]
                self.i += 1
                return b

        wst_n = [0]

        def load_weight(stg, dst, src_d, col0, ncols, dcol0):
            src = src_d.rearrange("(c p) n -> p c n", p=128)
            c = 0
            while c < ncols:
                n = min(512, ncols - c)
                i = wst_n[0] % 2
                wst_n[0] += 1
                s = stg[i]
                dma("sp", s.t[:, :, 0:n], src[:, :, col0 + c:col0 + c + n], dw[i], writes=[s.b])
                eng = ("act", "dve")[wst_n[0] % 2]
                dd = dst.t[:, :, dcol0 + c:dcol0 + c + n]
                if eng == "act":
                    op("act", lambda e: e.copy(out=dd, in_=s.t[:, :, 0:n]), reads=[s.b], writes=[dst.b])
                else:
                    op(eng, lambda e: e.tensor_copy(out=dd, in_=s.t[:, :, 0:n]), reads=[s.b], writes=[dst.b])
                c += n

        def proj_tm(bankf, W, xT, col0, n):
            pb = bankf()
            for kc in range(8):
                op("pe", lambda e: e.matmul(pb.t[:, 0:n], lhsT=xT.t[:, kc, :], rhs=W.t[:, kc, col0:col0 + n],
                                            start=(kc == 0), stop=(kc == 7)), reads=[xT.b, W.b], writes=[pb.b])
            return pb

        def proj_fm(bankf, W, xT, col0, nch, m=128):
            pb = bankf()
            for ch in range(nch):
                for kc in range(8):
                    op("pe", lambda e: e.matmul(pb.t[0:m, ch * 128:(ch + 1) * 128],
                                                lhsT=W.t[:, kc, col0 + ch * m:col0 + (ch + 1) * m], rhs=xT.t[:, kc, :],
                                                start=(kc == 0), stop=(kc == 7)), reads=[xT.b, W.b], writes=[pb.b])
            return pb

        def front(fwd, bankf, W, xsl, xb, xT, lg, dec):
            r0 = 0 if fwd else 32
            mFM, mTM = (mU, mLs) if fwd else (mL, mUs)
            for c in range(8):
                op("pe", lambda e: e.transpose(out=pT.t[:, c, :], in_=xb.t[:, c * 128:(c + 1) * 128], identity=ident.t[:]),
                   reads=[xb.b, ident.b], writes=[pT.b])
            op("act", lambda e: e.copy(out=xT.t[:], in_=pT.t[:]), reads=[pT.b], writes=[xT.b])
            yield
            pg = proj_fm(bankf, W, xT, GD0, 1, m=64)
            yield
            op("act", lambda e: e.copy(out=gda.t[r0:r0 + 16, :], in_=pg.t[r0:r0 + 16, 0:128]), reads=[pg.b], writes=[gda.b])
            yield
            pl = bankf()
            op("pe", lambda e: e.matmul(pl.t[:, :], lhsT=gda.t[r0:r0 + 32, :], rhs=wup.t[r0:r0 + 32, :], start=True, stop=True),
               reads=[gda.b, wup.b], writes=[pl.b])
            yield
            op("act", lambda e: e.activation(out=lg.t[:], in_=pl.t[:], func=AF.Exp, scale=-1.0), reads=[pl.b], writes=[lg.b])
            op("act", lambda e: e.activation(out=lg.t[:], in_=lg.t[:], func=AF.Ln, bias=1.0), reads=[lg.b], writes=[lg.b])
            yield
            pc = bankf()
            for h in range(4):
                hs = slice(h * 128, (h + 1) * 128)
                op("pe", lambda e: e.matmul(pc.t[:, hs], lhsT=lg.t[:, hs], rhs=mFM.t[:], start=True, stop=True),
                   reads=[lg.b, mFM.b], writes=[pc.b])
            yield
            pp = bankf()
            op("pe", lambda e: e.matmul(pp.t[:, :], lhsT=mTM.t[:], rhs=lg.t[:], start=True, stop=True),
               reads=[lg.b, mTM.b], writes=[pp.b])
            yield
            ec, enc, epre = dec
            op("act", lambda e: e.activation(out=ec.t[:], in_=pc.t[:], func=AF.Exp), reads=[pc.b], writes=[ec.b])
            op("act", lambda e: e.activation(out=enc.t[:], in_=pc.t[:], func=AF.Exp, scale=-1.0), reads=[pc.b], writes=[enc.b])
            yield
            op("act", lambda e: e.activation(out=epre.t[:], in_=pp.t[:], func=AF.Exp), reads=[pp.b], writes=[epre.b])
            yield

        def gla_main(fwd, bankf, W, xT, dec, g, o_evac, raw=None, pre=None):
            ec, enc, epre = dec
            m01, col = (m01U, 127) if fwd else (m01Ls, 0)
            qt, kt, kh, sT = g["qt"], g["kt"], g["kh"], g["sT"]
            if pre is None:
                rq, rk, rkt, v = raw
                for (c0, fm, dst_) in ((QA0, True, rq), (KA0, True, rk), (VT0, False, v)):
                    pb = proj_fm(bankf, W, xT, c0, 4) if fm else proj_tm(bankf, W, xT, c0, 512)
                    op("act", lambda e: e.copy(out=dst_.t[:], in_=pb.t[:]), reads=[pb.b], writes=[dst_.b])
                    yield
                for h in range(4):
                    op("pe", lambda e: e.transpose(out=pT.t[:, h, :], in_=rk.t[:, h * 128:(h + 1) * 128], identity=ident.t[:]),
                       reads=[rk.b, ident.b], writes=[pT.b])
                op("act", lambda e: e.copy(out=rkt.t[:].rearrange("p (h d) -> p h d", h=4), in_=pT.t[:, 0:4, :]), reads=[pT.b], writes=[rkt.b])
                yield
            else:
                rq, rk, rkt, v = pre
            op("dve", lambda e: e.scalar_tensor_tensor(out=qt.t[:], in0=rq.t[:], scalar=128 ** -0.5, in1=ec.t[:], op0=ALU.mult, op1=ALU.mult),
               reads=[rq.b, ec.b], writes=[qt.b])
            op("dve", lambda e: e.tensor_tensor(out=kt.t[:], in0=rk.t[:], in1=enc.t[:], op=ALU.mult), reads=[rk.b, enc.b], writes=[kt.b])
            yield
            op("dve", lambda e: e.tensor_tensor(out=kh.t[:], in0=rkt.t[:], in1=epre.t[:], op=ALU.mult), reads=[rkt.b, epre.b], writes=[kh.b])
            yield
            psc = bankf()
            for h in range(4):
                hs = slice(h * 128, (h + 1) * 128)
                op("pe", lambda e: e.matmul(psc.t[:, hs], lhsT=kt.t[:, hs], rhs=qt.t[:, hs], start=True, stop=True),
                   reads=[kt.b, qt.b], writes=[psc.b])
            op("dve", lambda e: e.tensor_tensor(out=sT.t[:], in0=psc.t[:], in1=m01.t[:], op=ALU.mult), reads=[psc.b, m01.b], writes=[sT.b])
            yield
            po = bankf()
            for h in range(4):
                hs = slice(h * 128, (h + 1) * 128)
                op("pe", lambda e: e.matmul(po.t[:, hs], lhsT=sT.t[:, hs], rhs=v.t[:, hs], start=True, stop=False),
                   reads=[sT.b, v.b], writes=[po.b])
                op("pe", lambda e: e.matmul(po.t[:, hs], lhsT=qt.t[:, hs], rhs=Sbf.t[:, hs], start=False, stop=True),
                   reads=[qt.b, Sbf.b], writes=[po.b])
            o_evac(po)
            yield
            pu = bankf()
            for h in range(4):
                hs = slice(h * 128, (h + 1) * 128)
                op("pe", lambda e: e.matmul(pu.t[:, hs], lhsT=kh.t[:, hs], rhs=v.t[:, hs], start=True, stop=True),
                   reads=[kh.b, v.b], writes=[pu.b])
            for h in range(4):
                hs = slice(h * 128, (h + 1) * 128)
                cc = h * 128 + col
                op("dve", lambda e: e.scalar_tensor_tensor(out=S.t[:, hs], in0=S.t[:, hs], scalar=ec.t[:, cc:cc + 1], in1=pu.t[:, hs],
                                                           op0=ALU.mult, op1=ALU.add), reads=[S.b, ec.b, pu.b], writes=[S.b])
            op("pool", lambda e: e.tensor_copy(out=Sbf.t[:], in_=S.t[:]), reads=[S.b], writes=[Sbf.b])
            yield

        def reset_state(zero):
            if zero:
                op("pool", lambda e: e.memset(S.t[:], 0.0), writes=[S.b])
            else:
                op("dve", lambda e: e.tensor_scalar(out=S.t[:], in0=S.t[:], scalar1=rfl.t[:, 0:1], scalar2=None, op0=ALU.mult),
                   reads=[S.b, rfl.b], writes=[S.b])
            op("pool", lambda e: e.tensor_copy(out=Sbf.t[:], in_=S.t[:]), reads=[S.b], writes=[Sbf.b])

        def silu_from_psum(pz, ez, out_tb):
            op("act", lambda e: e.activation(out=ez.t[:], in_=pz.t[:], func=AF.Exp, scale=-1.0), reads=[pz.b], writes=[ez.b])
            op("act", lambda e: e.activation(out=ez.t[:], in_=ez.t[:], func=AF.Ln, bias=1.0), reads=[ez.b], writes=[ez.b])
            op("act", lambda e: e.activation(out=ez.t[:], in_=ez.t[:], func=AF.Exp, scale=-1.0), reads=[ez.b], writes=[ez.b])
            op("dve", lambda e: e.tensor_tensor(out=out_tb.t[:], in0=pz.t[:], in1=ez.t[:], op=ALU.mult), reads=[pz.b, ez.b], writes=[out_tb.b])

        def layer_norm(junk, stat, lng, lnb, h1):
            op("act", lambda e: e.activation(out=junk.t[:], in_=h1.t[:], func=AF.Copy, accum_out=stat.t[:, 0:1]),
               reads=[h1.b], writes=[junk.b, stat.b])
            op("act", lambda e: e.activation(out=junk.t[:], in_=h1.t[:], func=AF.Square, accum_out=stat.t[:, 1:2]),
               reads=[h1.b], writes=[junk.b, stat.b])
            op("dve", lambda e: e.tensor_scalar(out=stat.t[:, 0:2], in0=stat.t[:, 0:2], scalar1=1.0 / D, scalar2=None, op0=ALU.mult),
               reads=[stat.b], writes=[stat.b])
            op("dve", lambda e: e.tensor_tensor(out=stat.t[:, 2:3], in0=stat.t[:, 0:1], in1=stat.t[:, 0:1], op=ALU.mult),
               reads=[stat.b], writes=[stat.b])
            op("dve", lambda e: e.tensor_tensor(out=stat.t[:, 2:3], in0=stat.t[:, 1:2], in1=stat.t[:, 2:3], op=ALU.subtract),
               reads=[stat.b], writes=[stat.b])
            op("act", lambda e: e.activation(out=stat.t[:, 3:4], in_=stat.t[:, 2:3], func=AF.Ln, bias=LN_EPS), reads=[stat.b], writes=[stat.b])
            op("act", lambda e: e.activation(out=stat.t[:, 3:4], in_=stat.t[:, 3:4], func=AF.Exp, scale=-0.5), reads=[stat.b], writes=[stat.b])
            op("dve", lambda e: e.scalar_tensor_tensor(out=h1.t[:], in0=h1.t[:], scalar=stat.t[:, 0:1], in1=lng.t[:],
                                                       op0=ALU.subtract, op1=ALU.mult), reads=[h1.b, stat.b, lng.b], writes=[h1.b])
            op("dve", lambda e: e.scalar_tensor_tensor(out=h1.t[:], in0=h1.t[:], scalar=stat.t[:, 3:4], in1=lnb.t[:],
                                                       op0=ALU.mult, op1=ALU.add), reads=[h1.b, stat.b, lnb.b], writes=[h1.b])

        def init_state(src):
            op("dve", lambda e: e.tensor_copy(out=S.t[:], in_=src.t[:]), reads=[src.b], writes=[S.b])
            op("pool", lambda e: e.tensor_copy(out=Sbf.t[:], in_=S.t[:]), reads=[S.b], writes=[Sbf.b])

        def gla_bufs(st):
            g = {}
            for n in ("qt", "kt", "kh", "sT"):
                g[n] = sbt(st, "g_" + n, [128, 512], BF16)
            return g

        def run_pipe(stages):
            smin = min(lo + lag for (_, lag, lo, hi) in stages)
            smax = max(hi - 1 + lag for (_, lag, lo, hi) in stages)
            for s_ in range(smin, smax + 1):
                sched([fn(s_ - lag) for (fn, lag, lo, hi) in stages if lo <= s_ - lag < hi])

        def run_pipe2(stages, NU):
            smin = min(lag for (_, lag, _) in stages)
            smax = NU - 1 + max(lag for (_, lag, _) in stages)
            for s_ in range(smin, smax + 1):
                sched([fn(s_ - lag) for (fn, lag, pred) in stages if 0 <= s_ - lag < NU and pred(s_ - lag)])

        UE = [(J, e) for J in JOBS for e in range(J["n"] + 4)]
        UA = [(J, e) for J in JOBS for e in range(J["n"] + 2, 0, -1)]
        UP = [(J, t) for J in JOBS for t in range(J["NO"])]

        def main_e(u):
            J, e = UE[u]
            return 1 <= e < J["n"] + 3

        def own_e(u):
            J, e = UE[u]
            return 2 <= e < J["n"] + 2

        stW = ExitStack()
        stW.__enter__()
        W0 = sbt(stW, "W0", [128, 8, NC0], BF16)
        WO0 = sbt(stW, "WO0", [128, 8, D], BF16)
        with ExitStack() as st:
            stg = [sbt(st, "stgA%d" % i, [128, 8, 512], F32) for i in range(2)]
            W0e = TB(W0.t, "W0early")
            W0l = TB(W0.t, "W0late")
            load_weight(stg, W0e, w0_d, GD0, 64 + 1024, GD0)
            late_blocks = []
            for (dst_, src_, c0, n_) in ((W0l, w0_d, 0, GD0), (W0l, w0_d, ZA0, NC0 - ZA0), (WO0, wo0_d, 0, D)):
                c_ = 0
                while c_ < n_:
                    late_blocks.append((dst_, src_, c0 + c_, min(512, n_ - c_)))
                    c_ += 512
            WA = W0e
            SinB = [sbt(st, "SinB%d" % i, [128, 512], F32) for i in range(2)]
            xs = [sbt(st, "xsA%d" % i, [128, D], F32) for i in range(3)]
            xb = [sbt(st, "xbA%d" % i, [128, D], BF16) for i in range(2)]
            xT = [sbt(st, "xTA%d" % i, [128, 8, 128], BF16) for i in range(2)]

            with ExitStack() as st2:
                lgf = sbt(st2, "lgf", [128, 512], F32)
                lgb = sbt(st2, "lgb", [128, 512], F32)
                ebs = [sbt(st2, "ebs%d" % i, [128, 512], F32) for i in range(2)]
                epr = [sbt(st2, "epr%d" % i, [128, 512], F32) for i in range(2)]
                Dfb = [sbt(st2, "Dfb%d" % i, [128, 8], F32) for i in range(2)]
                khf = sbt(st2, "khf", [128, 512], BF16)
                khb = sbt(st2, "khb", [128, 512], BF16)
                vpp = sbt(st2, "vpp", [128, 512], BF16)
                bankA, bankB = Rot(banks[0:4]), Rot(banks[4:7])
                if True:
                    def p_load(u):
                        J, t = UP[u]
                        xo_d = J["xo_d"]
                        dma("sp", xs[u % 3].t[:], xo_d[t * 128:(t + 1) * 128, :], dx[u % 3], writes=[xs[u % 3].b])
                        op("pool", lambda e: e.tensor_copy(out=xb[u % 2].t[:], in_=xs[u % 3].t[:]), reads=[xs[u % 3].b], writes=[xb[u % 2].b])
                        yield

                    def p_front(u):
                        J, t = UP[u]
                        woff = J["woff"]
                        xb_, xT_ = xb[u % 2], xT[u % 2]
                        wf = wts.t[:, woff + 2 * t:woff + 2 * t + 1]
                        wb = wts.t[:, woff + 2 * t + 1:woff + 2 * t + 2]
                        for c in range(8):
                            op("pe", lambda e: e.transpose(out=pT.t[:, c, :], in_=xb_.t[:, c * 128:(c + 1) * 128], identity=ident.t[:]),
                               reads=[xb_.b, ident.b], writes=[pT.b])
                        op("act", lambda e: e.copy(out=xT_.t[:], in_=pT.t[:]), reads=[pT.b], writes=[xT_.b])
                        yield
                        pg = proj_fm(bankA, WA, xT_, GD0, 1, m=64)
                        op("act", lambda e: e.copy(out=gda.t[0:16, :], in_=pg.t[0:16, 0:128]), reads=[pg.b], writes=[gda.b])
                        op("act", lambda e: e.copy(out=gda.t[32:48, :], in_=pg.t[32:48, 0:128]), reads=[pg.b], writes=[gda.b])
                        yield
                        plf = bankA()
                        op("pe", lambda e: e.matmul(plf.t[:, :], lhsT=gda.t[0:32, :], rhs=wup.t[0:32, :], start=True, stop=True),
                           reads=[gda.b, wup.b], writes=[plf.b])
                        plb = bankA()
                        op("pe", lambda e: e.matmul(plb.t[:, :], lhsT=gda.t[32:64, :], rhs=wup.t[32:64, :], start=True, stop=True),
                           reads=[gda.b, wup.b], writes=[plb.b])
                        for (pl, lg_) in ((plf, lgf), (plb, lgb)):
                            op("act", lambda e: e.activation(out=lg_.t[:], in_=pl.t[:], func=AF.Exp, scale=-1.0), reads=[pl.b], writes=[lg_.b])
                            op("act", lambda e: e.activation(out=lg_.t[:], in_=lg_.t[:], func=AF.Ln, bias=1.0), reads=[lg_.b], writes=[lg_.b])
                        yield
                        ppf = bankA()
                        op("pe", lambda e: e.matmul(ppf.t[:, :], lhsT=mLs.t[:], rhs=lgf.t[:], start=True, stop=True),
                           reads=[lgf.b, mLs.b], writes=[ppf.b])
                        op("act", lambda e: e.activation(out=ebs[u % 2].t[:], in_=ppf.t[:], func=AF.Exp), reads=[ppf.b], writes=[ebs[u % 2].b])
                        yield
                        ppb = bankA()
                        op("pe", lambda e: e.matmul(ppb.t[:, :], lhsT=mUs.t[:], rhs=lgb.t[:], start=True, stop=True),
                           reads=[lgb.b, mUs.b], writes=[ppb.b])
                        op("act", lambda e: e.activation(out=epr[u % 2].t[:], in_=ppb.t[:], func=AF.Exp), reads=[ppb.b], writes=[epr[u % 2].b])
                        yield
                        ptot = bankA()
                        for di, lg_ in enumerate((lgf, lgb)):
                            for h in range(4):
                                cc = di * 4 + h
                                op("pe", lambda e: e.matmul(ptot.t[:, cc:cc + 1], lhsT=lg_.t[:, h * 128:(h + 1) * 128], rhs=m16c.t[:, 0:1],
                                                            start=True, stop=True), reads=[lg_.b, m16c.b], writes=[ptot.b])
                        dd = Dfb[u % 2]
                        op("act", lambda e: e.activation(out=dd.t[:, 0:4], in_=ptot.t[:, 0:4], func=AF.Exp, scale=wf), reads=[ptot.b, wts.b], writes=[dd.b])
                        op("act", lambda e: e.activation(out=dd.t[:, 4:8], in_=ptot.t[:, 4:8], func=AF.Exp, scale=wb), reads=[ptot.b, wts.b], writes=[dd.b])
                        yield

                    def p_upd(u):
                        J, t = UP[u]
                        woff = J["woff"]
                        Sf, Ab = SinF[J["j"]], SinB[J["j"]]
                        if t == 0:
                            op("pool", lambda e: e.memset(Sf.t[:], 0.0), writes=[Sf.b])
                            op("pool", lambda e: e.memset(Ab.t[:], 0.0), writes=[Ab.b])
                            op("pool", lambda e: e.memset(Pdec.t[:], 1.0), writes=[Pdec.b])
                        xT_ = xT[u % 2]
                        wf = wts.t[:, woff + 2 * t:woff + 2 * t + 1]
                        wb = wts.t[:, woff + 2 * t + 1:woff + 2 * t + 2]
                        dd = Dfb[u % 2]
                        pkt = proj_tm(bankB, WA, xT_, KT0, 512)
                        op("dve", lambda e: e.scalar_tensor_tensor(out=khf.t[:], in0=pkt.t[:], scalar=wf, in1=ebs[u % 2].t[:], op0=ALU.mult, op1=ALU.mult),
                           reads=[pkt.b, wts.b, ebs[u % 2].b], writes=[khf.b])
                        op("dve", lambda e: e.scalar_tensor_tensor(out=khb.t[:], in0=pkt.t[:], scalar=wb, in1=epr[u % 2].t[:], op0=ALU.mult, op1=ALU.mult),
                           reads=[pkt.b, wts.b, epr[u % 2].b], writes=[khb.b])
                        yield
                        pvt = proj_tm(bankB, WA, xT_, VT0, 512)
                        op("act", lambda e: e.copy(out=vpp.t[:], in_=pvt.t[:]), reads=[pvt.b], writes=[vpp.b])
                        yield
                        puf = bankB()
                        for h in range(4):
                            hs = slice(h * 128, (h + 1) * 128)
                            op("pe", lambda e: e.matmul(puf.t[:, hs], lhsT=khf.t[:, hs], rhs=vpp.t[:, hs], start=True, stop=True),
                               reads=[khf.b, vpp.b], writes=[puf.b])
                        for h in range(4):
                            hs = slice(h * 128, (h + 1) * 128)
                            op("dve", lambda e: e.scalar_tensor_tensor(out=Sf.t[:, hs], in0=Sf.t[:, hs], scalar=dd.t[:, h:h + 1], in1=puf.t[:, hs],
                                                                       op0=ALU.mult, op1=ALU.add), reads=[Sf.b, dd.b, puf.b], writes=[Sf.b])
                        yield
                        pub = bankB()
                        for h in range(4):
                            hs = slice(h * 128, (h + 1) * 128)
                            op("pe", lambda e: e.matmul(pub.t[:, hs], lhsT=khb.t[:, hs], rhs=vpp.t[:, hs], start=True, stop=True),
                               reads=[khb.b, vpp.b], writes=[pub.b])
                        for h in range(4):
                            hs = slice(h * 128, (h + 1) * 128)
                            op("dve", lambda e: e.scalar_tensor_tensor(out=Ab.t[:, hs], in0=pub.t[:, hs], scalar=Pdec.t[:, h:h + 1], in1=Ab.t[:, hs],
                                                                       op0=ALU.mult, op1=ALU.add), reads=[Ab.b, Pdec.b, pub.b], writes=[Ab.b])
                        op("dve", lambda e: e.tensor_tensor(out=Pdec.t[:], in0=Pdec.t[:], in1=dd.t[:, 4:8], op=ALU.mult), reads=[Pdec.b, dd.b], writes=[Pdec.b])
                        yield

                    def w_late(u):
                        dst_, src_, c0, n_ = late_blocks[u]
                        load_weight(stg, dst_, src_, c0, n_, c0)
                        yield

                    assert len(late_blocks) <= len(UP)
                    run_pipe2([(p_load, -2, lambda u: True), (p_front, -1, lambda u: True), (p_upd, 0, lambda u: True),
                               (w_late, 0, lambda u: u < len(late_blocks))], len(UP))
                fw.barrier()

            WA = W0
            lg = sbt(st, "lgA", [128, 512], F32)
            decs = [[sbt(st, "decA%d_%d" % (i, j), [128, 512], F32) for j in range(3)] for i in range(2)]
            obs = [sbt(st, "obsA%d" % i, [128, 512], F32) for i in range(2)]
            raws = [[sbt(st, "rawA%d_%d" % (i, k), [128, 512], BF16) for k in range(4)] for i in range(2)]
            g = gla_bufs(st)
            bankA, bankB = Rot(banks[0:3]), Rot(banks[3:7])
            if True:
                def a_load(i):
                    J, t = UA[i]
                    x_d = J["x_d"]
                    dma("sp", xs[i % 3].t[:], x_d[t * 128:(t + 1) * 128, :], dx[i % 3], writes=[xs[i % 3].b])
                    op("pool", lambda e: e.tensor_copy(out=xb[i % 2].t[:], in_=xs[i % 3].t[:]), reads=[xs[i % 3].b], writes=[xb[i % 2].b])
                    yield

                def a_front(i):
                    yield from front(False, bankA, WA, xs[i % 3], xb[i % 2], xT[i % 2], lg, decs[i % 2])

                def a_main(i):
                    J, t = UA[i]
                    ob_d = J["ob_d"]
                    if t == J["n"] + 2:
                        init_state(SinB[J["j"]])
                    ob = obs[i % 2]

                    def o_evac(po):
                        op("act", lambda e: e.copy(out=ob.t[:], in_=po.t[:]), reads=[po.b], writes=[ob.b])
                        dma("sp", ob_d[t * 128:(t + 1) * 128, :], ob.t[:], dob[i % 2], reads=[ob.b])
                    rw = raws[i % 2]
                    yield from gla_main(False, bankB, WA, xT[i % 2], decs[i % 2], g, o_evac, raw=rw)
                    qkv_d = J["qkv_d"]
                    for k in range(4):
                        dma("sp", qkv_d[t * 128:(t + 1) * 128, k, :], rw[k].t[:], dqs[i % 2], reads=[rw[k].b])
                    for k in range(4):
                        rw[k].b.r[dqs[i % 2]] = dqs[i % 2].cnt
                    yield

                run_pipe2([(a_load, -2, lambda u: True), (a_front, -1, lambda u: True), (a_main, 0, lambda u: True)], len(UA))
            fw.barrier()

        with ExitStack() as st:
            WB = W0
            WO = WO0
            biasT = sbt(st, "biasT", [128, 8, 3, 128], F32)
            esink = sbt(st, "esink", [128, 8], F32)
            gn4 = sbt(st, "gn4", [128, 512], F32)
            lng = sbt(st, "lng", [128, D], F32)
            lnb = sbt(st, "lnb", [128, D], F32)
            dma_c(biasT.t[:].rearrange("p h b q -> p (h b q)"), bias_d[:, :], [biasT.b])
            dma_c(esink.t[:], sink_d[:, :], [esink.b])
            dma_c(gn4.t[:], gn_d[:, :], [gn4.b])
            dma_c(lng.t[:], lng_d[:, 0:D], [lng.b])
            dma_c(lnb.t[:], lnb_d[:, 0:D], [lnb.b])
            op("act", lambda e: e.activation(out=esink.t[:], in_=esink.t[:], func=AF.Exp), reads=[esink.b], writes=[esink.b])
            op("pool", lambda e: e.affine_select(out=biasT.t[:, :, 0, :], in_=biasT.t[:, :, 0, :], pattern=[[0, 8], [-1, 128]],
                                                 compare_op=ALU.is_ge, fill=NEG, base=0, channel_multiplier=1),
               reads=[biasT.b], writes=[biasT.b])
            op("pool", lambda e: e.affine_select(out=biasT.t[:, :, 2, :], in_=biasT.t[:, :, 2, :], pattern=[[0, 8], [1, 128]],
                                                 compare_op=ALU.is_ge, fill=NEG, base=0, channel_multiplier=-1),
               reads=[biasT.b], writes=[biasT.b])

            xs = [sbt(st, "xsB%d" % i, [128, D], F32) for i in range(5)]
            xb = [sbt(st, "xbB%d" % i, [128, D], BF16) for i in range(2)]
            lg = sbt(st, "lgB", [128, 512], F32)
            xT = [sbt(st, "xTB%d" % i, [128, 8, 128], BF16) for i in range(2)]
            decs = [[sbt(st, "decB%d_%d" % (i, j), [128, 512], F32) for j in range(3)] for i in range(2)]
            obl = [sbt(st, "oblB%d" % i, [128, 512], F32) for i in range(2)]
            raws = [[sbt(st, "rawB%d_%d" % (i, k), [128, 512], BF16) for k in range(4)] for i in range(2)]
            g = gla_bufs(st)
            o_sb = sbt(st, "o_sb", [128, 512], F32)
            ss = sbt(st, "ssB", [128, 8], F32)
            ez = sbt(st, "ezB", [128, 512], F32)
            gz = sbt(st, "gzB", [128, 512], F32)
            szb = [sbt(st, "szbB%d" % i, [128, 512], F32) for i in range(2)]
            ybuf = [sbt(st, "yB%d" % i, [128, D], BF16) for i in range(3)]
            qTB = [sbt(st, "qTB%d" % i, [128, 512], BF16) for i in range(2)]
            kTB = [sbt(st, "kTB%d" % i, [128, 128], BF16) for i in range(4)]
            vB = [sbt(st, "vB%d" % i, [128, 2, 65], BF16) for i in range(4)]
            sc = [sbt(st, "scB%d" % i, [128, 384], F32) for i in range(2)]
            pTs = [sbt(st, "pTsB%d" % i, [128, 384], BF16) for i in range(2)]
            den = sbt(st, "denB", [128, 8], F32)
            yT = sbt(st, "yTB", [128, 8, 128], BF16)
            h1s = [sbt(st, "h1B%d" % i, [128, D], F32) for i in range(2)]
            junk = sbt(st, "junkB", [128, D], BF16)
            sq = junk
            stat = sbt(st, "statB", [128, 4], F32)
            for v3 in vB:
                op("pool", lambda e: e.memset(v3.t[:], 1.0), writes=[v3.b])
            bankA, bankB, bankC = Rot(banks[0:2]), Rot(banks[2:4]), Rot(banks[5:7])

            if True:
                def b_load(u):
                    J, e_ = UE[u]
                    x_d, t = J["x_d"], u
                    dma("sp", xs[t % 5].t[:], x_d[e_ * 128:(e_ + 1) * 128, :], dx[t % 5], writes=[xs[t % 5].b])
                    op("pool", lambda e: e.tensor_copy(out=xb[t % 2].t[:], in_=xs[t % 5].t[:]), reads=[xs[t % 5].b], writes=[xb[t % 2].b])
                    yield

                def b_loadob(t):
                    J, e_ = UE[t]
                    ob_d = J["ob_d"]
                    dma("sp", obl[t % 2].t[:], ob_d[e_ * 128:(e_ + 1) * 128, :], dob[t % 2], writes=[obl[t % 2].b])
                    qkv_d, rw = J["qkv_d"], raws[t % 2]
                    for k in range(4):
                        dma("sp", rw[k].t[:], qkv_d[e_ * 128:(e_ + 1) * 128, k, :], dql[t % 2], writes=[rw[k].b])
                    for k in range(4):
                        rw[k].b.w = (dql[t % 2], dql[t % 2].cnt)
                    yield

                def b_front(t):
                    yield from front(True, bankA, WB, xs[t % 5], xb[t % 2], xT[t % 2], lg, decs[t % 2])
                    pkb = proj_fm(bankA, WB, xT[t % 2], KB0, 1)
                    op("act", lambda e: e.copy(out=kTB[t % 4].t[:], in_=pkb.t[:, 0:128]), reads=[pkb.b], writes=[kTB[t % 4].b])
                    yield
                    pvb = proj_tm(bankA, WB, xT[t % 2], VB0, 128)
                    vs = vB[t % 4]
                    op("act", lambda e: e.copy(out=vs.t[:, :, 0:64], in_=pvb.t[:, 0:128].rearrange("p (a b) -> p a b", a=2)),
                       reads=[pvb.b], writes=[vs.b])
                    yield

                def b_main(t):
                    J, e_ = UE[t]
                    par = t % 2
                    if e_ == 1:
                        init_state(SinF[J["j"]])
                    y = ybuf[t % 3]

                    def o_evac(po):
                        op("dve", lambda e: e.tensor_tensor(out=o_sb.t[:], in0=po.t[:], in1=obl[par].t[:], op=ALU.add),
                           reads=[po.b, obl[par].b], writes=[o_sb.b])
                    yield from gla_main(True, bankB, WB, xT[par], decs[par], g, o_evac, pre=raws[par])
                    for h in range(4):
                        hs = slice(h * 128, (h + 1) * 128)
                        op("act", lambda e: e.activation(out=sq.t[:, hs], in_=o_sb.t[:, hs], func=AF.Square, accum_out=ss.t[:, h:h + 1]),
                           reads=[o_sb.b], writes=[sq.b, ss.b])
                    op("dve", lambda e: e.tensor_scalar(out=ss.t[:, 0:4], in0=ss.t[:, 0:4], scalar1=1.0 / 128, scalar2=None, op0=ALU.mult),
                       reads=[ss.b], writes=[ss.b])
                    op("act", lambda e: e.activation(out=ss.t[:, 0:4], in_=ss.t[:, 0:4], func=AF.Ln, bias=NORM_EPS), reads=[ss.b], writes=[ss.b])
                    op("act", lambda e: e.activation(out=ss.t[:, 0:4], in_=ss.t[:, 0:4], func=AF.Exp, scale=-0.5), reads=[ss.b], writes=[ss.b])
                    yield
                    pz = proj_tm(bankB, WB, xT[par], ZA0, 512)
                    silu_from_psum(pz, ez, gz)
                    op("pool", lambda e: e.tensor_tensor(out=gz.t[:], in0=gz.t[:], in1=gn4.t[:], op=ALU.mult), reads=[gz.b, gn4.b], writes=[gz.b])
                    yield
                    for h in range(4):
                        hs = slice(h * 128, (h + 1) * 128)
                        op("dve", lambda e: e.scalar_tensor_tensor(out=y.t[:, hs], in0=o_sb.t[:, hs], scalar=ss.t[:, h:h + 1], in1=gz.t[:, hs],
                                                                   op0=ALU.mult, op1=ALU.mult), reads=[o_sb.b, ss.b, gz.b], writes=[y.b])
                    yield
                    pz2 = proj_tm(bankB, WB, xT[par], ZB0, 512)
                    silu_from_psum(pz2, ez, szb[par])
                    yield
                    pqb = proj_fm(bankB, WB, xT[par], QB0, 4)
                    op("act", lambda e: e.activation(out=qTB[par].t[:], in_=pqb.t[:], func=AF.Copy, scale=0.125), reads=[pqb.b], writes=[qTB[par].b])
                    yield

                def b_back(t):
                    J, e_ = UE[t]
                    n, jj = J["n"], J["j"]
                    par = t % 2
                    y = ybuf[t % 3]
                    blks = [(0, t - 1, 2 * jj if e_ == 2 else None), (1, t, None), (2, t + 1, 2 * jj + 1 if e_ == n + 1 else None)]
                    b0, b1 = 0, 3

                    def scores(h):
                        kv, c = h // 4, h % 4
                        rs = slice(kv * 64, (kv + 1) * 64)
                        pss = banks[5 + h % 2]
                        for (bi, tt, _) in blks:
                            op("pe", lambda e: e.matmul(pss.t[:, bi * 128:(bi + 1) * 128], lhsT=kTB[tt % 4].t[rs, :],
                                                        rhs=qTB[par].t[rs, c * 128:(c + 1) * 128], start=True, stop=True),
                               reads=[kTB[tt % 4].b, qTB[par].b], writes=[pss.b])
                        s_, p_ = sc[h % 2], pTs[h % 2]
                        op("dve", lambda e: e.tensor_tensor(out=s_.t[:, b0 * 128:b1 * 128], in0=pss.t[:, b0 * 128:b1 * 128],
                                                            in1=biasT.t[:, h, b0:b1, :].rearrange("p b q -> p (b q)"), op=ALU.add),
                           reads=[pss.b, biasT.b], writes=[s_.b])
                        for (bi, tt, fc) in blks:
                            if fc is not None:
                                op("dve", lambda e: e.tensor_scalar(out=s_.t[:, bi * 128:(bi + 1) * 128], in0=s_.t[:, bi * 128:(bi + 1) * 128],
                                                                    scalar1=negm.t[:, fc:fc + 1], scalar2=None, op0=ALU.add),
                                   reads=[s_.b, negm.b], writes=[s_.b])
                        op("act", lambda e: e.activation(out=p_.t[:, b0 * 128:b1 * 128], in_=s_.t[:, b0 * 128:b1 * 128], func=AF.Exp),
                           reads=[s_.b], writes=[p_.b])

                    for half in range(2):
                        pv = banks[4]
                        for hh in range(4):
                            h = half * 4 + hh
                            kv = h // 4
                            if h == 0:
                                scores(0)
                            if h + 1 < 8:
                                scores(h + 1)
                            yield
                            p_ = pTs[h % 2]
                            hc = hh * 65
                            for n_, (bi, tt, _) in enumerate(blks):
                                op("pe", lambda e: e.matmul(pv.t[:, hc:hc + 65], lhsT=p_.t[:, bi * 128:(bi + 1) * 128], rhs=vB[tt % 4].t[:, kv, :],
                                                            start=(n_ == 0), stop=(n_ == len(blks) - 1)),
                                   reads=[p_.b, vB[tt % 4].b], writes=[pv.b])
                        pvv = pv.t[:, 0:260].rearrange("p (h c) -> p h c", c=65)
                        op("dve", lambda e: e.tensor_tensor(out=den.t[:, half * 4:half * 4 + 4], in0=pvv[:, :, 64],
                                                            in1=esink.t[:, half * 4:half * 4 + 4], op=ALU.add),
                           reads=[pv.b, esink.b], writes=[den.b])
                        op("dve", lambda e: e.reciprocal(out=den.t[:, half * 4:half * 4 + 4], in_=den.t[:, half * 4:half * 4 + 4]),
                           reads=[den.b], writes=[den.b])
                        for hh in range(4):
                            h = half * 4 + hh
                            op("dve", lambda e: e.scalar_tensor_tensor(out=y.t[:, 512 + h * 64:512 + (h + 1) * 64], in0=pvv[:, hh, 0:64],
                                                                       scalar=den.t[:, h:h + 1], in1=szb[par].t[:, h * 64:(h + 1) * 64],
                                                                       op0=ALU.mult, op1=ALU.mult),
                               reads=[pv.b, den.b, szb[par].b], writes=[y.b])
                        yield

                def b_out(t):
                    J, e_ = UE[t]
                    x1_d = J["x1_d"]
                    par = t % 2
                    y = ybuf[t % 3]
                    for c in range(8):
                        op("pe", lambda e: e.transpose(out=pT.t[:, c, :], in_=y.t[:, c * 128:(c + 1) * 128], identity=ident.t[:]),
                           reads=[y.b, ident.b], writes=[pT.b])
                    op("act", lambda e: e.copy(out=yT.t[:], in_=pT.t[:]), reads=[pT.b], writes=[yT.b])
                    yield
                    h1 = h1s[par]
                    for nb in range(2):
                        pb = bankC()
                        for kc in range(8):
                            op("pe", lambda e: e.matmul(pb.t[:, :], lhsT=yT.t[:, kc, :], rhs=WO.t[:, kc, nb * 512:(nb + 1) * 512],
                                                        start=(kc == 0), stop=(kc == 7)), reads=[yT.b, WO.b], writes=[pb.b])
                        op("dve", lambda e: e.scalar_tensor_tensor(out=h1.t[:, nb * 512:(nb + 1) * 512], in0=xs[t % 5].t[:, nb * 512:(nb + 1) * 512],
                                                                   scalar=ALPHA, in1=pb.t[:, :], op0=ALU.mult, op1=ALU.add),
                           reads=[xs[t % 5].b, pb.b], writes=[h1.b])
                        yield
                    layer_norm(junk, stat, lng, lnb, h1)
                    dma("sp", x1_d[e_ * 128:(e_ + 1) * 128, :], h1.t[:], dst[par], reads=[h1.b])
                    yield

                run_pipe2([(b_load, -2, lambda u: True), (b_loadob, -1, main_e), (b_front, -1, lambda u: True),
                           (b_main, 0, main_e), (b_back, 1, main_e), (b_out, 2, main_e)], len(UE))
            fw.barrier()

        stW.close()
        with ExitStack() as st:
            W1 = sbt(st, "W1", [128, 8, 4 * D], BF16)
            WO = sbt(st, "WO1", [128, 8, D], BF16)
            with ExitStack() as st2:
                stg = [sbt(st2, "stgC%d" % i, [128, 8, 512], F32) for i in range(2)]
                load_weight(stg, W1, w1_d, 0, 4 * D, 0)
                load_weight(stg, WO, wo1_d, 0, D, 0)
                fw.barrier()
            lng = sbt(st, "lngC", [128, D], F32)
            lnb = sbt(st, "lnbC", [128, D], F32)
            cw = sbt(st, "cw", [128, 8, 3], F32)
            dma_c(lng.t[:], lng_d[:, D:2 * D], [lng.b])
            dma_c(lnb.t[:], lnb_d[:, D:2 * D], [lnb.b])
            dma_c(cw.t[:].rearrange("p c k -> p (c k)"), cw_d[:, :], [cw.b])
            xs = [sbt(st, "xsC%d" % i, [128, D], F32) for i in range(6)]
            xb = [sbt(st, "xbC%d" % i, [128, D], BF16) for i in range(2)]
            xT = [sbt(st, "xTC%d" % i, [128, 8, 128], BF16) for i in range(2)]
            TT = [sbt(st, "TT%d" % i, [128, 8, 130], F32) for i in range(3)]
            ub = [sbt(st, "ub%d" % i, [128, D], F32) for i in range(3)]
            hsb = sbt(st, "hsb", [128, 512], F32)
            ez = sbt(st, "ezC", [128, 512], F32)
            cv = sbt(st, "cvC", [128, D], F32)
            gTs = [sbt(st, "gTC%d" % i, [128, D], BF16) for i in range(2)]
            h1s = [sbt(st, "h1C%d" % i, [128, D], F32) for i in range(2)]
            junk = sbt(st, "junkC", [128, D], BF16)
            stat = sbt(st, "statC", [128, 4], F32)
            for T_ in TT:
                op("pool", lambda e: e.memset(T_.t[:], 0.0), writes=[T_.b])
            cwe = [sbt(st, "cwe%d" % k, [128, 8, 128], F32) for k in range(3)]
            ctmp = sbt(st, "ctmp", [128, D], F32)
            for k in range(3):
                op("pool", lambda e: e.memset(cwe[k].t[:], 1.0), writes=[cwe[k].b])
                for c in range(8):
                    op("dve", lambda e: e.tensor_scalar(out=cwe[k].t[:, c, :], in0=cwe[k].t[:, c, :], scalar1=cw.t[:, c, k:k + 1], scalar2=None,
                                                        op0=ALU.mult), reads=[cwe[k].b, cw.b], writes=[cwe[k].b])
            bankB, bankC = Rot(banks[0:5]), Rot(banks[5:7])

            if True:
                def c_load(t):
                    J, e_ = UE[t]
                    x1_d = J["x1_d"]
                    dma("sp", xs[t % 6].t[:], x1_d[e_ * 128:(e_ + 1) * 128, :], dx[t % 6], writes=[xs[t % 6].b])
                    op("pool", lambda e: e.tensor_copy(out=xb[t % 2].t[:], in_=xs[t % 6].t[:]), reads=[xs[t % 6].b], writes=[xb[t % 2].b])
                    yield

                def c_front(t):
                    xT_ = xT[t % 2]
                    xb_ = xb[t % 2]
                    for c in range(8):
                        op("pe", lambda e: e.transpose(out=pT.t[:, c, :], in_=xb_.t[:, c * 128:(c + 1) * 128], identity=ident.t[:]),
                           reads=[xb_.b, ident.b], writes=[pT.b])
                    op("act", lambda e: e.copy(out=xT_.t[:], in_=pT.t[:]), reads=[pT.b], writes=[xT_.b])
                    yield

                def c_main(t):
                    J, e_ = UE[t]
                    n, jj = J["n"], J["j"]
                    xT_ = xT[t % 2]
                    T_ = TT[t % 3]
                    u_ = ub[t % 3]
                    op("pool", lambda e: e.memset(T_.t[:, :, 0:1], 0.0), writes=[T_.b])
                    op("pool", lambda e: e.memset(T_.t[:, :, 129:130], 0.0), writes=[T_.b])
                    for half in range(2):
                        fs = slice(half * 512, (half + 1) * 512)
                        ph = proj_fm(bankB, W1, xT_, 2 * D + half * 512, 4)
                        op("act", lambda e: e.copy(out=hsb.t[:], in_=ph.t[:]), reads=[ph.b], writes=[hsb.b])
                        yield
                        pc_ = proj_fm(bankB, W1, xT_, D + half * 512, 4)
                        op("dve", lambda e: e.tensor_tensor(out=T_.t[:, half * 4:half * 4 + 4, 1:129],
                                                            in0=pc_.t[:].rearrange("p (c q) -> p c q", c=4),
                                                            in1=hsb.t[:].rearrange("p (c q) -> p c q", c=4), op=ALU.mult),
                           reads=[pc_.b, hsb.b], writes=[T_.b])
                        yield
                        pz = proj_fm(bankB, W1, xT_, 3 * D + half * 512, 4)
                        op("act", lambda e: e.activation(out=ez.t[:], in_=pz.t[:], func=AF.Exp, scale=-1.0), reads=[pz.b], writes=[ez.b])
                        op("act", lambda e: e.activation(out=ez.t[:], in_=ez.t[:], func=AF.Ln, bias=1.0), reads=[ez.b], writes=[ez.b])
                        op("act", lambda e: e.activation(out=ez.t[:], in_=ez.t[:], func=AF.Exp, scale=-1.0), reads=[ez.b], writes=[ez.b])
                        op("dve", lambda e: e.tensor_tensor(out=ez.t[:], in0=pz.t[:], in1=ez.t[:], op=ALU.mult), reads=[pz.b, ez.b], writes=[ez.b])
                        yield
                        pbg = proj_fm(bankB, W1, xT_, half * 512, 4)
                        op("dve", lambda e: e.tensor_tensor(out=u_.t[:, fs], in0=pbg.t[:], in1=ez.t[:], op=ALU.mult),
                           reads=[pbg.b, ez.b], writes=[u_.b])
                        yield
                    if e_ > 1:
                        Tp = TT[(t - 1) % 3]
                        if e_ == 2:
                            fc = 2 * jj
                            op("dve", lambda e: e.tensor_scalar(out=T_.t[:, :, 0:1], in0=Tp.t[:, :, 128:129], scalar1=rfl.t[:, fc:fc + 1], scalar2=None,
                                                                op0=ALU.mult), reads=[Tp.b, rfl.b], writes=[T_.b])
                        elif e_ == n + 2:
                            fc = 2 * jj + 1
                            op("dve", lambda e: e.tensor_scalar(out=Tp.t[:, :, 129:130], in0=T_.t[:, :, 1:2], scalar1=rfl.t[:, fc:fc + 1], scalar2=None,
                                                                op0=ALU.mult), reads=[T_.b, rfl.b], writes=[Tp.b])
                        else:
                            op("pool", lambda e: e.tensor_copy(out=T_.t[:, :, 0:1], in_=Tp.t[:, :, 128:129]), reads=[Tp.b], writes=[T_.b])
                            op("pool", lambda e: e.tensor_copy(out=Tp.t[:, :, 129:130], in_=T_.t[:, :, 1:2]), reads=[T_.b], writes=[Tp.b])
                    yield

                def c_back(t):
                    T_ = TT[t % 3]
                    u_ = ub[t % 3]
                    cv3 = cv.t[:].rearrange("p (c q) -> p c q", c=8)
                    tm3 = ctmp.t[:].rearrange("p (c q) -> p c q", c=8)
                    op("dve", lambda e: e.tensor_tensor(out=cv3, in0=T_.t[:, :, 0:128], in1=cwe[0].t[:], op=ALU.mult),
                       reads=[T_.b, cwe[0].b], writes=[cv.b])
                    op("dve", lambda e: e.tensor_tensor(out=tm3, in0=T_.t[:, :, 1:129], in1=cwe[1].t[:], op=ALU.mult),
                       reads=[T_.b, cwe[1].b], writes=[ctmp.b])
                    yield
                    op("dve", lambda e: e.tensor_tensor(out=cv.t[:], in0=cv.t[:], in1=ctmp.t[:], op=ALU.add),
                       reads=[cv.b, ctmp.b], writes=[cv.b])
                    op("dve", lambda e: e.tensor_tensor(out=tm3, in0=T_.t[:, :, 2:130], in1=cwe[2].t[:], op=ALU.mult),
                       reads=[T_.b, cwe[2].b], writes=[ctmp.b])
                    yield
                    op("dve", lambda e: e.tensor_tensor(out=cv.t[:], in0=cv.t[:], in1=ctmp.t[:], op=ALU.add),
                       reads=[cv.b, ctmp.b], writes=[cv.b])
                    gT = gTs[t % 2]
                    op("dve", lambda e: e.tensor_tensor(out=gT.t[:], in0=cv.t[:], in1=u_.t[:], op=ALU.mult), reads=[cv.b, u_.b], writes=[gT.b])
                    yield

                def c_out(t):
                    J, e_ = UE[t]
                    y_d = J["y_d"]
                    gT = gTs[t % 2]
                    h1 = h1s[t % 2]
                    for nb in range(2):
                        pb = bankC()
                        for kc in range(8):
                            op("pe", lambda e: e.matmul(pb.t[:, :], lhsT=gT.t[:, kc * 128:(kc + 1) * 128], rhs=WO.t[:, kc, nb * 512:(nb + 1) * 512],
                                                        start=(kc == 0), stop=(kc == 7)), reads=[gT.b, WO.b], writes=[pb.b])
                        op("dve", lambda e: e.scalar_tensor_tensor(out=h1.t[:, nb * 512:(nb + 1) * 512], in0=xs[t % 6].t[:, nb * 512:(nb + 1) * 512],
                                                                   scalar=ALPHA, in1=pb.t[:, :], op0=ALU.mult, op1=ALU.add),
                           reads=[xs[t % 6].b, pb.b], writes=[h1.b])
                        yield
                    layer_norm(junk, stat, lng, lnb, h1)
                    dma("sp", y_d[(e_ - 2) * 128:(e_ - 1) * 128, :], h1.t[:], dst[t % 2], reads=[h1.b])
                    yield

                run_pipe2([(c_load, -2, main_e), (c_front, -1, main_e), (c_main, 0, main_e),
                           (c_back, 2, own_e), (c_out, 3, own_e)], len(UE))
            fw.barrier()
        for d_ in dst:
            nc.sync.wait_ge(d_.h, d_.cnt)
    return nc


def _rel_buckets():
    BLOCK, REL_BUCKETS, REL_MAX_DIST = 128, 32, 128
    i = np.arange(BLOCK)[:, None]
    j = np.arange(3 * BLOCK)[None, :]
    rel = j - BLOCK - i
    half = REL_BUCKETS // 2
    max_exact = half // 2
    n = np.abs(rel)
    large = max_exact + (np.log(np.maximum(n, 1) / max_exact) / np.log(REL_MAX_DIST / max_exact)
                         * (half - max_exact)).astype(np.int32)
    large = np.minimum(large, half - 1)
    bucket = (rel > 0).astype(np.int32) * half + np.where(n < max_exact, n, large)
    return bucket.astype(np.int32)


def pack_w0(w):
    qa, ka, va, za = w[:, 0:512], w[:, 512:1024], w[:, 1024:1536], w[:, 1536:2048]
    gd = w[:, 2048:2080]
    o = 2080
    qb, kb, vb, zb = w[:, o:o + 512], w[:, o + 512:o + 640], w[:, o + 640:o + 768], w[:, o + 768:o + 1280]
    out = np.zeros((D, NC0), np.float32)
    out[:, QA0:QA0 + 512] = qa
    out[:, KA0:KA0 + 512] = ka
    for c in range(4):
        out[:, QB0 + c * 128:QB0 + c * 128 + 64] = qb[:, c * 64:(c + 1) * 64]
        out[:, QB0 + c * 128 + 64:QB0 + (c + 1) * 128] = qb[:, (c + 4) * 64:(c + 5) * 64]
    out[:, KB0:KB0 + 128] = kb
    out[:, GD0:GD0 + 16] = gd[:, 0:16]
    out[:, GD0 + 32:GD0 + 48] = gd[:, 16:32]
    out[:, KT0:KT0 + 512] = ka
    out[:, VT0:VT0 + 512] = va
    out[:, ZA0:ZA0 + 512] = za
    out[:, ZB0:ZB0 + 512] = zb
    out[:, VB0:VB0 + 128] = vb
    return out


def make_shared(inp):
    f = lambda a: np.ascontiguousarray(np.asarray(a, dtype=np.float32))
    sh = {}
    sh["w0"] = pack_w0(f(inp["w_in_even"])[0])
    sh["wo0"] = f(inp["w_out_even"])[0]
    sh["w1"] = f(inp["w_in_odd"])[0]
    sh["wo1"] = f(inp["w_out_odd"])[0]
    wup = np.zeros((64, 512), np.float32)
    wup[0:16] = f(inp["gla_w_up_fwd"])[0]
    wup[16] = f(inp["gla_b_fwd"])[0]
    wup[32:48] = f(inp["gla_w_up_bwd"])[0]
    wup[48] = f(inp["gla_b_bwd"])[0]
    sh["wup"] = wup
    bucket = _rel_buckets()
    rb = f(inp["rel_bias"])
    bfull = rb[bucket]
    bT = bfull.reshape(128, 3, 128, 8).transpose(2, 3, 1, 0)
    sh["biasT"] = np.ascontiguousarray(bT).reshape(128, 8 * 3 * 128)
    sh["sink"] = np.ascontiguousarray(np.broadcast_to(f(inp["swa_sink"])[0][None, :], (128, 8)))
    sh["gn"] = np.ascontiguousarray(np.broadcast_to(np.tile(f(inp["gla_norm_g"])[0], 4)[None, :], (128, 512)))
    sh["lng"] = np.ascontiguousarray(np.broadcast_to(f(inp["ln_g"]).reshape(1, 2 * D), (128, 2 * D)))
    sh["lnb"] = np.ascontiguousarray(np.broadcast_to(f(inp["ln_b"]).reshape(1, 2 * D), (128, 2 * D)))
    cw = f(inp["conv_w"])[0]
    sh["convw"] = np.ascontiguousarray(cw.reshape(3, 8, 128).transpose(2, 1, 0)).reshape(128, 24)
    return sh


_NC_CACHE = {}


def _ext(xseq, a, n):
    L = xseq.shape[0]
    out = np.zeros(((n + 4) * 128, D), np.float32)
    lo, hi = (a - 2) * 128, (a + n + 2) * 128
    slo, shi = max(lo, 0), min(hi, L)
    out[slo - lo:shi - lo] = xseq[slo:shi]
    return out


def _oth(xseq, a, n, NO):
    Ls = xseq.shape[0] // 128
    pre = list(range(0, max(a - 1, 0)))
    suf = list(range(min(a + n + 1, Ls), Ls))
    out = np.zeros((NO * 128, D), np.float32)
    wts = np.zeros((NO, 2), np.float32)
    for k, t in enumerate(pre + suf):
        out[k * 128:(k + 1) * 128] = xseq[t * 128:(t + 1) * 128]
        wts[k, 0 if k < len(pre) else 1] = 1.0
    return out, wts


def run(xp, xs_, inp, n_cores=8):
    Bp, LP, _ = xp.shape
    Bs, LS, _ = xs_.shape
    assert Bp == 2 and Bs == 4 and LP % 512 == 0 and LS % 256 == 0
    nP, nS = LP // 128 // 4, LS // 128 // 2
    key = (nP, nS)
    if key not in _NC_CACHE:
        _NC_CACHE[key] = build(nP, nS)
    nc = _NC_CACHE[key]
    sh = make_shared(inp)
    in_maps = []
    for c in range(n_cores):
        m = dict(sh)
        aP, aS = (c % 4) * nP, (c % 2) * nS
        xP, xS = xp[c // 4], xs_[c // 2]
        m["xextP"] = _ext(xP, aP, nP)
        m["xextS"] = _ext(xS, aS, nS)
        m["xothP"], wP = _oth(xP, aP, nP, 3 * nP)
        m["xothS"], wS = _oth(xS, aS, nS, nS)
        wrow = np.concatenate([wP.reshape(-1), wS.reshape(-1)])[None, :]
        m["wts"] = np.ascontiguousarray(np.broadcast_to(wrow, (128, wrow.shape[1]))).astype(np.float32)
        fl = np.array([[float(aP > 0), float(aP + nP < 4 * nP), float(aS > 0), float(aS + nS < 2 * nS)]], np.float32)
        m["flags"] = np.ascontiguousarray(np.broadcast_to(fl, (128, 4)))
        in_maps.append(m)
    res = run_bass_kernel_spmd(nc, in_maps, core_ids=list(range(n_cores)))
    yp = np.zeros((Bp, LP, D), np.float32)
    ys = np.zeros((Bs, LS, D), np.float32)
    for c in range(n_cores):
        aP, aS = (c % 4) * nP, (c % 2) * nS
        yp[c // 4, aP * 128:(aP + nP) * 128] = res.results[c]["yP"]
        ys[c // 2, aS * 128:(aS + nS) * 128] = res.results[c]["yS"]
    return yp, ys


def kernel(x_prompt, x_sample, **w):
    xp = np.ascontiguousarray(np.asarray(x_prompt, dtype=np.float32))
    xs_ = np.ascontiguousarray(np.asarray(x_sample, dtype=np.float32))
    return run(xp, xs_, w)
```
